# Optimizing a Trainium2 kernel written in Bass

```python
import jax, jax.numpy as jnp
from jax import lax
import numpy as np

D_MODEL = 2048
BATCH = 1
SEQ = 8192
DEPTH = 4
DEC_BATCH = 16
DEC_SEQ = 16
PAST_LEN = 2048

CHUNK = 64
N_A_LAYERS = DEPTH // 2
N_B_LAYERS = DEPTH - N_A_LAYERS
D_FF = 5632
GMLP_CHUNK = 128
D_GATE = 2 * D_MODEL
GMLP_GROUPS = 4
N_HEADS = 16
HEAD_DIM = D_MODEL // N_HEADS
Q_BLOCK = 128
RMS_EPS = 1e-6
LN_EPS = 1e-5
NEG_INF = -1e30

kernel_name = 'yoco_gmlp_fox_macaron_stream_step'


def rmsnorm(x, g):
    xf = x.astype(jnp.float32)
    y = xf * lax.rsqrt(jnp.mean(xf * xf, axis=-1, keepdims=True) + RMS_EPS)
    return (y * g.astype(jnp.float32)).astype(x.dtype)


def layernorm(x, g, b):
    xf = x.astype(jnp.float32)
    mu = jnp.mean(xf, axis=-1, keepdims=True)
    xc = xf - mu
    y = xc * lax.rsqrt(jnp.mean(xc * xc, axis=-1, keepdims=True) + LN_EPS)
    return (y * g.astype(jnp.float32) + b.astype(jnp.float32)).astype(x.dtype)


def half_ffn(x, g, w_gate, w_up, w_down):
    h = rmsnorm(x, g)
    return x + 0.5 * ((jax.nn.silu(h @ w_gate) * (h @ w_up)) @ w_down)


def gmlp_mix(h, w_in, ln_g, ln_b, w_s, b_s, w_out):
    bsz, s, _ = h.shape
    L = min(s, GMLP_CHUNK)
    z = jax.nn.gelu(h @ w_in, approximate=False)
    u, v = jnp.split(z, 2, axis=-1)
    vn = layernorm(v, ln_g, ln_b)
    tril = jnp.tril(jnp.ones((L, L), dtype=bool))
    ws = jnp.where(tril[None], w_s[:, :L, :L], 0).astype(vn.dtype)
    v5 = vn.reshape(bsz, s // L, L, GMLP_GROUPS, D_GATE // GMLP_GROUPS)
    mixed = jnp.einsum('gts,bnsgc->bntgc', ws, v5) + b_s[:, :L].T[:, :, None]
    gated = u * mixed.reshape(bsz, s, D_GATE)
    return gated @ w_out, vn


def shared_kv(x, kv_norm, w_k, w_v, w_f, b_f):
    bsz, s, _ = x.shape
    h = rmsnorm(x, kv_norm)
    k = (h @ w_k).reshape(bsz, s, N_HEADS, HEAD_DIM)
    v = (h @ w_v).reshape(bsz, s, N_HEADS, HEAD_DIM)
    logf = jax.nn.log_sigmoid((h @ w_f).astype(jnp.float32) + b_f.astype(jnp.float32)).astype(x.dtype)
    return k, v, logf


def fox_prompt(q, k, v, logf):
    bsz, s = q.shape[:2]
    nb = s // Q_BLOCK
    c = jnp.cumsum(logf.astype(jnp.float32), axis=1)
    c_k = jnp.transpose(c, (0, 2, 1))[:, :, None, :]
    k_pos = jnp.arange(s)
    q_blk = jnp.moveaxis(q.reshape(bsz, nb, Q_BLOCK, N_HEADS, HEAD_DIM), 1, 0)
    cq_blk = jnp.moveaxis(c.reshape(bsz, nb, Q_BLOCK, N_HEADS), 1, 0)
    pos_blk = jnp.arange(s).reshape(nb, Q_BLOCK)
    scale = HEAD_DIM ** -0.5

    def block(args):
        qb, cqb, qpos = args
        logits = jnp.einsum('bqhd,bkhd->bhqk', qb, k, preferred_element_type=jnp.float32) * scale
        logits = logits + jnp.transpose(cqb, (0, 2, 1))[..., None] - c_k
        logits = jnp.where(qpos[:, None] >= k_pos[None, :], logits, NEG_INF)
        p = jax.nn.softmax(logits, axis=-1).astype(v.dtype)
        return jnp.einsum('bhqk,bkhd->bqhd', p, v)

    out = lax.map(block, (q_blk, cq_blk, pos_blk))
    return jnp.moveaxis(out, 0, 1).reshape(bsz, s, N_HEADS * HEAD_DIM)


def fox_sample(q, k_all, v_all, logf_all, past_len):
    bsz, t = q.shape[:2]
    n = k_all.shape[1]
    c = jnp.cumsum(logf_all.astype(jnp.float32), axis=1)
    cq = c[:, past_len:]
    scale = HEAD_DIM ** -0.5
    logits = jnp.einsum('bqhd,bkhd->bhqk', q, k_all, preferred_element_type=jnp.float32) * scale
    logits = logits + jnp.transpose(cq, (0, 2, 1))[..., None] - jnp.transpose(c, (0, 2, 1))[:, :, None, :]
    mask = (past_len + jnp.arange(t))[:, None] >= jnp.arange(n)[None, :]
    logits = jnp.where(mask, logits, NEG_INF)
    p = jax.nn.softmax(logits, axis=-1).astype(v_all.dtype)
    out = jnp.einsum('bhqk,bkhd->bqhd', p, v_all)
    return out.reshape(bsz, t, N_HEADS * HEAD_DIM)


def run_trunk(x, p, cache):
    bsz, s, _ = x.shape
    gmlp_v = []
    for l in range(DEPTH):
        if l == N_A_LAYERS:
            k_new, v_new, logf_new = shared_kv(x, p['kv_norm'], p['w_k'], p['w_v'], p['w_f'], p['b_f'])
            if cache is None:
                ctx = (k_new, v_new, logf_new)
            else:
                ctx = (jnp.concatenate([cache[0], k_new], axis=1),
                       jnp.concatenate([cache[1], v_new], axis=1),
                       jnp.concatenate([cache[2], logf_new], axis=1))
        x = half_ffn(x, p['ffn1_norm'][l], p['ffn1_w_gate'][l], p['ffn1_w_up'][l], p['ffn1_w_down'][l])
        h = rmsnorm(x, p['mix_norm'][l])
        if l < N_A_LAYERS:
            a, vn = gmlp_mix(h, p['gmlp_w_in'][l], p['gmlp_ln_g'][l], p['gmlp_ln_b'][l],
                             p['gmlp_w_s'][l], p['gmlp_b_s'][l], p['gmlp_w_out'][l])
            gmlp_v.append(vn)
        else:
            j = l - N_A_LAYERS
            q = (h @ p['fox_w_q'][j]).reshape(bsz, s, N_HEADS, HEAD_DIM)
            if cache is None:
                o = fox_prompt(q, ctx[0], ctx[1], ctx[2])
            else:
                o = fox_sample(q, ctx[0], ctx[1], ctx[2], cache[0].shape[1])
            a = o @ p['fox_w_o'][j]
        x = x + a
        x = half_ffn(x, p['ffn2_norm'][l], p['ffn2_w_gate'][l], p['ffn2_w_up'][l], p['ffn2_w_down'][l])
    y = rmsnorm(x, p['final_norm'])
    return y, k_new, v_new, logf_new, gmlp_v


def setup_inputs(seed: int = 0) -> dict:
    key = jax.random.key(seed)
    ks = iter(jax.random.split(key, 40))

    def nrm(shape, scale):
        return scale * jax.random.normal(next(ks), shape, jnp.float32)

    def gain(shape):
        return 1.0 + 0.02 * jax.random.normal(next(ks), shape, jnp.float32)

    hw = N_HEADS * HEAD_DIM
    x_prompt = nrm((BATCH, SEQ, D_MODEL), 1.0)
    x_sample = nrm((DEC_BATCH, DEC_SEQ, D_MODEL), 1.0)
    cache_k = nrm((DEC_BATCH, PAST_LEN, N_HEADS, HEAD_DIM), 1.0)
    cache_v = nrm((DEC_BATCH, PAST_LEN, N_HEADS, HEAD_DIM), 1.0)
    b_f = jax.random.uniform(next(ks), (N_HEADS,), jnp.float32, 1.0, 6.0)
    cache_logf = jax.nn.log_sigmoid(b_f + nrm((DEC_BATCH, PAST_LEN, N_HEADS), 1.0))
    return {
        'x_prompt': x_prompt,
        'x_sample': x_sample,
        'cache_k': cache_k,
        'cache_v': cache_v,
        'cache_logf': cache_logf,
        'ffn1_norm': gain((DEPTH, D_MODEL)),
        'ffn1_w_gate': nrm((DEPTH, D_MODEL, D_FF), D_MODEL ** -0.5),
        'ffn1_w_up': nrm((DEPTH, D_MODEL, D_FF), D_MODEL ** -0.5),
        'ffn1_w_down': nrm((DEPTH, D_FF, D_MODEL), D_FF ** -0.5),
        'mix_norm': gain((DEPTH, D_MODEL)),
        'ffn2_norm': gain((DEPTH, D_MODEL)),
        'ffn2_w_gate': nrm((DEPTH, D_MODEL, D_FF), D_MODEL ** -0.5),
        'ffn2_w_up': nrm((DEPTH, D_MODEL, D_FF), D_MODEL ** -0.5),
        'ffn2_w_down': nrm((DEPTH, D_FF, D_MODEL), D_FF ** -0.5),
        'gmlp_w_in': nrm((N_A_LAYERS, D_MODEL, 2 * D_GATE), D_MODEL ** -0.5),
        'gmlp_ln_g': gain((N_A_LAYERS, D_GATE)),
        'gmlp_ln_b': nrm((N_A_LAYERS, D_GATE), 0.02),
        'gmlp_w_s': nrm((N_A_LAYERS, GMLP_GROUPS, GMLP_CHUNK, GMLP_CHUNK), GMLP_CHUNK ** -0.5),
        'gmlp_b_s': 1.0 + nrm((N_A_LAYERS, GMLP_GROUPS, GMLP_CHUNK), 0.1),
        'gmlp_w_out': nrm((N_A_LAYERS, D_GATE, D_MODEL), D_GATE ** -0.5),
        'kv_norm': gain((D_MODEL,)),
        'w_k': nrm((D_MODEL, hw), D_MODEL ** -0.5),
        'w_v': nrm((D_MODEL, hw), D_MODEL ** -0.5),
        'w_f': nrm((D_MODEL, N_HEADS), 0.5 * D_MODEL ** -0.5),
        'b_f': b_f,
        'fox_w_q': nrm((N_B_LAYERS, D_MODEL, hw), D_MODEL ** -0.5),
        'fox_w_o': nrm((N_B_LAYERS, hw, D_MODEL), hw ** -0.5),
        'final_norm': gain((D_MODEL,)),
    }


def reference(x_prompt, x_sample, cache_k, cache_v, cache_logf,
              ffn1_norm, ffn1_w_gate, ffn1_w_up, ffn1_w_down, mix_norm,
              ffn2_norm, ffn2_w_gate, ffn2_w_up, ffn2_w_down,
              gmlp_w_in, gmlp_ln_g, gmlp_ln_b, gmlp_w_s, gmlp_b_s, gmlp_w_out,
              kv_norm, w_k, w_v, w_f, b_f, fox_w_q, fox_w_o, final_norm):
    p = dict(ffn1_norm=ffn1_norm, ffn1_w_gate=ffn1_w_gate, ffn1_w_up=ffn1_w_up, ffn1_w_down=ffn1_w_down,
             mix_norm=mix_norm, ffn2_norm=ffn2_norm, ffn2_w_gate=ffn2_w_gate, ffn2_w_up=ffn2_w_up,
             ffn2_w_down=ffn2_w_down, gmlp_w_in=gmlp_w_in, gmlp_ln_g=gmlp_ln_g, gmlp_ln_b=gmlp_ln_b,
             gmlp_w_s=gmlp_w_s, gmlp_b_s=gmlp_b_s, gmlp_w_out=gmlp_w_out, kv_norm=kv_norm,
             w_k=w_k, w_v=w_v, w_f=w_f, b_f=b_f, fox_w_q=fox_w_q, fox_w_o=fox_w_o, final_norm=final_norm)
    y_prompt, k_prompt, v_prompt, logf_prompt, _ = run_trunk(x_prompt, p, None)
    y_sample, k_sample, v_sample, logf_sample, gv = run_trunk(x_sample, p, (cache_k, cache_v, cache_logf))
    gmlp_v_sample = jnp.stack(gv, axis=0)
    return (y_prompt, y_sample, k_prompt, v_prompt, logf_prompt,
            k_sample, v_sample, logf_sample, gmlp_v_sample)
```

```python
import contextlib
import numpy as np
import concourse.bass as bass
import concourse.mybir as mybir
from concourse.bass_utils import run_bass_kernel_spmd

F32 = mybir.dt.float32
BF16 = mybir.dt.bfloat16
AF = mybir.ActivationFunctionType
ALU = mybir.AluOpType
AX = mybir.AxisListType

D = 2048
DFF = 5632
NT = 528
NEG = -1.0e30
TG = [(0, 264), (264, 528)]
TC = [(0, 128), (128, 256), (256, 384), (384, 512), (512, 528)]
SCALE = 128 ** -0.5
P_F1, P_MX, P_F2, P_KV, P_FN, P_LG, P_LB, NPRM = 0, 64, 128, 192, 208, 224, 288, 352


class Tl:
    __slots__ = ("ap", "lw", "rd", "name", "ps")

    def __init__(self, ap, name="", ps=False):
        self.ap = ap
        self.lw = None
        self.rd = {}
        self.name = name
        self.ps = ps

    def __getitem__(self, k):
        return self.ap[k]


class Eng:
    def __init__(self, name):
        self.name = name
        self.count = 0
        self.ops = []
        self.waited = {}
        self.pool = []
        self.pool_cum = []
        self.pool_i = 0


class KB:
    NPOOL = 16

    def __init__(self, nc, stack):
        self.nc = nc
        self.stack = stack
        self.E = {n: Eng(n) for n in ("pe", "act", "dve", "pool", "sp")}
        self.sems = {}
        for n in self.E:
            self.sems[n] = stack.enter_context(nc.semaphore("s_" + n))
        for q in ("sp", "pool"):
            e = self.E[q]
            for i in range(self.NPOOL):
                key = f"d_{q}{i}"
                self.sems[key] = stack.enter_context(nc.semaphore(key))
                e.pool.append(key)
                e.pool_cum.append(0)
        self.ntile = 0

    def sb(self, shape, dt, name=None):
        self.ntile += 1
        name = name or f"t{self.ntile}"
        h = self.stack.enter_context(self.nc.sbuf_tensor(name, list(shape), dt))
        return Tl(h, name)

    def ps(self, shape, dt, name=None):
        self.ntile += 1
        name = name or f"p{self.ntile}"
        h = self.stack.enter_context(self.nc.psum_tensor(name, list(shape), dt))
        return Tl(h, name, ps=True)

    def dram(self, name, shape, dt):
        h = self.nc.dram_tensor(name, list(shape), dt).ap()
        return Tl(h, name)

    def _need(self, e, waits, tok, same_ok):
        if tok is None:
            return
        k, v = tok
        if same_ok and k == e.name:
            return
        if e.waited.get(k, 0) >= v:
            return
        if waits.get(k, 0) < v:
            waits[k] = v

    def _deps(self, e, reads, writes):
        waits = {}
        for t in reads:
            self._need(e, waits, t.lw, e.name == "pe")
            if getattr(t, "ps", False):
                for k, v in t.rd.items():
                    self._need(e, waits, (k, v), True)
        for t in writes:
            self._need(e, waits, t.lw, True)
            for k, v in t.rd.items():
                self._need(e, waits, (k, v), True)
        for k, v in waits.items():
            e.waited[k] = v
        return waits

    def op(self, eng, fn, reads=(), writes=(), inc=True):
        e = self.E[eng]
        waits = self._deps(e, reads, writes)
        if inc:
            e.count += 1
            tok = (eng, e.count)
        else:
            tok = (eng, e.count + 1)
        for t in reads:
            if t.rd.get(tok[0], 0) < tok[1]:
                t.rd[tok[0]] = tok[1]
        for t in writes:
            t.lw = tok
            t.rd = {}
        e.ops.append((waits, fn, eng if inc else None, 1))
        return tok

    def dma(self, q, out_t, in_t, out_ap, in_ap):
        e = self.E[q]
        reads = [in_t] if in_t is not None else []
        writes = [out_t] if out_t is not None else []
        waits = self._deps(e, reads, writes)
        i = e.pool_i
        e.pool_i = (i + 1) % len(e.pool)
        key = e.pool[i]
        prev = e.pool_cum[i]
        if prev > 0 and e.waited.get(key, 0) < prev:
            waits[key] = max(waits.get(key, 0), prev)
            e.waited[key] = prev
        e.pool_cum[i] = prev + 16
        tok = (key, prev + 16)
        for t in reads:
            t.rd[key] = tok[1]
        for t in writes:
            t.lw = tok
            t.rd = {}
        e.ops.append((waits, lambda g: g.dma_start(out=out_ap, in_=in_ap), key, 16))
        return tok

    def emit(self):
        nc = self.nc
        fin = {}
        for q in ("sp", "pool"):
            e = self.E[q]
            for key, cum in zip(e.pool, e.pool_cum):
                if cum > 0:
                    fin[key] = cum
        for n in ("pe", "act", "dve", "pool"):
            if self.E[n].count > 0:
                fin[n] = self.E[n].count
        sems = self.sems
        E = self.E

        def run(e, g):
            for waits, fn, inck, incv in e.ops:
                for k, v in waits.items():
                    g.wait_ge(sems[k], v)
                ins = fn(g)
                if inck is not None:
                    ins.then_inc(sems[inck], incv)

        with nc.Block() as block:
            @block.tensor
            def _(g):
                run(E["pe"], g)

            @block.scalar
            def _(g):
                run(E["act"], g)

            @block.vector
            def _(g):
                run(E["dve"], g)

            @block.gpsimd
            def _(g):
                run(E["pool"], g)

            @block.sync
            def _(g):
                run(E["sp"], g)
                for k, v in fin.items():
                    g.wait_ge(sems[k], v)


def build_program():
    nc = bass.Bass("TRN2", target_bir_lowering=False)

    def din(name, shape, dt=F32):
        return nc.dram_tensor(name, list(shape), dt, kind="ExternalInput").ap()

    def dout(name, shape):
        return nc.dram_tensor(name, list(shape), F32, kind="ExternalOutput").ap()

    xin = din("xin", [16 * NT, D])
    prm_d = din("prm", [128, NPRM])
    vb_d = din("vb", [2, 56])
    wg = [din("ffn1_w_gate", [WL, D, DFF]), din("ffn2_w_gate", [WL, D, DFF])]
    wu = [din("ffn1_w_up", [WL, D, DFF]), din("ffn2_w_up", [WL, D, DFF])]
    wd = [din("ffn1_w_down", [WL, DFF, D]), din("ffn2_w_down", [WL, DFF, D])]
    w_in = din("gmlp_w_in", [2, D, 8192])
    w_out = din("gmlp_w_out", [2, 4096, D])
    wsT_d = din("wsT", [2, 4, 128, 128])
    bs_d = din("gmlp_b_s", [2, 4, 128])
    lng_d = din("gmlp_ln_g", [2, 4096])
    lnb_d = din("gmlp_ln_b", [2, 4096])
    wk_d = din("w_k", [D, D])
    wv_d = din("w_v", [D, D])
    wf_d = din("w_f", [D, 16])
    bf_d = din("b_f", [16])
    wq_d = din("fox_w_q", [2, D, D])
    wo_d = din("fox_w_o", [2, D, D])
    ck_d = din("cache_k", [2, 2048, D])
    cv_d = din("cache_v", [2, 2048, D])
    cl_d = din("cache_logf", [2, 2048, 16])

    y_p = dout("y_p", [1024, D]); y_s = dout("y_s", [32, D])
    k_p = dout("k_p", [1024, D]); v_p = dout("v_p", [1024, D]); lf_p = dout("lf_p", [1024, 16])
    k_s = dout("k_s", [32, D]); v_s = dout("v_s", [32, D]); lf_s = dout("lf_s", [32, 16])
    gv = dout("gv", [2, 32, 4096])

    with contextlib.ExitStack() as st:
        kb = KB(nc, st)
        op, dma = kb.op, kb.dma
        xT = kb.sb([128, 16, NT], F32, "xT")
        hT = kb.sb([128, 16, NT], BF16, "hT")
        actb = [kb.sb([128, 2, NT], BF16, f"act{i}") for i in range(2)]
        WGU = [kb.sb([128, 8192], BF16, f"WGU{i}") for i in range(2)]
        WDn = [kb.sb([128, 4096], BF16, f"WDn{i}") for i in range(2)]
        prm = kb.sb([128, NPRM], F32, "prm_sb")
        sqt = [kb.sb([128, NT], BF16, f"sqt{i}") for i in range(2)]
        rstd = kb.sb([128, NT], F32, "rstd")
        tmpA = [kb.sb([128, NT], F32, f"tmpA{i}") for i in range(2)]
        tmpB = [kb.sb([128, NT], BF16, f"tmpB{i}") for i in range(2)]
        identf = kb.sb([128, 128], F32, "identf")
        identb = kb.sb([128, 128], BF16, "identb")
        onesb = kb.sb([128, 128], BF16, "onesb")
        onesf = kb.sb([128, 128], F32, "onesf")
        UT = kb.sb([128, 128], F32, "UT")
        SL = kb.sb([128, 128], F32, "SL")
        TRIN = kb.sb([128, 128], F32, "TRIN")
        Z = kb.sb([128, 21504], BF16, "Z")
        class _V:
            def __init__(self, ap): self.ap = ap
            def __getitem__(self, k): return self.ap[k]
        class _Zv:
            def __init__(self, ap): self.ap = ap
            def __getitem__(self, k): return self.ap[k]
            lw = property(lambda s_: Z.lw, lambda s_, v: setattr(Z, "lw", v))
            rd = property(lambda s_: Z.rd, lambda s_, v: setattr(Z, "rd", v))
        zt = [_V(Z[:, i * 4096:(i + 1) * 4096]) for i in range(5)]
        stg = _V(Z[:, 16896:20992].bitcast(F32))
        qT = _V(Z[:, 0:8448].rearrange("p (h n) -> p h n", h=16))
        oT = _V(Z[:, 8448:16896].rearrange("p (h n) -> p h n", h=16))
        st1 = kb.sb([128, 5, 8], F32, "st1")
        st2 = kb.sb([128, 5, 8], F32, "st2")
        stv = kb.sb([128, 5, 8], F32, "stv")
        wsT = [kb.sb([128, 128], BF16, f"wsT{g}") for g in range(4)]
        wsf = kb.sb([128, 128], F32, "wsf")
        rsb = [kb.sb([128, 128], F32, f"rsb{g}") for g in range(4)]
        bsb = [kb.sb([128, 128], F32, f"bsb{g}") for g in range(4)]
        lfS = kb.sb([128, 64, 16], F32, "lfS")
        lfs = [kb.sb([128, 16], F32, f"lfs{a}") for a in range(2)]
        wfb = kb.sb([128, 16, 16], BF16, "wfb")
        bfb = kb.sb([128, 16], F32, "bfb")
        kTs = [kb.sb([128, 16, 16], BF16, f"kTs{a}") for a in range(2)]
        vss = [kb.sb([128, 2048], BF16, f"vss{a}") for a in range(2)]
        vbt = kb.sb([128, 2, 56], F32, "vbt")
        PG = [kb.ps([128, 512], F32, f"PG{i}") for i in range(2)]
        PU = [kb.ps([128, 512], F32, f"PU{i}") for i in range(2)]
        PD = [kb.ps([128, 512], F32, f"PD{i}") for i in range(2)]
        PM = [kb.ps([128, 512], F32, f"PM{i}") for i in range(2)]
        PMb = [Tl(PM[i].ap.bitcast(BF16), f"PMb{i}") for i in range(2)]
        kS = kb.dram("kS", [16, 16, 128, 512], BF16)
        vS = kb.dram("vS", [16, 4, 128, 2048], BF16)

        op("pool", lambda g: g.memset(identf[:], 1.0), [], [identf])
        op("pool", lambda g: g.affine_select(out=identf[:], in_=identf[:], pattern=[[-1, 128]], compare_op=ALU.is_equal, fill=0.0, base=0, channel_multiplier=1), [identf], [identf])
        op("pool", lambda g: g.memset(UT[:], 1.0), [], [UT])
        op("pool", lambda g: g.affine_select(out=UT[:], in_=UT[:], pattern=[[1, 128]], compare_op=ALU.is_ge, fill=0.0, base=0, channel_multiplier=-1), [UT], [UT])
        op("pool", lambda g: g.memset(SL[:], 1.0), [], [SL])
        op("pool", lambda g: g.affine_select(out=SL[:], in_=SL[:], pattern=[[-1, 128]], compare_op=ALU.is_gt, fill=0.0, base=0, channel_multiplier=1), [SL], [SL])
        op("pool", lambda g: g.memset(TRIN[:], 0.0), [], [TRIN])
        op("pool", lambda g: g.affine_select(out=TRIN[:], in_=TRIN[:], pattern=[[1, 128]], compare_op=ALU.is_ge, fill=NEG, base=0, channel_multiplier=-1), [TRIN], [TRIN])
        op("pool", lambda g: g.memset(onesb[:], 1.0), [], [onesb])
        op("pool", lambda g: g.memset(onesf[:], 1.0), [], [onesf])
        op("dve", lambda g: g.tensor_copy(out=identb[:], in_=identf[:]), [identf], [identb])
        dma("sp", prm, None, prm[:], prm_d[:, :])
        dma("sp", bfb, None, bfb[:], bf_d.partition_broadcast(128))
        dma("sp", vbt, None, vbt[:], vb_d.partition_broadcast(128))
        dma("pool", wfb, None, wfb[:], wf_d.rearrange("(kc p) f -> p kc f", p=128))

        cnt = {"w": 0, "wd": 0, "m": 0, "d": 0, "a": 0, "t": 0}

        def pcol(base, i):
            return prm[:, base + i:base + i + 1]

        def load_x(tile):
            for ci, (c0, c1) in enumerate(TC):
                M = c1 - c0
                r0 = tile * NT + c0
                dma("sp", Z, None, stg[0:M, :], xin[r0:r0 + M, :])
                for q4 in range(4):
                    pm = PM[cnt["m"] % 2]; cnt["m"] += 1
                    for j in range(4):
                        kc = q4 * 4 + j
                        op("pe", lambda g, pm=pm, j=j, kc=kc, M=M: g.transpose(out=pm[:, j * 128:j * 128 + M], in_=stg[0:M, kc * 128:(kc + 1) * 128], identity=identf[0:M, 0:M]),
                           [Z, identf], [pm], inc=(j == 3))
                    for j in range(4):
                        kc = q4 * 4 + j
                        eng = "act" if j % 2 else "dve"
                        if eng == "act":
                            op("act", lambda g, pm=pm, j=j, kc=kc, M=M, c0=c0, c1=c1: g.activation(out=xT[:, kc, c0:c1], in_=pm[:, j * 128:j * 128 + M], func=AF.Copy), [pm], [xT])
                        else:
                            op("dve", lambda g, pm=pm, j=j, kc=kc, M=M, c0=c0, c1=c1: g.tensor_copy(out=xT[:, kc, c0:c1], in_=pm[:, j * 128:j * 128 + M]), [pm], [xT])

        def rstd_compute():
            for kc in range(16):
                s = sqt[kc % 2]
                op("act", lambda g, s=s, kc=kc: g.activation(out=s[:], in_=xT[:, kc, :], func=AF.Square), [xT], [s])
                for ti, (a, b) in enumerate(TG):
                    op("pe", lambda g, s=s, ti=ti, a=a, b=b, kc=kc: g.matmul(PM[ti][:, 0:264], lhsT=onesb[:], rhs=s[:, a:b], start=(kc == 0), stop=(kc == 15)),
                       [s, onesb], [PM[ti]], inc=(kc == 15 or ti == 1))
            for ti, (a, b) in enumerate(TG):
                op("act", lambda g, ti=ti, a=a, b=b: g.activation(out=rstd[:, a:b], in_=PM[ti][:, 0:264], func=AF.Sqrt, bias=1e-6, scale=1.0 / D), [PM[ti]], [rstd])
            op("dve", lambda g: g.reciprocal(out=rstd[:], in_=rstd[:]), [rstd], [rstd])

        def rmsnorm(base):
            rstd_compute()
            for kc in range(16):
                op("dve", lambda g, kc=kc: g.scalar_tensor_tensor(out=hT[:, kc, :], in0=xT[:, kc, :], scalar=pcol(base, kc), in1=rstd[:], op0=ALU.mult, op1=ALU.mult),
                   [xT, prm, rstd], [hT])

        def proj_chunk(wt, wap_fn, src=None):
            src = src or hT
            for kc in range(16):
                for ti, (a, b) in enumerate(TG):
                    op("pe", lambda g, kc=kc, ti=ti, a=a, b=b: g.matmul(PG[ti][:, 0:264], lhsT=wap_fn(kc), rhs=src[:, kc, a:b], start=(kc == 0), stop=(kc == 15)),
                       [wt, src], [PG[ti]], inc=(kc == 15))

        def acc_down(wt, wap_fn, srct, nk, rhs_fn, scale):
            for dc in range(16):
                for ti, (a, b) in enumerate(TG):
                    pd = PD[cnt["d"] % 2]; cnt["d"] += 1
                    for j in range(nk):
                        op("pe", lambda g, pd=pd, j=j, dc=dc, a=a, b=b: g.matmul(pd[:, 0:264], lhsT=wap_fn(j, dc), rhs=rhs_fn(j, a, b), start=(j == 0), stop=(j == nk - 1)),
                           [wt, srct], [pd], inc=(j == nk - 1))
                    op("dve", lambda g, pd=pd, dc=dc, a=a, b=b: g.scalar_tensor_tensor(out=xT[:, dc, a:b], in0=pd[:, 0:264], scalar=scale, in1=xT[:, dc, a:b], op0=ALU.mult, op1=ALU.add),
                       [pd, xT], [xT])

        def ffn(l, which, base):
            rmsnorm(base + l * 16)
            wgl = wg[which][l].rearrange("(kc p) f -> p kc f", p=128)
            wul = wu[which][l].rearrange("(kc p) f -> p kc f", p=128)
            wdl = wd[which][l]
            NG = 22

            def load_gu(grp):
                wt = WGU[cnt["w"] % 2]; cnt["w"] += 1
                dma("pool", wt, None, wt[:, 0:4096].rearrange("p (kc f) -> p kc f", kc=16), wgl[:, :, grp * 256:(grp + 1) * 256])
                dma("pool", wt, None, wt[:, 4096:8192].rearrange("p (kc f) -> p kc f", kc=16), wul[:, :, grp * 256:(grp + 1) * 256])
                return wt

            def load_d(grp):
                wt = WDn[cnt["wd"] % 2]; cnt["wd"] += 1
                dma("pool", wt, None, wt[:, 0:4096].rearrange("p (j d) -> p j d", j=2), wdl[grp * 256:(grp + 1) * 256, :].rearrange("(j p) d -> p j d", p=128))
                return wt

            def gate_up(wt, ab):
                for j in range(2):
                    proj_chunk(wt, lambda kc, j=j, wt=wt: wt[:, kc * 256 + j * 128: kc * 256 + (j + 1) * 128])
                    for ti, (a, b) in enumerate(TG):
                        op("act", lambda g, ti=ti: g.activation(out=tmpA[ti][:, 0:264], in_=PG[ti][:, 0:264], func=AF.Silu), [PG[ti]], [tmpA[ti]])
                    for kc in range(16):
                        for ti, (a, b) in enumerate(TG):
                            op("pe", lambda g, kc=kc, ti=ti, a=a, b=b, j=j, wt=wt: g.matmul(PU[ti][:, 0:264], lhsT=wt[:, 4096 + kc * 256 + j * 128: 4096 + kc * 256 + (j + 1) * 128], rhs=hT[:, kc, a:b], start=(kc == 0), stop=(kc == 15)),
                               [wt, hT], [PU[ti]], inc=(kc == 15))
                    for ti, (a, b) in enumerate(TG):
                        op("dve", lambda g, ti=ti, a=a, b=b, j=j, ab=ab: g.tensor_tensor(out=ab[:, j, a:b], in0=tmpA[ti][:, 0:264], in1=PU[ti][:, 0:264], op=ALU.mult),
                           [tmpA[ti], PU[ti]], [ab])

            def down(wt, ab):
                acc_down(wt, lambda j, dc, wt=wt: wt[:, j * 2048 + dc * 128: j * 2048 + (dc + 1) * 128], ab, 2, lambda j, a, b, ab=ab: ab[:, j, a:b], 0.5)

            gu = {0: load_gu(0)}
            dd = {0: load_d(0)}
            prev = None
            for grp in range(NG):
                if grp + 1 < NG:
                    gu[grp + 1] = load_gu(grp + 1)
                ab = actb[cnt["a"] % 2]; cnt["a"] += 1
                gate_up(gu[grp], ab)
                if prev is not None:
                    down(*prev)
                if grp + 1 < NG:
                    dd[grp + 1] = load_d(grp + 1)
                prev = (dd[grp], ab)
            down(*prev)

        def gmlp(l, own_a):
            rmsnorm(P_MX + l * 16)
            for g4 in range(4):
                dma("sp", wsf, None, wsf[:], wsT_d[l, g4])
                op("dve", lambda g, g4=g4: g.tensor_tensor(out=wsT[g4][:], in0=wsf[:], in1=UT[:], op=ALU.mult), [wsf, UT], [wsT[g4]])
                pm = PM[cnt["m"] % 2]; cnt["m"] += 1
                op("pe", lambda g, pm=pm, g4=g4: g.matmul(pm[:, 0:128], lhsT=onesb[:], rhs=wsT[g4][:], start=True, stop=True), [onesb, wsT[g4]], [pm])
                op("act", lambda g, pm=pm, g4=g4: g.activation(out=rsb[g4][:], in_=pm[:, 0:128], func=AF.Copy), [pm], [rsb[g4]])
                dma("sp", bsb[g4], None, bsb[g4][:], bs_d[l, g4].partition_broadcast(128))
            winl = w_in[l].rearrange("(kc p) f -> p kc f", p=128)
            op("dve", lambda g: g.memset(st1[:], 0.0), [], [st1])
            op("dve", lambda g: g.memset(st2[:], 0.0), [], [st2])
            for cg in range(8):
                wt = WGU[cnt["w"] % 2]; cnt["w"] += 1
                dma("pool", wt, None, wt[:, 0:8192].rearrange("p (kc f) -> p kc f", kc=16), winl[:, :, 4096 + cg * 512: 4096 + (cg + 1) * 512])
                for ci, (c0, c1) in enumerate(TC):
                    M = c1 - c0
                    pm = PM[cnt["m"] % 2]; cnt["m"] += 1
                    for kc in range(16):
                        op("pe", lambda g, pm=pm, kc=kc, c0=c0, c1=c1, M=M, wt=wt: g.matmul(pm[0:M, :], lhsT=hT[:, kc, c0:c1], rhs=wt[:, kc * 512:(kc + 1) * 512], start=(kc == 0), stop=(kc == 15)),
                           [hT, wt], [pm], inc=(kc == 15))
                    op("act", lambda g, pm=pm, ci=ci, cg=cg, M=M: g.activation(out=zt[ci][0:M, cg * 512:(cg + 1) * 512], in_=pm[0:M, :], func=AF.Gelu, accum_out=st1[0:M, ci, cg:cg + 1]),
                       [pm], [Z, st1])
                    jt = tmpB[cnt["t"] % 2]; cnt["t"] += 1
                    op("dve", lambda g, jt=jt, ci=ci, cg=cg, M=M: g.scalar_tensor_tensor(out=jt[0:M, 0:512], in0=zt[ci][0:M, cg * 512:(cg + 1) * 512], scalar=1.0, in1=zt[ci][0:M, cg * 512:(cg + 1) * 512], op0=ALU.mult, op1=ALU.mult, accum_out=st2[0:M, ci, cg:cg + 1]),
                       [Z], [jt, st2])
            for ci, (c0, c1) in enumerate(TC):
                M = c1 - c0
                op("dve", lambda g, ci=ci, M=M: g.reduce_sum(out=stv[0:M, ci, 0:1], in_=st1[0:M, ci, :], axis=AX.X), [st1], [stv])
                op("dve", lambda g, ci=ci, M=M: g.reduce_sum(out=stv[0:M, ci, 1:2], in_=st2[0:M, ci, :], axis=AX.X), [st2, stv], [stv])
                op("dve", lambda g, ci=ci, M=M: g.tensor_scalar(out=stv[0:M, ci, 0:2], in0=stv[0:M, ci, 0:2], scalar1=1.0 / 4096, scalar2=None, op0=ALU.mult), [stv], [stv])
                op("dve", lambda g, ci=ci, M=M: g.tensor_tensor(out=stv[0:M, ci, 2:3], in0=stv[0:M, ci, 0:1], in1=stv[0:M, ci, 0:1], op=ALU.mult), [stv], [stv])
                op("dve", lambda g, ci=ci, M=M: g.tensor_tensor(out=stv[0:M, ci, 2:3], in0=stv[0:M, ci, 1:2], in1=stv[0:M, ci, 2:3], op=ALU.subtract), [stv], [stv])
                op("act", lambda g, ci=ci, M=M: g.activation(out=stv[0:M, ci, 2:3], in_=stv[0:M, ci, 2:3], func=AF.Sqrt, bias=1e-5, scale=1.0), [stv], [stv])
                op("dve", lambda g, ci=ci, M=M: g.reciprocal(out=stv[0:M, ci, 2:3], in_=stv[0:M, ci, 2:3]), [stv], [stv])
                op("dve", lambda g, ci=ci, M=M: g.scalar_tensor_tensor(out=stv[0:M, ci, 3:4], in0=stv[0:M, ci, 0:1], scalar=-1.0, in1=stv[0:M, ci, 2:3], op0=ALU.mult, op1=ALU.mult), [stv], [stv])
                op("act", lambda g, ci=ci, M=M: g.activation(out=zt[ci][0:M, :], in_=zt[ci][0:M, :], func=AF.Identity, bias=stv[0:M, ci, 3:4], scale=stv[0:M, ci, 2:3]), [Z, stv], [Z])
            if own_a is not None:
                for pc in range(8):
                    ga = tmpA[0]; ba = tmpA[1]
                    dma("sp", ga, None, ga[0:16, 0:512], lng_d[l, pc * 512:(pc + 1) * 512].partition_broadcast(16))
                    dma("sp", ba, None, ba[0:16, 0:512], lnb_d[l, pc * 512:(pc + 1) * 512].partition_broadcast(16))
                    op("dve", lambda g, pc=pc, ga=ga: g.tensor_tensor(out=ga[0:16, 0:512], in0=zt[4][0:16, pc * 512:(pc + 1) * 512], in1=ga[0:16, 0:512], op=ALU.mult), [Z, ga], [ga])
                    op("dve", lambda g, ga=ga, ba=ba: g.tensor_tensor(out=ga[0:16, 0:512], in0=ga[0:16, 0:512], in1=ba[0:16, 0:512], op=ALU.add), [ga, ba], [ga])
                    dma("sp", None, ga, gv[l, own_a * 16:(own_a + 1) * 16, pc * 512:(pc + 1) * 512], ga[0:16, 0:512])
            woutl = w_out[l]

            def load_u(cgrp):
                wt = WGU[cnt["w"] % 2]; cnt["w"] += 1
                dma("pool", wt, None, wt[:, 0:4096].rearrange("p (kc f) -> p kc f", kc=16), winl[:, :, cgrp * 256:(cgrp + 1) * 256])
                return wt

            def load_o(cgrp):
                wt = WDn[cnt["wd"] % 2]; cnt["wd"] += 1
                dma("pool", wt, None, wt[:, 0:4096].rearrange("p (j d) -> p j d", j=2), woutl[cgrp * 256:(cgrp + 1) * 256, :].rearrange("(j p) d -> p j d", p=128))
                return wt

            def umix(wt, ab, cgrp):
                for j in range(2):
                    cc = cgrp * 2 + j
                    g4 = cc // 8
                    proj_chunk(wt, lambda kc, j=j, wt=wt: wt[:, kc * 256 + j * 128: kc * 256 + (j + 1) * 128])
                    uT = tmpB[cnt["t"] % 2]; cnt["t"] += 1
                    for ti, (a, b) in enumerate(TG):
                        op("act", lambda g, uT=uT, ti=ti, a=a, b=b: g.activation(out=uT[:, a:b], in_=PG[ti][:, 0:264], func=AF.Gelu), [PG[ti]], [uT])
                    for ci in range(4):
                        op("pe", lambda g, ci=ci, cc=cc, g4=g4: g.matmul(PU[0][:, ci * 128:(ci + 1) * 128], lhsT=zt[ci][:, cc * 128:(cc + 1) * 128], rhs=wsT[g4][:], start=True, stop=True),
                           [Z, wsT[g4]], [PU[0]])
                    op("pe", lambda g, cc=cc, g4=g4: g.matmul(PU[1][:, 0:16], lhsT=zt[4][0:16, cc * 128:(cc + 1) * 128], rhs=wsT[g4][0:16, 0:16], start=True, stop=True),
                       [Z, wsT[g4]], [PU[1]])
                    E = tmpA[0]
                    for (c0, c1) in TC:
                        op("dve", lambda g, E=E, g4=g4, cc=cc, c0=c0, c1=c1: g.scalar_tensor_tensor(out=E[:, c0:c1], in0=rsb[g4][:, 0:c1 - c0], scalar=pcol(P_LB + l * 32, cc), in1=bsb[g4][:, 0:c1 - c0], op0=ALU.mult, op1=ALU.add),
                           [rsb[g4], bsb[g4], prm], [E])
                    op("dve", lambda g, E=E, cc=cc: g.scalar_tensor_tensor(out=E[:, 0:512], in0=PU[0][:, 0:512], scalar=pcol(P_LG + l * 32, cc), in1=E[:, 0:512], op0=ALU.mult, op1=ALU.add),
                       [PU[0], prm, E], [E])
                    op("dve", lambda g, E=E, cc=cc: g.scalar_tensor_tensor(out=E[:, 512:528], in0=PU[1][:, 0:16], scalar=pcol(P_LG + l * 32, cc), in1=E[:, 512:528], op0=ALU.mult, op1=ALU.add),
                       [PU[1], prm, E], [E])
                    op("dve", lambda g, E=E, uT=uT, j=j, ab=ab: g.tensor_tensor(out=ab[:, j, :], in0=E[:], in1=uT[:], op=ALU.mult), [E, uT], [ab])

            def wout(wt, ab):
                acc_down(wt, lambda j, dc, wt=wt: wt[:, j * 2048 + dc * 128: j * 2048 + (dc + 1) * 128], ab, 2, lambda j, a, b, ab=ab: ab[:, j, a:b], 1.0)

            NG = 16
            gu = {0: load_u(0)}
            dd = {0: load_o(0)}
            prev = None
            for cgrp in range(NG):
                if cgrp + 1 < NG:
                    gu[cgrp + 1] = load_u(cgrp + 1)
                ab = actb[cnt["a"] % 2]; cnt["a"] += 1
                umix(gu[cgrp], ab, cgrp)
                if prev is not None:
                    wout(*prev)
                if cgrp + 1 < NG:
                    dd[cgrp + 1] = load_o(cgrp + 1)
                prev = (dd[cgrp], ab)
            wout(*prev)

        def kv_phase(slot, own_a):
            rmsnorm(P_KV)
            wkl = wk_d.rearrange("(kc p) f -> p kc f", p=128)
            wvl = wv_d.rearrange("(kc p) f -> p kc f", p=128)
            for c4 in range(4 if 'kt' in KVS else 0):
                wt = WGU[cnt["w"] % 2]; cnt["w"] += 1
                dma("pool", wt, None, wt[:, 0:8192].rearrange("p (kc f) -> p kc f", kc=16), wkl[:, :, c4 * 512:(c4 + 1) * 512])
                for j in range(4):
                    h = c4 * 4 + j
                    proj_chunk(wt, lambda kc, j=j, wt=wt: wt[:, kc * 512 + j * 128: kc * 512 + (j + 1) * 128])
                    kt = tmpB[cnt["t"] % 2]; cnt["t"] += 1
                    for ti, (a, b) in enumerate(TG):
                        op("act", lambda g, kt=kt, ti=ti, a=a, b=b: g.activation(out=kt[:, a:b], in_=PG[ti][:, 0:264], func=AF.Copy), [PG[ti]], [kt])
                    dma("sp", kS, kt, kS[slot, h], kt[:, 0:512])
                    if own_a is not None:
                        op("dve", lambda g, kt=kt, h=h: g.tensor_copy(out=kTs[own_a][:, h, :], in_=kt[:, 512:528]), [kt], [kTs[own_a]])
                if own_a is not None and 'ktok' in KVS:
                    for ci, (c0, c1) in enumerate(TC):
                        M = c1 - c0
                        pm = PM[cnt["m"] % 2]; cnt["m"] += 1
                        for kc in range(16):
                            op("pe", lambda g, pm=pm, kc=kc, c0=c0, c1=c1, M=M, wt=wt: g.matmul(pm[0:M, :], lhsT=hT[:, kc, c0:c1], rhs=wt[:, kc * 512:(kc + 1) * 512], start=(kc == 0), stop=(kc == 15)),
                               [hT, wt], [pm], inc=(kc == 15))
                        ot = tmpA[cnt["t"] % 2]; cnt["t"] += 1
                        op("act", lambda g, pm=pm, ot=ot, M=M: g.activation(out=ot[0:M, 0:512], in_=pm[0:M, :], func=AF.Copy), [pm], [ot])
                        if ci < 4:
                            dma("sp", None, ot, k_p[own_a * 512 + c0: own_a * 512 + c1, c4 * 512:(c4 + 1) * 512], ot[0:M, 0:512])
                        else:
                            dma("sp", None, ot, k_s[own_a * 16:(own_a + 1) * 16, c4 * 512:(c4 + 1) * 512], ot[0:M, 0:512])
            for c4 in range(4 if 'v' in KVS else 0):
                wt = WGU[cnt["w"] % 2]; cnt["w"] += 1
                dma("pool", wt, None, wt[:, 0:8192].rearrange("p (kc f) -> p kc f", kc=16), wvl[:, :, c4 * 512:(c4 + 1) * 512])
                for ci, (c0, c1) in enumerate(TC):
                    M = c1 - c0
                    if ci == 4 and own_a is None:
                        continue
                    pm = PM[cnt["m"] % 2]; cnt["m"] += 1
                    for kc in range(16):
                        op("pe", lambda g, pm=pm, kc=kc, c0=c0, c1=c1, M=M, wt=wt: g.matmul(pm[0:M, :], lhsT=hT[:, kc, c0:c1], rhs=wt[:, kc * 512:(kc + 1) * 512], start=(kc == 0), stop=(kc == 15)),
                           [hT, wt], [pm], inc=(kc == 15))
                    if ci < 4:
                        vt = tmpB[cnt["t"] % 2]; cnt["t"] += 1
                        op("act", lambda g, pm=pm, vt=vt, M=M: g.activation(out=vt[0:M, 0:512], in_=pm[0:M, :], func=AF.Copy), [pm], [vt])
                        dma("sp", vS, vt, vS[slot, ci, :, c4 * 512:(c4 + 1) * 512], vt[:, 0:512])
                    else:
                        op("act", lambda g, pm=pm, c4=c4: g.activation(out=vss[own_a][0:16, c4 * 512:(c4 + 1) * 512], in_=pm[0:16, :], func=AF.Copy), [pm], [vss[own_a]])
                    if own_a is not None:
                        ot = tmpA[cnt["t"] % 2]; cnt["t"] += 1
                        op("dve", lambda g, pm=pm, ot=ot, M=M: g.tensor_copy(out=ot[0:M, 0:512], in_=pm[0:M, :]), [pm], [ot])
                        if ci < 4:
                            dma("sp", None, ot, v_p[own_a * 512 + c0: own_a * 512 + c1, c4 * 512:(c4 + 1) * 512], ot[0:M, 0:512])
                        else:
                            dma("sp", None, ot, v_s[own_a * 16:(own_a + 1) * 16, c4 * 512:(c4 + 1) * 512], ot[0:M, 0:512])
            for ci, (c0, c1) in enumerate(TC if 'lf' in KVS else []):
                M = c1 - c0
                if ci == 4 and own_a is None:
                    continue
                pm = PM[cnt["m"] % 2]; cnt["m"] += 1
                for kc in range(16):
                    op("pe", lambda g, pm=pm, kc=kc, c0=c0, c1=c1, M=M: g.matmul(pm[0:M, 0:16], lhsT=hT[:, kc, c0:c1], rhs=wfb[:, kc, :], start=(kc == 0), stop=(kc == 15)),
                       [hT, wfb], [pm], inc=(kc == 15))
                ta = tmpA[cnt["t"] % 2]; cnt["t"] += 1
                dst = lfS[0:M, slot * 4 + ci, :] if ci < 4 else lfs[own_a][0:16, :]
                dstt = lfS if ci < 4 else lfs[own_a]
                op("dve", lambda g, pm=pm, ta=ta, M=M: g.tensor_tensor(out=ta[0:M, 0:16], in0=pm[0:M, 0:16], in1=bfb[0:M, :], op=ALU.add), [pm, bfb], [ta])
                op("act", lambda g, ta=ta, M=M: g.activation(out=ta[0:M, 16:32], in_=ta[0:M, 0:16], func=AF.Abs), [ta], [ta])
                op("act", lambda g, ta=ta, M=M: g.activation(out=ta[0:M, 16:32], in_=ta[0:M, 16:32], func=AF.Exp, scale=-1.0), [ta], [ta])
                op("act", lambda g, ta=ta, M=M: g.activation(out=ta[0:M, 16:32], in_=ta[0:M, 16:32], func=AF.Ln, bias=1.0, scale=1.0), [ta], [ta])
                op("dve", lambda g, ta=ta, M=M: g.tensor_scalar(out=ta[0:M, 0:16], in0=ta[0:M, 0:16], scalar1=0.0, scalar2=None, op0=ALU.min), [ta], [ta])
                op("dve", lambda g, ta=ta, M=M, dst=dst: g.tensor_tensor(out=dst, in0=ta[0:M, 0:16], in1=ta[0:M, 16:32], op=ALU.subtract), [ta], [dstt])
                if own_a is not None:
                    if ci < 4:
                        dma("sp", None, lfS, lf_p[own_a * 512 + c0: own_a * 512 + c1, :], dst)
                    else:
                        dma("sp", None, lfs[own_a], lf_s[own_a * 16:(own_a + 1) * 16, :], dst)


        Bk = kb.sb([128, 64, 16], F32, "Bk")
        Wq = kb.sb([128, 8, 16], F32, "Wq")
        Lm = _Zv(Z[:, 16896:18944].bitcast(F32).rearrange("p (b h) -> p b h", h=16))
        Rc = kb.sb([128, 17, 16], F32, "Rc")
        Lc = kb.sb([128, 17, 16], F32, "Lc")
        cqb = kb.sb([128, NT], F32, "cqb")
        rhc = kb.sb([128, 512], F32, "rhc")
        Pt = [kb.sb([128, 512], BF16, f"Pt{i}") for i in range(2)]
        kcs = kb.sb([128, 2048], BF16, "kcs")
        kcT = kb.sb([128, 2048], BF16, "kcT")
        vcs = kb.sb([128, 2048], BF16, "vcs")
        rec = rstd

        def suffix_excl(L, R, nb):
            n = nb * 16
            Lf = L[:, 0:nb, :].rearrange("p b h -> p (b h)"); Rf = R[:, 0:nb, :].rearrange("p b h -> p (b h)")
            for c0 in range(0, n, 512):
                c1 = min(n, c0 + 512)
                op("pe", lambda g, c0=c0, c1=c1: g.matmul(PM[0][:, 0:c1 - c0], lhsT=SL[:], rhs=Lf[:, c0:c1], start=True, stop=True), [SL, L], [PM[0]])
                op("pe", lambda g, c0=c0, c1=c1: g.matmul(PM[1][:, 0:c1 - c0], lhsT=onesf[:], rhs=Lf[:, c0:c1], start=True, stop=True), [onesf, L], [PM[1]])
                op("dve", lambda g, c0=c0, c1=c1: g.tensor_copy(out=Rf[:, c0:c1], in_=PM[0][:, 0:c1 - c0]), [PM[0]], [R])
                op("act", lambda g, c0=c0, c1=c1: g.activation(out=Lf[:, c0:c1], in_=PM[1][:, 0:c1 - c0], func=AF.Copy), [PM[1]], [L])
            op("dve", lambda g: g.memset(rhc[:, 0:16], 0.0), [], [rhc])
            for b in range(nb - 2, -1, -1):
                op("dve", lambda g, b=b: g.tensor_tensor(out=rhc[:, 0:16], in0=rhc[:, 0:16], in1=L[:, b + 1, :], op=ALU.add), [rhc, L], [rhc])
                op("dve", lambda g, b=b: g.tensor_tensor(out=R[:, b, :], in0=R[:, b, :], in1=rhc[:, 0:16], op=ALU.add), [R, rhc], [R])

        def prefix_incl(L, W, nb, rows=128):
            n = nb * 16
            Lf = L[0:rows, 0:nb, :].rearrange("p b h -> p (b h)"); Wf = W[0:rows, 0:nb, :].rearrange("p b h -> p (b h)")
            op("pe", lambda g: g.matmul(PM[0][0:rows, 0:n], lhsT=UT[0:rows, 0:rows], rhs=Lf, start=True, stop=True), [UT, L], [PM[0]])
            op("pe", lambda g: g.matmul(PM[1][0:rows, 0:n], lhsT=onesf[0:rows, 0:rows], rhs=Lf, start=True, stop=True), [onesf, L], [PM[1]])
            op("dve", lambda g: g.tensor_copy(out=Wf, in_=PM[0][0:rows, 0:n]), [PM[0]], [W])
            op("act", lambda g: g.activation(out=Lf, in_=PM[1][0:rows, 0:n], func=AF.Copy), [PM[1]], [L])
            op("dve", lambda g: g.memset(rhc[:, 0:16], 0.0), [], [rhc])
            for b in range(1, nb):
                op("dve", lambda g, b=b: g.tensor_tensor(out=rhc[0:rows, 0:16], in0=rhc[0:rows, 0:16], in1=L[0:rows, b - 1, :], op=ALU.add), [rhc, L], [rhc])
                op("dve", lambda g, b=b: g.tensor_tensor(out=W[0:rows, b, :], in0=W[0:rows, b, :], in1=rhc[0:rows, 0:16], op=ALU.add), [W, rhc], [W])

        def attn_prep(a):
            for b in range(56 if CTX else 0):
                op("dve", lambda g, b=b: g.tensor_scalar(out=Lm[:, b, :], in0=lfS[:, b, :], scalar1=vbt[:, 0, b:b + 1], scalar2=None, op0=ALU.mult), [lfS, vbt], [Lm])
            if CTX:
                suffix_excl(Lm, Bk, 56)
            for b in range(56 if CTX else 0):
                op("dve", lambda g, b=b: g.tensor_scalar(out=Bk[:, b, :], in0=Bk[:, b, :], scalar1=vbt[:, 1, b:b + 1], scalar2=None, op0=ALU.add), [Bk, vbt], [Bk])
            nbo = 4 * (a + 1)
            op("dve", lambda g: g.tensor_copy(out=Lm[:, 0:nbo, :], in_=lfS[:, 56:56 + nbo, :]), [lfS], [Lm])
            prefix_incl(Lm, Wq, nbo)
            op("dve", lambda g: g.tensor_scalar(out=Bk[:, 56:56 + nbo, :], in0=Wq[:, 0:nbo, :], scalar1=-1.0, scalar2=None, op0=ALU.mult), [Wq], [Bk])
            dma("sp", Lc, None, Lc[:, 0:16, :], cl_d[a].rearrange("(b t) h -> t b h", t=128))
            suffix_excl(Lc, Rc, 16)
            op("dve", lambda g: g.tensor_copy(out=Lc[0:16, 16:17, :], in_=lfs[a][0:16, :].rearrange("p (o h) -> p o h", o=1)), [lfs[a]], [Lc])
            prefix_incl(_Sl(Lc, 16), _Sl(Rc, 16), 1, rows=16)

        class _Sl:
            def __init__(self, t, b0):
                self.t = t; self.b0 = b0
            def __getitem__(self, k):
                p, b, h = k
                b = slice(b.start + self.b0, b.stop + self.b0) if isinstance(b, slice) else b + self.b0
                return self.t.ap[p, b, h]
            lw = property(lambda s: s.t.lw, lambda s, v: setattr(s.t, "lw", v))
            rd = property(lambda s: s.t.rd, lambda s, v: setattr(s.t, "rd", v))

        def bcast_rows(dst_ap, dst_t, src_col_fn, nblk, rows, width):
            for blk in range(nblk):
                op("dve", lambda g, blk=blk: g.tensor_scalar(out=rhc[0:rows, blk * width:(blk + 1) * width], in0=identf[0:rows, 0:width], scalar1=src_col_fn(blk), scalar2=None, op0=ALU.mult), [identf, Wq, Rc], [rhc])
            n = nblk * width
            op("pe", lambda g: g.matmul(PM[0][:, 0:n], lhsT=onesf[0:rows, :], rhs=rhc[0:rows, 0:n], start=True, stop=True), [onesf, rhc], [PM[0]])
            op("act", lambda g: g.activation(out=dst_ap, in_=PM[0][:, 0:n], func=AF.Copy), [PM[0]], [dst_t])

        def fox(j, a):
            l = 2 + j
            rmsnorm(P_MX + l * 16)
            wql = wq_d[j].rearrange("(kc p) f -> p kc f", p=128)
            for c4 in range(4):
                wt = WGU[cnt["w"] % 2]; cnt["w"] += 1
                dma("pool", wt, None, wt[:, 0:8192].rearrange("p (kc f) -> p kc f", kc=16), wql[:, :, c4 * 512:(c4 + 1) * 512])
                for jj in range(4):
                    h = c4 * 4 + jj
                    proj_chunk(wt, lambda kc, jj=jj, wt=wt: wt[:, kc * 512 + jj * 128: kc * 512 + (jj + 1) * 128])
                    for ti, (ca, cb) in enumerate(TG):
                        op("act", lambda g, h=h, ti=ti, ca=ca, cb=cb: g.activation(out=qT[:, h, ca:cb], in_=PG[ti][:, 0:264], func=AF.Copy), [PG[ti]], [Z])
            KH, VH = WGU[0], WGU[1]
            nslot = 15 + a
            for h in range(16):
                slo = 0 if CTX else 14
                dma("sp", KH, kS, KH[:, slo * 512:nslot * 512].rearrange("p (s n) -> p s n", s=nslot - slo), kS[slo:nslot, h].rearrange("s d n -> d s n"))
                for s_ in range(slo, nslot):
                    dma("sp", VH, vS, VH[:, s_ * 512:(s_ + 1) * 512].rearrange("p (b d) -> p b d", b=4), vS[s_, :, :, h * 128:(h + 1) * 128].rearrange("b t d -> t b d"))
                bcast_rows(cqb[:, 0:512], cqb, lambda blk, h=h: Wq[:, 4 * a + blk, h:h + 1], 4, 128, 128)
                blocks = ([(s_, b_) for s_ in range(14) for b_ in range(4)] if CTX else []) + [(14 + a2, b_) for a2 in range(a + 1) for b_ in range(4)]
                for bi, (s_, b_) in enumerate(blocks):
                    diag = (s_ == 14 + a)
                    q0 = b_ * 128 if diag else 0
                    nq = 512 - q0
                    ps = PG[bi % 2]
                    pt = Pt[bi % 2]
                    ta = tmpA[bi % 2]
                    gb = s_ * 4 + b_
                    op("pe", lambda g, ps=ps, s_=s_, b_=b_, q0=q0, nq=nq, h=h: g.matmul(ps[:, 0:nq], lhsT=KH[:, s_ * 512 + b_ * 128: s_ * 512 + (b_ + 1) * 128], rhs=qT[:, h, q0:512], start=True, stop=True), [KH, Z], [ps])
                    op("dve", lambda g, ps=ps, ta=ta, q0=q0, nq=nq: g.scalar_tensor_tensor(out=ta[:, 0:nq], in0=ps[:, 0:nq], scalar=SCALE, in1=cqb[:, q0:512], op0=ALU.mult, op1=ALU.add), [ps, cqb], [ta])
                    if diag:
                        op("dve", lambda g, ta=ta: g.tensor_tensor(out=ta[:, 0:128], in0=ta[:, 0:128], in1=TRIN[:], op=ALU.add), [ta, TRIN], [ta])
                    op("act", lambda g, ta=ta, pt=pt, nq=nq, gb=gb, h=h: g.activation(out=pt[:, 0:nq], in_=ta[:, 0:nq], func=AF.Exp, bias=Bk[:, gb, h:h + 1], scale=1.0), [ta, Bk], [pt])
                    first = (bi == 0); last = (bi == len(blocks) - 1)
                    op("pe", lambda g, pt=pt, s_=s_, b_=b_, q0=q0, nq=nq, first=first, last=last: g.matmul(PU[0][:, q0:512], lhsT=VH[:, s_ * 512 + b_ * 128: s_ * 512 + (b_ + 1) * 128], rhs=pt[:, 0:nq], start=first, stop=last), [VH, pt], [PU[0]], inc=False)
                    op("pe", lambda g, pt=pt, q0=q0, nq=nq, first=first, last=last: g.matmul(PU[1][:, q0:512], lhsT=onesb[:], rhs=pt[:, 0:nq], start=first, stop=last), [onesb, pt], [PU[1]])
                op("dve", lambda g: g.reciprocal(out=rec[:, 0:512], in_=PU[1][:]), [PU[1]], [rec])
                op("dve", lambda g, h=h: g.tensor_tensor(out=oT[:, h, 0:512], in0=PU[0][:], in1=rec[:, 0:512], op=ALU.mult), [PU[0], rec], [Z])
                dma("pool", kcs, None, kcs[:].rearrange("p (b d) -> p b d", b=16), ck_d[a, :, h * 128:(h + 1) * 128].rearrange("(b t) d -> t b d", t=128))
                dma("pool", vcs, None, vcs[:].rearrange("p (b d) -> p b d", b=16), cv_d[a, :, h * 128:(h + 1) * 128].rearrange("(b t) d -> t b d", t=128))
                for q4 in range(4):
                    pmb = PMb[q4 % 2]; pmf = PM[q4 % 2]
                    for jj in range(4):
                        b = q4 * 4 + jj
                        op("pe", lambda g, pmb=pmb, jj=jj, b=b: g.transpose(out=pmb[:, jj * 128:(jj + 1) * 128], in_=kcs[:, b * 128:(b + 1) * 128], identity=identb[:]), [kcs, identb], [pmf], inc=(jj == 3))
                    op("dve", lambda g, pmb=pmb, q4=q4: g.tensor_copy(out=kcT[:, q4 * 512:(q4 + 1) * 512], in_=pmb[:, 0:512]), [pmf], [kcT])
                bcast_rows(cqb[:, 512:528], cqb, lambda blk, h=h: Rc[0:16, 16, h:h + 1], 1, 16, 16)
                for b in range(16):
                    op("pe", lambda g, b=b, h=h: g.matmul(PD[0][:, b * 16:(b + 1) * 16], lhsT=kcT[:, b * 128:(b + 1) * 128], rhs=qT[:, h, 512:528], start=True, stop=True), [kcT, Z], [PD[0]], inc=False)
                op("pe", lambda g, h=h: g.matmul(PD[0][0:16, 256:272], lhsT=kTs[a][:, h, :], rhs=qT[:, h, 512:528], start=True, stop=True), [kTs[a], Z], [PD[0]])
                ta = tmpA[0]; pt = Pt[0]
                for b in range(17):
                    rows = 128 if b < 16 else 16
                    op("dve", lambda g, b=b, rows=rows: g.scalar_tensor_tensor(out=ta[0:rows, b * 16:(b + 1) * 16], in0=PD[0][0:rows, b * 16:(b + 1) * 16], scalar=SCALE, in1=cqb[0:rows, 512:528], op0=ALU.mult, op1=ALU.add), [PD[0], cqb], [ta])
                op("dve", lambda g: g.tensor_tensor(out=ta[0:16, 256:272], in0=ta[0:16, 256:272], in1=TRIN[0:16, 0:16], op=ALU.add), [ta, TRIN], [ta])
                op("dve", lambda g, h=h: g.tensor_scalar(out=rhc[0:16, 16:17], in0=Rc[0:16, 16, h:h + 1], scalar1=-1.0, scalar2=None, op0=ALU.mult), [Rc], [rhc])
                for b in range(17):
                    rows = 128 if b < 16 else 16
                    bias = Rc[:, b, h:h + 1] if b < 16 else rhc[0:16, 16:17]
                    op("act", lambda g, b=b, rows=rows, bias=bias: g.activation(out=pt[0:rows, b * 16:(b + 1) * 16], in_=ta[0:rows, b * 16:(b + 1) * 16], func=AF.Exp, bias=bias, scale=1.0), [ta, Rc, rhc], [pt])
                for b in range(17):
                    rows = 128 if b < 16 else 16
                    lv = vcs[:, b * 128:(b + 1) * 128] if b < 16 else vss[a][0:16, h * 128:(h + 1) * 128]
                    op("pe", lambda g, b=b, rows=rows, lv=lv: g.matmul(PD[1][:, 0:16], lhsT=lv, rhs=pt[0:rows, b * 16:(b + 1) * 16], start=(b == 0), stop=(b == 16)), [vcs, vss[a], pt], [PD[1]], inc=False)
                    op("pe", lambda g, b=b, rows=rows: g.matmul(PM[1][:, 16:32], lhsT=onesb[0:rows, :], rhs=pt[0:rows, b * 16:(b + 1) * 16], start=(b == 0), stop=(b == 16)), [onesb, pt], [PM[1]], inc=(b == 16))
                op("dve", lambda g: g.reciprocal(out=rec[:, 0:16], in_=PM[1][:, 16:32]), [PM[1]], [rec])
                op("dve", lambda g, h=h: g.tensor_tensor(out=oT[:, h, 512:528], in0=PD[1][:, 0:16], in1=rec[:, 0:16], op=ALU.mult), [PD[1], rec], [Z])
            wol = wo_d[j]
            for hg in range(8):
                wt = WDn[cnt["wd"] % 2]; cnt["wd"] += 1
                dma("pool", wt, None, wt[:, 0:4096].rearrange("p (jj d) -> p jj d", jj=2), wol[hg * 256:(hg + 1) * 256, :].rearrange("(jj p) d -> p jj d", p=128))
                acc_down(wt, lambda jj, dc, wt=wt: wt[:, jj * 2048 + dc * 128: jj * 2048 + (dc + 1) * 128], Z, 2, lambda jj, ca, cb, hg=hg: oT[:, hg * 2 + jj, ca:cb], 1.0)

        def final_out(a):
            rstd_compute()
            for ci, (c0, c1) in enumerate(TC):
                M = c1 - c0
                for q4 in range(4):
                    pm = PM[cnt["m"] % 2]; cnt["m"] += 1
                    for jj in range(4):
                        kc = q4 * 4 + jj
                        yt = tmpA[jj % 2]
                        op("dve", lambda g, kc=kc, yt=yt, c0=c0, c1=c1, M=M: g.scalar_tensor_tensor(out=yt[:, 0:M], in0=xT[:, kc, c0:c1], scalar=pcol(P_FN, kc), in1=rstd[:, c0:c1], op0=ALU.mult, op1=ALU.mult), [xT, prm, rstd], [yt])
                        op("pe", lambda g, pm=pm, jj=jj, yt=yt, M=M: g.transpose(out=pm[0:M, jj * 128:(jj + 1) * 128], in_=yt[:, 0:M], identity=identf[:]), [yt, identf], [pm])
                    op("act", lambda g, pm=pm, q4=q4, M=M: g.activation(out=stg[0:M, q4 * 512:(q4 + 1) * 512], in_=pm[0:M, :], func=AF.Copy), [pm], [Z])
                if ci < 4:
                    dma("sp", None, Z, y_p[a * 512 + c0: a * 512 + c1, :], stg[0:M, :])
                else:
                    dma("sp", None, Z, y_s[a * 16:(a + 1) * 16, :], stg[0:M, :])

        for tile in TILES:
            own_a = tile - 14 if tile >= 14 else None
            load_x(tile)
            for l in range(NLA):
                if "ffn1" in STG:
                    ffn(l, 0, P_F1)
                if "gmlp" in STG:
                    gmlp(l, own_a)
                if "ffn2" in STG:
                    ffn(l, 1, P_F2)
            if "kv" in STG:
                kv_phase(tile, own_a)
            if own_a is not None and BL:
                attn_prep(own_a)
                for j in range(2):
                    ffn(2 + j, 0, P_F1)
                    fox(j, own_a)
                    ffn(2 + j, 1, P_F2)
                final_out(own_a)
        kb.emit()
    return nc


TILES = list(range(16))
CTX = True
BL = True
NLA = 2
WL = 4
KVS = {'kt', 'ktok', 'v', 'lf'}
STG = {'ffn1', 'gmlp', 'ffn2', 'kv'}
_NC = None


def _prep_inputs(inp, c):
    f = np.float32
    xp = np.asarray(inp["x_prompt"], f)[0]
    xs = np.asarray(inp["x_sample"], f)
    own = [2 * c, 2 * c + 1]
    others = [t for t in range(16) if t not in own]
    tiles = others + own
    xin = np.empty((16, NT, D), f)
    for i, t in enumerate(tiles):
        xin[i, :512] = xp[t * 512:(t + 1) * 512]
        xin[i, 512:] = xs[2 * c + (i - 14 if i >= 14 else 0)]
    def pl(arr):
        arr = np.asarray(arr, f)
        L = arr.shape[0]
        return arr.reshape(L, -1, 128).transpose(2, 0, 1).reshape(128, -1)
    prm = np.concatenate([pl(inp["ffn1_norm"]), pl(inp["mix_norm"]), pl(inp["ffn2_norm"]), pl(np.asarray(inp["kv_norm"])[None]),
                          pl(np.asarray(inp["final_norm"])[None]), pl(inp["gmlp_ln_g"]), pl(inp["gmlp_ln_b"])], axis=1)
    assert prm.shape == (128, NPRM)
    vb = np.zeros((2, 56), f)
    for i, t in enumerate(others):
        ok = t < 2 * c
        vb[0, i * 4:(i + 1) * 4] = 1.0 if ok else 0.0
        vb[1, i * 4:(i + 1) * 4] = 0.0 if ok else NEG
    m = {"xin": xin.reshape(16 * NT, D), "prm": np.ascontiguousarray(prm), "vb": vb,
         "wsT": np.ascontiguousarray(np.asarray(inp["gmlp_w_s"], f).transpose(0, 1, 3, 2)),
         "cache_k": np.asarray(inp["cache_k"], f)[2 * c:2 * c + 2].reshape(2, 2048, D),
         "cache_v": np.asarray(inp["cache_v"], f)[2 * c:2 * c + 2].reshape(2, 2048, D),
         "cache_logf": np.asarray(inp["cache_logf"], f)[2 * c:2 * c + 2]}
    for k in ("ffn1_w_gate", "ffn2_w_gate", "ffn1_w_up", "ffn2_w_up", "ffn1_w_down", "ffn2_w_down", "gmlp_w_in", "gmlp_w_out",
              "gmlp_b_s", "gmlp_ln_g", "gmlp_ln_b", "w_k", "w_v", "w_f", "b_f", "fox_w_q", "fox_w_o"):
        m[k] = np.asarray(inp[k], f)[:WL] if k.startswith('ffn') else np.asarray(inp[k], f)
    return m


def kernel(**inp):
    global _NC
    if _NC is None:
        _NC = build_program()
    cores = list(range(8))
    in_maps = [_prep_inputs(inp, c) for c in cores]
    res = run_bass_kernel_spmd(_NC, in_maps, core_ids=cores)
    R = res.results
    cat = lambda k: np.concatenate([R[c][k] for c in cores], axis=0)
    y_prompt = cat("y_p").reshape(1, 8192, D)
    y_sample = cat("y_s").reshape(16, 16, D)
    k_prompt = cat("k_p").reshape(1, 8192, 16, 128)
    v_prompt = cat("v_p").reshape(1, 8192, 16, 128)
    logf_prompt = cat("lf_p").reshape(1, 8192, 16)
    k_sample = cat("k_s").reshape(16, 16, 16, 128)
    v_sample = cat("v_s").reshape(16, 16, 16, 128)
    logf_sample = cat("lf_s").reshape(16, 16, 16)
    gvs = np.concatenate([R[c]["gv"] for c in cores], axis=1).reshape(2, 16, 16, 4096)
    return (y_prompt, y_sample, k_prompt, v_prompt, logf_prompt, k_sample, v_sample, logf_sample, gvs)
```

```python
import contextlib
import numpy as np
import concourse.bass as bass
import concourse.mybir as mybir
from concourse.bass_utils import run_bass_kernel_spmd

F32 = mybir.dt.float32
BF16 = mybir.dt.bfloat16
AF = mybir.ActivationFunctionType
ALU = mybir.AluOpType
AX = mybir.AxisListType

D = 2048
DFF = 5632
NT = 528
NEG = -1.0e30
TG = [(0, 264), (264, 528)]
TC = [(0, 128), (128, 256), (256, 384), (384, 512), (512, 528)]
SCALE = 128 ** -0.5
P_F1, P_MX, P_F2, P_KV, P_FN, P_LG, P_LB, NPRM = 0, 64, 128, 192, 208, 224, 288, 352


class Tl:
    __slots__ = ("ap", "lw", "rd", "name", "ps")

    def __init__(self, ap, name="", ps=False):
        self.ap = ap
        self.lw = None
        self.rd = {}
        self.name = name
        self.ps = ps

    def __getitem__(self, k):
        return self.ap[k]


class Eng:
    def __init__(self, name):
        self.name = name
        self.count = 0
        self.ops = []
        self.waited = {}
        self.pool = []
        self.pool_cum = []
        self.pool_i = 0


class KB:
    NPOOL = 16

    def __init__(self, nc, stack):
        self.nc = nc
        self.stack = stack
        self.E = {n: Eng(n) for n in ("pe", "act", "dve", "pool", "sp")}
        self.sems = {}
        for n in self.E:
            self.sems[n] = stack.enter_context(nc.semaphore("s_" + n))
        for q in ("sp", "pool"):
            e = self.E[q]
            for i in range(self.NPOOL):
                key = f"d_{q}{i}"
                self.sems[key] = stack.enter_context(nc.semaphore(key))
                e.pool.append(key)
                e.pool_cum.append(0)
        self.ntile = 0

    def sb(self, shape, dt, name=None):
        self.ntile += 1
        name = name or f"t{self.ntile}"
        h = self.stack.enter_context(self.nc.sbuf_tensor(name, list(shape), dt))
        return Tl(h, name)

    def ps(self, shape, dt, name=None):
        self.ntile += 1
        name = name or f"p{self.ntile}"
        h = self.stack.enter_context(self.nc.psum_tensor(name, list(shape), dt))
        return Tl(h, name, ps=True)

    def dram(self, name, shape, dt):
        h = self.nc.dram_tensor(name, list(shape), dt).ap()
        return Tl(h, name)

    def _need(self, e, waits, tok, same_ok):
        if tok is None:
            return
        k, v = tok
        if same_ok and k == e.name:
            return
        if e.waited.get(k, 0) >= v:
            return
        if waits.get(k, 0) < v:
            waits[k] = v

    def _deps(self, e, reads, writes):
        waits = {}
        for t in reads:
            self._need(e, waits, t.lw, e.name == "pe")
            if getattr(t, "ps", False):
                for k, v in t.rd.items():
                    self._need(e, waits, (k, v), True)
        for t in writes:
            self._need(e, waits, t.lw, True)
            for k, v in t.rd.items():
                self._need(e, waits, (k, v), True)
        for k, v in waits.items():
            e.waited[k] = v
        return waits

    def op(self, eng, fn, reads=(), writes=(), inc=True):
        e = self.E[eng]
        waits = self._deps(e, reads, writes)
        if inc:
            e.count += 1
            tok = (eng, e.count)
        else:
            tok = (eng, e.count + 1)
        for t in reads:
            if t.rd.get(tok[0], 0) < tok[1]:
                t.rd[tok[0]] = tok[1]
        for t in writes:
            t.lw = tok
            t.rd = {}
        e.ops.append((waits, fn, eng if inc else None, 1))
        return tok

    def dma(self, q, out_t, in_t, out_ap, in_ap):
        e = self.E[q]
        reads = [in_t] if in_t is not None else []
        writes = [out_t] if out_t is not None else []
        waits = self._deps(e, reads, writes)
        i = e.pool_i
        e.pool_i = (i + 1) % len(e.pool)
        key = e.pool[i]
        prev = e.pool_cum[i]
        if prev > 0 and e.waited.get(key, 0) < prev:
            waits[key] = max(waits.get(key, 0), prev)
            e.waited[key] = prev
        e.pool_cum[i] = prev + 16
        tok = (key, prev + 16)
        for t in reads:
            t.rd[key] = tok[1]
        for t in writes:
            t.lw = tok
            t.rd = {}
        e.ops.append((waits, lambda g: g.dma_start(out=out_ap, in_=in_ap), key, 16))
        return tok

    def emit(self):
        nc = self.nc
        fin = {}
        for q in ("sp", "pool"):
            e = self.E[q]
            for key, cum in zip(e.pool, e.pool_cum):
                if cum > 0:
                    fin[key] = cum
        for n in ("pe", "act", "dve", "pool"):
            if self.E[n].count > 0:
                fin[n] = self.E[n].count
        sems = self.sems
        E = self.E

        def run(e, g):
            for waits, fn, inck, incv in e.ops:
                for k, v in waits.items():
                    g.wait_ge(sems[k], v)
                ins = fn(g)
                if inck is not None:
                    ins.then_inc(sems[inck], incv)

        with nc.Block() as block:
            @block.tensor
            def _(g):
                run(E["pe"], g)

            @block.scalar
            def _(g):
                run(E["act"], g)

            @block.vector
            def _(g):
                run(E["dve"], g)

            @block.gpsimd
            def _(g):
                run(E["pool"], g)

            @block.sync
            def _(g):
                run(E["sp"], g)
                for k, v in fin.items():
                    g.wait_ge(sems[k], v)


def build_program():
    nc = bass.Bass("TRN2", target_bir_lowering=False)

    def din(name, shape, dt=F32):
        return nc.dram_tensor(name, list(shape), dt, kind="ExternalInput").ap()

    def dout(name, shape):
        return nc.dram_tensor(name, list(shape), F32, kind="ExternalOutput").ap()

    xin = din("xin", [16 * NT, D])
    prm_d = din("prm", [128, NPRM])
    vb_d = din("vb", [2, 56])
    wg = [din("ffn1_w_gate", [WL, D, DFF]), din("ffn2_w_gate", [WL, D, DFF])]
    wu = [din("ffn1_w_up", [WL, D, DFF]), din("ffn2_w_up", [WL, D, DFF])]
    wd = [din("ffn1_w_down", [WL, DFF, D]), din("ffn2_w_down", [WL, DFF, D])]
    w_in = din("gmlp_w_in", [2, D, 8192])
    w_out = din("gmlp_w_out", [2, 4096, D])
    wsT_d = din("wsT", [2, 4, 128, 128])
    bs_d = din("gmlp_b_s", [2, 4, 128])
    lng_d = din("gmlp_ln_g", [2, 4096])
    lnb_d = din("gmlp_ln_b", [2, 4096])
    wk_d = din("w_k", [D, D])
    wv_d = din("w_v", [D, D])
    wf_d = din("w_f", [D, 16])
    bf_d = din("b_f", [16])
    wq_d = din("fox_w_q", [2, D, D])
    wo_d = din("fox_w_o", [2, D, D])
    ck_d = din("cache_k", [2, 2048, D])
    cv_d = din("cache_v", [2, 2048, D])
    cl_d = din("cache_logf", [2, 2048, 16])

    y_p = dout("y_p", [1024, D]); y_s = dout("y_s", [32, D])
    k_p = dout("k_p", [1024, D]); v_p = dout("v_p", [1024, D]); lf_p = dout("lf_p", [1024, 16])
    k_s = dout("k_s", [32, D]); v_s = dout("v_s", [32, D]); lf_s = dout("lf_s", [32, 16])
    gv = dout("gv", [2, 32, 4096])

    with contextlib.ExitStack() as st:
        kb = KB(nc, st)
        op, dma = kb.op, kb.dma
        xT = kb.sb([128, 16, NT], F32, "xT")
        xTt = [[Tl(xT.ap, f"xT{dc}_{ti}") for ti in range(2)] for dc in range(16)]
        def xd(dc):
            return [xTt[dc][0], xTt[dc][1]]
        hT = kb.sb([128, 16, NT], BF16, "hT")
        actb = [kb.sb([128, 2, NT], BF16, f"act{i}") for i in range(2)]
        WGU = [kb.sb([128, 8192], BF16, f"WGU{i}") for i in range(2)]
        WDn = [kb.sb([128, 4096], BF16, f"WDn{i}") for i in range(2)]
        prm = kb.sb([128, NPRM], F32, "prm_sb")
        sqt = [kb.sb([128, NT], BF16, f"sqt{i}") for i in range(2)]
        rstd = kb.sb([128, NT], F32, "rstd")
        tmpA = [kb.sb([128, NT], F32, f"tmpA{i}") for i in range(2)]
        tmpB = [kb.sb([128, NT], BF16, f"tmpB{i}") for i in range(2)]
        identf = kb.sb([128, 128], F32, "identf")
        identb = kb.sb([128, 128], BF16, "identb")
        onesb = kb.sb([128, 128], BF16, "onesb")
        onesf = kb.sb([128, 128], F32, "onesf")
        UT = kb.sb([128, 128], F32, "UT")
        SL = kb.sb([128, 128], F32, "SL")
        TRIN = kb.sb([128, 128], F32, "TRIN")
        Z = kb.sb([128, 21504], BF16, "Z")
        class _V:
            def __init__(self, ap): self.ap = ap
            def __getitem__(self, k): return self.ap[k]
        class _Zv:
            def __init__(self, ap): self.ap = ap
            def __getitem__(self, k): return self.ap[k]
            lw = property(lambda s_: Z.lw, lambda s_, v: setattr(Z, "lw", v))
            rd = property(lambda s_: Z.rd, lambda s_, v: setattr(Z, "rd", v))
        zt = [_V(Z[:, i * 4096:(i + 1) * 4096]) for i in range(5)]
        stg = _V(Z[:, 16896:20992].bitcast(F32))
        qT = _V(Z[:, 0:8448].rearrange("p (h n) -> p h n", h=16))
        oT = _V(Z[:, 8448:16896].rearrange("p (h n) -> p h n", h=16))
        st1 = kb.sb([128, 5, 8], F32, "st1")
        st2 = kb.sb([128, 5, 8], F32, "st2")
        stv = kb.sb([128, 5, 8], F32, "stv")
        wsT = [kb.sb([128, 128], BF16, f"wsT{g}") for g in range(4)]
        wsf = kb.sb([128, 128], F32, "wsf")
        rsb = [kb.sb([128, 128], F32, f"rsb{g}") for g in range(4)]
        bsb = [kb.sb([128, 128], F32, f"bsb{g}") for g in range(4)]
        rsx = kb.sb([128, NT], F32, "rsx")
        bsx = kb.sb([128, NT], F32, "bsx")
        lfS = kb.sb([128, 64, 16], F32, "lfS")
        lfs = [kb.sb([128, 16], F32, f"lfs{a}") for a in range(2)]
        wfb = kb.sb([128, 16, 16], BF16, "wfb")
        bfb = kb.sb([128, 16], F32, "bfb")
        kTs = [kb.sb([128, 16, 16], BF16, f"kTs{a}") for a in range(2)]
        vss = [kb.sb([128, 2048], BF16, f"vss{a}") for a in range(2)]
        vbt = kb.sb([128, 2, 56], F32, "vbt")
        PG = [kb.ps([128, 512], F32, f"PG{i}") for i in range(2)]
        PU = [kb.ps([128, 512], F32, f"PU{i}") for i in range(2)]
        PD = [kb.ps([128, 512], F32, f"PD{i}") for i in range(2)]
        PM = [kb.ps([128, 512], F32, f"PM{i}") for i in range(2)]
        PMb = [Tl(PM[i].ap.bitcast(BF16), f"PMb{i}") for i in range(2)]
        kS = kb.dram("kS", [16, 16, 128, 512], BF16)
        vS = kb.dram("vS", [16, 4, 128, 2048], BF16)

        op("pool", lambda g: g.memset(identf[:], 1.0), [], [identf])
        op("pool", lambda g: g.affine_select(out=identf[:], in_=identf[:], pattern=[[-1, 128]], compare_op=ALU.is_equal, fill=0.0, base=0, channel_multiplier=1), [identf], [identf])
        op("pool", lambda g: g.memset(UT[:], 1.0), [], [UT])
        op("pool", lambda g: g.affine_select(out=UT[:], in_=UT[:], pattern=[[1, 128]], compare_op=ALU.is_ge, fill=0.0, base=0, channel_multiplier=-1), [UT], [UT])
        op("pool", lambda g: g.memset(SL[:], 1.0), [], [SL])
        op("pool", lambda g: g.affine_select(out=SL[:], in_=SL[:], pattern=[[-1, 128]], compare_op=ALU.is_gt, fill=0.0, base=0, channel_multiplier=1), [SL], [SL])
        op("pool", lambda g: g.memset(TRIN[:], 0.0), [], [TRIN])
        op("pool", lambda g: g.affine_select(out=TRIN[:], in_=TRIN[:], pattern=[[1, 128]], compare_op=ALU.is_ge, fill=NEG, base=0, channel_multiplier=-1), [TRIN], [TRIN])
        op("pool", lambda g: g.memset(onesb[:], 1.0), [], [onesb])
        op("pool", lambda g: g.memset(onesf[:], 1.0), [], [onesf])
        op("dve", lambda g: g.tensor_copy(out=identb[:], in_=identf[:]), [identf], [identb])
        dma("sp", prm, None, prm[:], prm_d[:, :])
        dma("sp", bfb, None, bfb[:], bf_d.partition_broadcast(128))
        dma("sp", vbt, None, vbt[:], vb_d.partition_broadcast(128))
        dma("pool", wfb, None, wfb[:], wf_d.rearrange("(kc p) f -> p kc f", p=128))

        cnt = {"w": 0, "wd": 0, "m": 0, "d": 0, "a": 0, "t": 0}

        def pcol(base, i):
            return prm[:, base + i:base + i + 1]

        def load_x(tile):
            for ci, (c0, c1) in enumerate(TC):
                M = c1 - c0
                r0 = tile * NT + c0
                dma("sp", Z, None, stg[0:M, :], xin[r0:r0 + M, :])
                for q4 in range(4):
                    pm = PM[cnt["m"] % 2]; cnt["m"] += 1
                    for j in range(4):
                        kc = q4 * 4 + j
                        op("pe", lambda g, pm=pm, j=j, kc=kc, M=M: g.transpose(out=pm[:, j * 128:j * 128 + M], in_=stg[0:M, kc * 128:(kc + 1) * 128], identity=identf[0:M, 0:M]),
                           [Z, identf], [pm], inc=(j == 3))
                    for j in range(4):
                        kc = q4 * 4 + j
                        eng = "act" if j % 2 else "dve"
                        if eng == "act":
                            op("act", lambda g, pm=pm, j=j, kc=kc, M=M, c0=c0, c1=c1: g.activation(out=xT[:, kc, c0:c1], in_=pm[:, j * 128:j * 128 + M], func=AF.Copy), [pm], xd(kc))
                        else:
                            op("dve", lambda g, pm=pm, j=j, kc=kc, M=M, c0=c0, c1=c1: g.tensor_copy(out=xT[:, kc, c0:c1], in_=pm[:, j * 128:j * 128 + M]), [pm], xd(kc))

        def rstd_compute():
            for kc in range(16):
                s = sqt[kc % 2]
                op("act", lambda g, s=s, kc=kc: g.activation(out=s[:], in_=xT[:, kc, :], func=AF.Square), xd(kc), [s])
                for ti, (a, b) in enumerate(TG):
                    op("pe", lambda g, s=s, ti=ti, a=a, b=b, kc=kc: g.matmul(PM[ti][:, 0:264], lhsT=onesb[:], rhs=s[:, a:b], start=(kc == 0), stop=(kc == 15)),
                       [s, onesb], [PM[ti]], inc=(kc == 15 or ti == 1))
            for ti, (a, b) in enumerate(TG):
                op("act", lambda g, ti=ti, a=a, b=b: g.activation(out=rstd[:, a:b], in_=PM[ti][:, 0:264], func=AF.Sqrt, bias=1e-6, scale=1.0 / D), [PM[ti]], [rstd])
            op("dve", lambda g: g.reciprocal(out=rstd[:], in_=rstd[:]), [rstd], [rstd])

        def rmsnorm(base):
            rstd_compute()
            for kc in range(16):
                op("dve", lambda g, kc=kc: g.scalar_tensor_tensor(out=hT[:, kc, :], in0=xT[:, kc, :], scalar=pcol(base, kc), in1=rstd[:], op0=ALU.mult, op1=ALU.mult),
                   xd(kc) + [prm, rstd], [hT])

        def proj_chunk(wt, wap_fn, src=None):
            src = src or hT
            for kc in range(16):
                for ti, (a, b) in enumerate(TG):
                    op("pe", lambda g, kc=kc, ti=ti, a=a, b=b: g.matmul(PG[ti][:, 0:264], lhsT=wap_fn(kc), rhs=src[:, kc, a:b], start=(kc == 0), stop=(kc == 15)),
                       [wt, src], [PG[ti]], inc=(kc == 15))

        PDP = [PD[0], PD[1], PM[0], PM[1]]

        def down_units(wt, wap_fn, srct, nk, rhs_fn, scale):
            units = []
            for dc in range(16):
                for ti, (a, b) in enumerate(TG):
                    def emit(dc=dc, ti=ti, a=a, b=b):
                        pd = PDP[cnt["d"] % 4]; cnt["d"] += 1
                        for j in range(nk):
                            op("pe", lambda g, pd=pd, j=j: g.matmul(pd[:, 0:264], lhsT=wap_fn(j, dc), rhs=rhs_fn(j, a, b), start=(j == 0), stop=(j == nk - 1)),
                               [wt, srct], [pd], inc=(j == nk - 1))
                        op("dve", lambda g, pd=pd: g.scalar_tensor_tensor(out=xT[:, dc, a:b], in0=pd[:, 0:264], scalar=scale, in1=xT[:, dc, a:b], op0=ALU.mult, op1=ALU.add),
                           [pd, xTt[dc][ti]], [xTt[dc][ti]])
                    units.append(emit)
            return units

        def acc_down(wt, wap_fn, srct, nk, rhs_fn, scale):
            for u in down_units(wt, wap_fn, srct, nk, rhs_fn, scale):
                u()

        def drain(pending, n):
            for _ in range(min(n, len(pending))):
                pending.pop(0)()

        def ffn(l, which, base):
            rmsnorm(base + l * 16)
            wgl = wg[which][l].rearrange("(kc p) f -> p kc f", p=128)
            wul = wu[which][l].rearrange("(kc p) f -> p kc f", p=128)
            wdl = wd[which][l]
            NG = 22

            def load_gu(grp):
                wt = WGU[cnt["w"] % 2]; cnt["w"] += 1
                dma("pool", wt, None, wt[:, 0:4096].rearrange("p (kc f) -> p kc f", kc=16), wgl[:, :, grp * 256:(grp + 1) * 256])
                dma("pool", wt, None, wt[:, 4096:8192].rearrange("p (kc f) -> p kc f", kc=16), wul[:, :, grp * 256:(grp + 1) * 256])
                return wt

            def load_d(grp):
                wt = WDn[cnt["wd"] % 2]; cnt["wd"] += 1
                dma("pool", wt, None, wt[:, 0:4096].rearrange("p (j d) -> p j d", j=2), wdl[grp * 256:(grp + 1) * 256, :].rearrange("(j p) d -> p j d", p=128))
                return wt

            def gate_up(wt, ab, pending):
                for j in range(2):
                    proj_chunk(wt, lambda kc, j=j, wt=wt: wt[:, kc * 256 + j * 128: kc * 256 + (j + 1) * 128])
                    for ti, (a, b) in enumerate(TG):
                        op("act", lambda g, ti=ti: g.activation(out=tmpA[ti][:, 0:264], in_=PG[ti][:, 0:264], func=AF.Silu), [PG[ti]], [tmpA[ti]])
                    drain(pending, 8)
                    for kc in range(16):
                        for ti, (a, b) in enumerate(TG):
                            op("pe", lambda g, kc=kc, ti=ti, a=a, b=b, j=j, wt=wt: g.matmul(PU[ti][:, 0:264], lhsT=wt[:, 4096 + kc * 256 + j * 128: 4096 + kc * 256 + (j + 1) * 128], rhs=hT[:, kc, a:b], start=(kc == 0), stop=(kc == 15)),
                               [wt, hT], [PU[ti]], inc=(kc == 15))
                    for ti, (a, b) in enumerate(TG):
                        op("dve", lambda g, ti=ti, a=a, b=b, j=j, ab=ab: g.tensor_tensor(out=ab[:, j, a:b], in0=tmpA[ti][:, 0:264], in1=PU[ti][:, 0:264], op=ALU.mult),
                           [tmpA[ti], PU[ti]], [ab])
                    drain(pending, 8)

            def down(wt, ab):
                return down_units(wt, lambda j, dc, wt=wt: wt[:, j * 2048 + dc * 128: j * 2048 + (dc + 1) * 128], ab, 2, lambda j, a, b, ab=ab: ab[:, j, a:b], 0.5)

            gu = {0: load_gu(0)}
            dd = {0: load_d(0)}
            pending = []
            for grp in range(NG):
                if grp + 1 < NG:
                    gu[grp + 1] = load_gu(grp + 1)
                ab = actb[cnt["a"] % 2]; cnt["a"] += 1
                gate_up(gu[grp], ab, pending)
                drain(pending, 1000)
                if grp + 1 < NG:
                    dd[grp + 1] = load_d(grp + 1)
                pending = down(dd[grp], ab)
            drain(pending, 1000)

        def gmlp(l, own_a):
            rmsnorm(P_MX + l * 16)
            for g4 in range(4):
                dma("sp", wsf, None, wsf[:], wsT_d[l, g4])
                op("dve", lambda g, g4=g4: g.tensor_tensor(out=wsT[g4][:], in0=wsf[:], in1=UT[:], op=ALU.mult), [wsf, UT], [wsT[g4]])
                pm = PM[cnt["m"] % 2]; cnt["m"] += 1
                op("pe", lambda g, pm=pm, g4=g4: g.matmul(pm[:, 0:128], lhsT=onesb[:], rhs=wsT[g4][:], start=True, stop=True), [onesb, wsT[g4]], [pm])
                op("act", lambda g, pm=pm, g4=g4: g.activation(out=rsb[g4][:], in_=pm[:, 0:128], func=AF.Copy), [pm], [rsb[g4]])
                dma("sp", bsb[g4], None, bsb[g4][:], bs_d[l, g4].partition_broadcast(128))
            winl = w_in[l].rearrange("(kc p) f -> p kc f", p=128)
            op("dve", lambda g: g.memset(st1[:], 0.0), [], [st1])
            op("dve", lambda g: g.memset(st2[:], 0.0), [], [st2])
            for cg in range(8):
                wt = WGU[cnt["w"] % 2]; cnt["w"] += 1
                dma("pool", wt, None, wt[:, 0:8192].rearrange("p (kc f) -> p kc f", kc=16), winl[:, :, 4096 + cg * 512: 4096 + (cg + 1) * 512])
                for ci, (c0, c1) in enumerate(TC):
                    M = c1 - c0
                    pm = PM[cnt["m"] % 2]; cnt["m"] += 1
                    for kc in range(16):
                        op("pe", lambda g, pm=pm, kc=kc, c0=c0, c1=c1, M=M, wt=wt: g.matmul(pm[0:M, :], lhsT=hT[:, kc, c0:c1], rhs=wt[:, kc * 512:(kc + 1) * 512], start=(kc == 0), stop=(kc == 15)),
                           [hT, wt], [pm], inc=(kc == 15))
                    op("act", lambda g, pm=pm, ci=ci, cg=cg, M=M: g.activation(out=zt[ci][0:M, cg * 512:(cg + 1) * 512], in_=pm[0:M, :], func=AF.Gelu, accum_out=st1[0:M, ci, cg:cg + 1]),
                       [pm], [Z, st1])
                    jt = tmpB[cnt["t"] % 2]; cnt["t"] += 1
                    op("dve", lambda g, jt=jt, ci=ci, cg=cg, M=M: g.scalar_tensor_tensor(out=jt[0:M, 0:512], in0=zt[ci][0:M, cg * 512:(cg + 1) * 512], scalar=1.0, in1=zt[ci][0:M, cg * 512:(cg + 1) * 512], op0=ALU.mult, op1=ALU.mult, accum_out=st2[0:M, ci, cg:cg + 1]),
                       [Z], [jt, st2])
            for ci, (c0, c1) in enumerate(TC):
                M = c1 - c0
                op("dve", lambda g, ci=ci, M=M: g.reduce_sum(out=stv[0:M, ci, 0:1], in_=st1[0:M, ci, :], axis=AX.X), [st1], [stv])
                op("dve", lambda g, ci=ci, M=M: g.reduce_sum(out=stv[0:M, ci, 1:2], in_=st2[0:M, ci, :], axis=AX.X), [st2, stv], [stv])
                op("dve", lambda g, ci=ci, M=M: g.tensor_scalar(out=stv[0:M, ci, 0:2], in0=stv[0:M, ci, 0:2], scalar1=1.0 / 4096, scalar2=None, op0=ALU.mult), [stv], [stv])
                op("dve", lambda g, ci=ci, M=M: g.tensor_tensor(out=stv[0:M, ci, 2:3], in0=stv[0:M, ci, 0:1], in1=stv[0:M, ci, 0:1], op=ALU.mult), [stv], [stv])
                op("dve", lambda g, ci=ci, M=M: g.tensor_tensor(out=stv[0:M, ci, 2:3], in0=stv[0:M, ci, 1:2], in1=stv[0:M, ci, 2:3], op=ALU.subtract), [stv], [stv])
                op("act", lambda g, ci=ci, M=M: g.activation(out=stv[0:M, ci, 2:3], in_=stv[0:M, ci, 2:3], func=AF.Sqrt, bias=1e-5, scale=1.0), [stv], [stv])
                op("dve", lambda g, ci=ci, M=M: g.reciprocal(out=stv[0:M, ci, 2:3], in_=stv[0:M, ci, 2:3]), [stv], [stv])
                op("dve", lambda g, ci=ci, M=M: g.scalar_tensor_tensor(out=stv[0:M, ci, 3:4], in0=stv[0:M, ci, 0:1], scalar=-1.0, in1=stv[0:M, ci, 2:3], op0=ALU.mult, op1=ALU.mult), [stv], [stv])
                op("act", lambda g, ci=ci, M=M: g.activation(out=zt[ci][0:M, :], in_=zt[ci][0:M, :], func=AF.Identity, bias=stv[0:M, ci, 3:4], scale=stv[0:M, ci, 2:3]), [Z, stv], [Z])
            if own_a is not None:
                for pc in range(8):
                    ga = tmpA[0]; ba = tmpA[1]
                    dma("sp", ga, None, ga[0:16, 0:512], lng_d[l, pc * 512:(pc + 1) * 512].partition_broadcast(16))
                    dma("sp", ba, None, ba[0:16, 0:512], lnb_d[l, pc * 512:(pc + 1) * 512].partition_broadcast(16))
                    op("dve", lambda g, pc=pc, ga=ga: g.tensor_tensor(out=ga[0:16, 0:512], in0=zt[4][0:16, pc * 512:(pc + 1) * 512], in1=ga[0:16, 0:512], op=ALU.mult), [Z, ga], [ga])
                    op("dve", lambda g, ga=ga, ba=ba: g.tensor_tensor(out=ga[0:16, 0:512], in0=ga[0:16, 0:512], in1=ba[0:16, 0:512], op=ALU.add), [ga, ba], [ga])
                    dma("sp", None, ga, gv[l, own_a * 16:(own_a + 1) * 16, pc * 512:(pc + 1) * 512], ga[0:16, 0:512])
            woutl = w_out[l]

            def load_u(cgrp):
                wt = WGU[cnt["w"] % 2]; cnt["w"] += 1
                dma("pool", wt, None, wt[:, 0:4096].rearrange("p (kc f) -> p kc f", kc=16), winl[:, :, cgrp * 256:(cgrp + 1) * 256])
                return wt

            def load_o(cgrp):
                wt = WDn[cnt["wd"] % 2]; cnt["wd"] += 1
                dma("pool", wt, None, wt[:, 0:4096].rearrange("p (j d) -> p j d", j=2), woutl[cgrp * 256:(cgrp + 1) * 256, :].rearrange("(j p) d -> p j d", p=128))
                return wt

            def umix(wt, ab, cgrp, pending):
                for j in range(2):
                    drain(pending, 16)
                    cc = cgrp * 2 + j
                    g4 = cc // 8
                    proj_chunk(wt, lambda kc, j=j, wt=wt: wt[:, kc * 256 + j * 128: kc * 256 + (j + 1) * 128])
                    uT = tmpB[cnt["t"] % 2]; cnt["t"] += 1
                    for ti, (a, b) in enumerate(TG):
                        op("act", lambda g, uT=uT, ti=ti, a=a, b=b: g.activation(out=uT[:, a:b], in_=PG[ti][:, 0:264], func=AF.Gelu), [PG[ti]], [uT])
                    for ci in range(4):
                        op("pe", lambda g, ci=ci, cc=cc, g4=g4: g.matmul(PU[0][:, ci * 128:(ci + 1) * 128], lhsT=zt[ci][:, cc * 128:(cc + 1) * 128], rhs=wsT[g4][:], start=True, stop=True),
                           [Z, wsT[g4]], [PU[0]])
                    op("pe", lambda g, cc=cc, g4=g4: g.matmul(PU[1][:, 0:16], lhsT=zt[4][0:16, cc * 128:(cc + 1) * 128], rhs=wsT[g4][0:16, 0:16], start=True, stop=True),
                       [Z, wsT[g4]], [PU[1]])
                    E = tmpA[0]
                    if cc % 8 == 0:
                        for (c0, c1) in TC:
                            op("act", lambda g, g4=g4, c0=c0, c1=c1: g.activation(out=rsx[:, c0:c1], in_=rsb[g4][:, 0:c1 - c0], func=AF.Copy), [rsb[g4]], [rsx])
                            op("act", lambda g, g4=g4, c0=c0, c1=c1: g.activation(out=bsx[:, c0:c1], in_=bsb[g4][:, 0:c1 - c0], func=AF.Copy), [bsb[g4]], [bsx])
                    op("dve", lambda g, E=E, cc=cc: g.scalar_tensor_tensor(out=E[:], in0=rsx[:], scalar=pcol(P_LB + l * 32, cc), in1=bsx[:], op0=ALU.mult, op1=ALU.add),
                       [rsx, bsx, prm], [E])
                    op("dve", lambda g, E=E, cc=cc: g.scalar_tensor_tensor(out=E[:, 0:512], in0=PU[0][:, 0:512], scalar=pcol(P_LG + l * 32, cc), in1=E[:, 0:512], op0=ALU.mult, op1=ALU.add),
                       [PU[0], prm, E], [E])
                    op("dve", lambda g, E=E, cc=cc: g.scalar_tensor_tensor(out=E[:, 512:528], in0=PU[1][:, 0:16], scalar=pcol(P_LG + l * 32, cc), in1=E[:, 512:528], op0=ALU.mult, op1=ALU.add),
                       [PU[1], prm, E], [E])
                    op("dve", lambda g, E=E, uT=uT, j=j, ab=ab: g.tensor_tensor(out=ab[:, j, :], in0=E[:], in1=uT[:], op=ALU.mult), [E, uT], [ab])

            def wout(wt, ab):
                return down_units(wt, lambda j, dc, wt=wt: wt[:, j * 2048 + dc * 128: j * 2048 + (dc + 1) * 128], ab, 2, lambda j, a, b, ab=ab: ab[:, j, a:b], 1.0)

            NG = 16
            gu = {0: load_u(0)}
            dd = {0: load_o(0)}
            pending = []
            for cgrp in range(NG):
                if cgrp + 1 < NG:
                    gu[cgrp + 1] = load_u(cgrp + 1)
                ab = actb[cnt["a"] % 2]; cnt["a"] += 1
                umix(gu[cgrp], ab, cgrp, pending)
                drain(pending, 1000)
                if cgrp + 1 < NG:
                    dd[cgrp + 1] = load_o(cgrp + 1)
                pending = wout(dd[cgrp], ab)
            drain(pending, 1000)

        def kv_phase(slot, own_a):
            rmsnorm(P_KV)
            wkl = wk_d.rearrange("(kc p) f -> p kc f", p=128)
            wvl = wv_d.rearrange("(kc p) f -> p kc f", p=128)
            for c4 in range(4 if 'kt' in KVS else 0):
                wt = WGU[cnt["w"] % 2]; cnt["w"] += 1
                dma("pool", wt, None, wt[:, 0:8192].rearrange("p (kc f) -> p kc f", kc=16), wkl[:, :, c4 * 512:(c4 + 1) * 512])
                for j in range(4):
                    h = c4 * 4 + j
                    proj_chunk(wt, lambda kc, j=j, wt=wt: wt[:, kc * 512 + j * 128: kc * 512 + (j + 1) * 128])
                    kt = tmpB[cnt["t"] % 2]; cnt["t"] += 1
                    for ti, (a, b) in enumerate(TG):
                        op("act", lambda g, kt=kt, ti=ti, a=a, b=b: g.activation(out=kt[:, a:b], in_=PG[ti][:, 0:264], func=AF.Copy), [PG[ti]], [kt])
                    dma("sp", kS, kt, kS[slot, h], kt[:, 0:512])
                    if own_a is not None:
                        op("dve", lambda g, kt=kt, h=h: g.tensor_copy(out=kTs[own_a][:, h, :], in_=kt[:, 512:528]), [kt], [kTs[own_a]])
                if own_a is not None and 'ktok' in KVS:
                    for ci, (c0, c1) in enumerate(TC):
                        M = c1 - c0
                        pm = PM[cnt["m"] % 2]; cnt["m"] += 1
                        for kc in range(16):
                            op("pe", lambda g, pm=pm, kc=kc, c0=c0, c1=c1, M=M, wt=wt: g.matmul(pm[0:M, :], lhsT=hT[:, kc, c0:c1], rhs=wt[:, kc * 512:(kc + 1) * 512], start=(kc == 0), stop=(kc == 15)),
                               [hT, wt], [pm], inc=(kc == 15))
                        ot = tmpA[cnt["t"] % 2]; cnt["t"] += 1
                        op("act", lambda g, pm=pm, ot=ot, M=M: g.activation(out=ot[0:M, 0:512], in_=pm[0:M, :], func=AF.Copy), [pm], [ot])
                        if ci < 4:
                            dma("sp", None, ot, k_p[own_a * 512 + c0: own_a * 512 + c1, c4 * 512:(c4 + 1) * 512], ot[0:M, 0:512])
                        else:
                            dma("sp", None, ot, k_s[own_a * 16:(own_a + 1) * 16, c4 * 512:(c4 + 1) * 512], ot[0:M, 0:512])
            for c4 in range(4 if 'v' in KVS else 0):
                wt = WGU[cnt["w"] % 2]; cnt["w"] += 1
                dma("pool", wt, None, wt[:, 0:8192].rearrange("p (kc f) -> p kc f", kc=16), wvl[:, :, c4 * 512:(c4 + 1) * 512])
                for ci, (c0, c1) in enumerate(TC):
                    M = c1 - c0
                    if ci == 4 and own_a is None:
                        continue
                    pm = PM[cnt["m"] % 2]; cnt["m"] += 1
                    for kc in range(16):
                        op("pe", lambda g, pm=pm, kc=kc, c0=c0, c1=c1, M=M, wt=wt: g.matmul(pm[0:M, :], lhsT=hT[:, kc, c0:c1], rhs=wt[:, kc * 512:(kc + 1) * 512], start=(kc == 0), stop=(kc == 15)),
                           [hT, wt], [pm], inc=(kc == 15))
                    if ci < 4:
                        vt = tmpB[cnt["t"] % 2]; cnt["t"] += 1
                        op("act", lambda g, pm=pm, vt=vt, M=M: g.activation(out=vt[0:M, 0:512], in_=pm[0:M, :], func=AF.Copy), [pm], [vt])
                        dma("sp", vS, vt, vS[slot, ci, :, c4 * 512:(c4 + 1) * 512], vt[:, 0:512])
                    else:
                        op("act", lambda g, pm=pm, c4=c4: g.activation(out=vss[own_a][0:16, c4 * 512:(c4 + 1) * 512], in_=pm[0:16, :], func=AF.Copy), [pm], [vss[own_a]])
                    if own_a is not None:
                        ot = tmpA[cnt["t"] % 2]; cnt["t"] += 1
                        op("dve", lambda g, pm=pm, ot=ot, M=M: g.tensor_copy(out=ot[0:M, 0:512], in_=pm[0:M, :]), [pm], [ot])
                        if ci < 4:
                            dma("sp", None, ot, v_p[own_a * 512 + c0: own_a * 512 + c1, c4 * 512:(c4 + 1) * 512], ot[0:M, 0:512])
                        else:
                            dma("sp", None, ot, v_s[own_a * 16:(own_a + 1) * 16, c4 * 512:(c4 + 1) * 512], ot[0:M, 0:512])
            for ci, (c0, c1) in enumerate(TC if 'lf' in KVS else []):
                M = c1 - c0
                if ci == 4 and own_a is None:
                    continue
                pm = PM[cnt["m"] % 2]; cnt["m"] += 1
                for kc in range(16):
                    op("pe", lambda g, pm=pm, kc=kc, c0=c0, c1=c1, M=M: g.matmul(pm[0:M, 0:16], lhsT=hT[:, kc, c0:c1], rhs=wfb[:, kc, :], start=(kc == 0), stop=(kc == 15)),
                       [hT, wfb], [pm], inc=(kc == 15))
                ta = tmpA[cnt["t"] % 2]; cnt["t"] += 1
                dst = lfS[0:M, slot * 4 + ci, :] if ci < 4 else lfs[own_a][0:16, :]
                dstt = lfS if ci < 4 else lfs[own_a]
                op("dve", lambda g, pm=pm, ta=ta, M=M: g.tensor_tensor(out=ta[0:M, 0:16], in0=pm[0:M, 0:16], in1=bfb[0:M, :], op=ALU.add), [pm, bfb], [ta])
                op("act", lambda g, ta=ta, M=M: g.activation(out=ta[0:M, 16:32], in_=ta[0:M, 0:16], func=AF.Abs), [ta], [ta])
                op("act", lambda g, ta=ta, M=M: g.activation(out=ta[0:M, 16:32], in_=ta[0:M, 16:32], func=AF.Exp, scale=-1.0), [ta], [ta])
                op("act", lambda g, ta=ta, M=M: g.activation(out=ta[0:M, 16:32], in_=ta[0:M, 16:32], func=AF.Ln, bias=1.0, scale=1.0), [ta], [ta])
                op("dve", lambda g, ta=ta, M=M: g.tensor_scalar(out=ta[0:M, 0:16], in0=ta[0:M, 0:16], scalar1=0.0, scalar2=None, op0=ALU.min), [ta], [ta])
                op("dve", lambda g, ta=ta, M=M, dst=dst: g.tensor_tensor(out=dst, in0=ta[0:M, 0:16], in1=ta[0:M, 16:32], op=ALU.subtract), [ta], [dstt])
                if own_a is not None:
                    if ci < 4:
                        dma("sp", None, lfS, lf_p[own_a * 512 + c0: own_a * 512 + c1, :], dst)
                    else:
                        dma("sp", None, lfs[own_a], lf_s[own_a * 16:(own_a + 1) * 16, :], dst)


        Bk = kb.sb([128, 64, 16], F32, "Bk")
        Wq = kb.sb([128, 8, 16], F32, "Wq")
        Lm = _Zv(Z[:, 16896:18944].bitcast(F32).rearrange("p (b h) -> p b h", h=16))
        Rc = kb.sb([128, 17, 16], F32, "Rc")
        Lc = kb.sb([128, 17, 16], F32, "Lc")
        cqb = kb.sb([128, NT], F32, "cqb")
        rhc = kb.sb([128, 512], F32, "rhc")
        Pt = [kb.sb([128, 512], BF16, f"Pt{i}") for i in range(2)]
        kcs = kb.sb([128, 2048], BF16, "kcs")
        kcT = kb.sb([128, 2048], BF16, "kcT")
        vcs = kb.sb([128, 2048], BF16, "vcs")
        rec = rstd

        def suffix_excl(L, R, nb):
            n = nb * 16
            Lf = L[:, 0:nb, :].rearrange("p b h -> p (b h)"); Rf = R[:, 0:nb, :].rearrange("p b h -> p (b h)")
            for c0 in range(0, n, 512):
                c1 = min(n, c0 + 512)
                op("pe", lambda g, c0=c0, c1=c1: g.matmul(PM[0][:, 0:c1 - c0], lhsT=SL[:], rhs=Lf[:, c0:c1], start=True, stop=True), [SL, L], [PM[0]])
                op("pe", lambda g, c0=c0, c1=c1: g.matmul(PM[1][:, 0:c1 - c0], lhsT=onesf[:], rhs=Lf[:, c0:c1], start=True, stop=True), [onesf, L], [PM[1]])
                op("dve", lambda g, c0=c0, c1=c1: g.tensor_copy(out=Rf[:, c0:c1], in_=PM[0][:, 0:c1 - c0]), [PM[0]], [R])
                op("act", lambda g, c0=c0, c1=c1: g.activation(out=Lf[:, c0:c1], in_=PM[1][:, 0:c1 - c0], func=AF.Copy), [PM[1]], [L])
            op("dve", lambda g: g.memset(rhc[:, 0:16], 0.0), [], [rhc])
            for b in range(nb - 2, -1, -1):
                op("dve", lambda g, b=b: g.tensor_tensor(out=rhc[:, 0:16], in0=rhc[:, 0:16], in1=L[:, b + 1, :], op=ALU.add), [rhc, L], [rhc])
                op("dve", lambda g, b=b: g.tensor_tensor(out=R[:, b, :], in0=R[:, b, :], in1=rhc[:, 0:16], op=ALU.add), [R, rhc], [R])

        def prefix_incl(L, W, nb, rows=128):
            n = nb * 16
            Lf = L[0:rows, 0:nb, :].rearrange("p b h -> p (b h)"); Wf = W[0:rows, 0:nb, :].rearrange("p b h -> p (b h)")
            op("pe", lambda g: g.matmul(PM[0][0:rows, 0:n], lhsT=UT[0:rows, 0:rows], rhs=Lf, start=True, stop=True), [UT, L], [PM[0]])
            op("pe", lambda g: g.matmul(PM[1][0:rows, 0:n], lhsT=onesf[0:rows, 0:rows], rhs=Lf, start=True, stop=True), [onesf, L], [PM[1]])
            op("dve", lambda g: g.tensor_copy(out=Wf, in_=PM[0][0:rows, 0:n]), [PM[0]], [W])
            op("act", lambda g: g.activation(out=Lf, in_=PM[1][0:rows, 0:n], func=AF.Copy), [PM[1]], [L])
            op("dve", lambda g: g.memset(rhc[:, 0:16], 0.0), [], [rhc])
            for b in range(1, nb):
                op("dve", lambda g, b=b: g.tensor_tensor(out=rhc[0:rows, 0:16], in0=rhc[0:rows, 0:16], in1=L[0:rows, b - 1, :], op=ALU.add), [rhc, L], [rhc])
                op("dve", lambda g, b=b: g.tensor_tensor(out=W[0:rows, b, :], in0=W[0:rows, b, :], in1=rhc[0:rows, 0:16], op=ALU.add), [W, rhc], [W])

        def attn_prep(a):
            for b in range(56 if CTX else 0):
                op("dve", lambda g, b=b: g.tensor_scalar(out=Lm[:, b, :], in0=lfS[:, b, :], scalar1=vbt[:, 0, b:b + 1], scalar2=None, op0=ALU.mult), [lfS, vbt], [Lm])
            if CTX:
                suffix_excl(Lm, Bk, 56)
            for b in range(56 if CTX else 0):
                op("dve", lambda g, b=b: g.tensor_scalar(out=Bk[:, b, :], in0=Bk[:, b, :], scalar1=vbt[:, 1, b:b + 1], scalar2=None, op0=ALU.add), [Bk, vbt], [Bk])
            nbo = 4 * (a + 1)
            op("dve", lambda g: g.tensor_copy(out=Lm[:, 0:nbo, :], in_=lfS[:, 56:56 + nbo, :]), [lfS], [Lm])
            prefix_incl(Lm, Wq, nbo)
            op("dve", lambda g: g.tensor_scalar(out=Bk[:, 56:56 + nbo, :], in0=Wq[:, 0:nbo, :], scalar1=-1.0, scalar2=None, op0=ALU.mult), [Wq], [Bk])
            dma("sp", Lc, None, Lc[:, 0:16, :], cl_d[a].rearrange("(b t) h -> t b h", t=128))
            suffix_excl(Lc, Rc, 16)
            op("dve", lambda g: g.tensor_copy(out=Lc[0:16, 16:17, :], in_=lfs[a][0:16, :].rearrange("p (o h) -> p o h", o=1)), [lfs[a]], [Lc])
            prefix_incl(_Sl(Lc, 16), _Sl(Rc, 16), 1, rows=16)

        class _Sl:
            def __init__(self, t, b0):
                self.t = t; self.b0 = b0
            def __getitem__(self, k):
                p, b, h = k
                b = slice(b.start + self.b0, b.stop + self.b0) if isinstance(b, slice) else b + self.b0
                return self.t.ap[p, b, h]
            lw = property(lambda s: s.t.lw, lambda s, v: setattr(s.t, "lw", v))
            rd = property(lambda s: s.t.rd, lambda s, v: setattr(s.t, "rd", v))

        def bcast_rows(dst_ap, dst_t, src_col_fn, nblk, rows, width):
            for blk in range(nblk):
                op("dve", lambda g, blk=blk: g.tensor_scalar(out=rhc[0:rows, blk * width:(blk + 1) * width], in0=identf[0:rows, 0:width], scalar1=src_col_fn(blk), scalar2=None, op0=ALU.mult), [identf, Wq, Rc], [rhc])
            n = nblk * width
            op("pe", lambda g: g.matmul(PM[0][:, 0:n], lhsT=onesf[0:rows, :], rhs=rhc[0:rows, 0:n], start=True, stop=True), [onesf, rhc], [PM[0]])
            op("act", lambda g: g.activation(out=dst_ap, in_=PM[0][:, 0:n], func=AF.Copy), [PM[0]], [dst_t])

        def fox(j, a):
            l = 2 + j
            rmsnorm(P_MX + l * 16)
            wql = wq_d[j].rearrange("(kc p) f -> p kc f", p=128)
            for c4 in range(4):
                wt = WGU[cnt["w"] % 2]; cnt["w"] += 1
                dma("pool", wt, None, wt[:, 0:8192].rearrange("p (kc f) -> p kc f", kc=16), wql[:, :, c4 * 512:(c4 + 1) * 512])
                for jj in range(4):
                    h = c4 * 4 + jj
                    proj_chunk(wt, lambda kc, jj=jj, wt=wt: wt[:, kc * 512 + jj * 128: kc * 512 + (jj + 1) * 128])
                    for ti, (ca, cb) in enumerate(TG):
                        op("act", lambda g, h=h, ti=ti, ca=ca, cb=cb: g.activation(out=qT[:, h, ca:cb], in_=PG[ti][:, 0:264], func=AF.Copy), [PG[ti]], [Z])
            KH, VH = WGU[0], WGU[1]
            nslot = 15 + a
            for h in range(16):
                slo = 0 if CTX else 14
                dma("sp", KH, kS, KH[:, slo * 512:nslot * 512].rearrange("p (s n) -> p s n", s=nslot - slo), kS[slo:nslot, h].rearrange("s d n -> d s n"))
                for s_ in range(slo, nslot):
                    dma("sp", VH, vS, VH[:, s_ * 512:(s_ + 1) * 512].rearrange("p (b d) -> p b d", b=4), vS[s_, :, :, h * 128:(h + 1) * 128].rearrange("b t d -> t b d"))
                bcast_rows(cqb[:, 0:512], cqb, lambda blk, h=h: Wq[:, 4 * a + blk, h:h + 1], 4, 128, 128)
                blocks = ([(s_, b_) for s_ in range(14) for b_ in range(4)] if CTX else []) + [(14 + a2, b_) for a2 in range(a + 1) for b_ in range(4)]
                for bi, (s_, b_) in enumerate(blocks):
                    diag = (s_ == 14 + a)
                    q0 = b_ * 128 if diag else 0
                    nq = 512 - q0
                    ps = PG[bi % 2]
                    pt = Pt[bi % 2]
                    ta = tmpA[bi % 2]
                    gb = s_ * 4 + b_
                    op("pe", lambda g, ps=ps, s_=s_, b_=b_, q0=q0, nq=nq, h=h: g.matmul(ps[:, 0:nq], lhsT=KH[:, s_ * 512 + b_ * 128: s_ * 512 + (b_ + 1) * 128], rhs=qT[:, h, q0:512], start=True, stop=True), [KH, Z], [ps])
                    op("dve", lambda g, ps=ps, ta=ta, q0=q0, nq=nq: g.scalar_tensor_tensor(out=ta[:, 0:nq], in0=ps[:, 0:nq], scalar=SCALE, in1=cqb[:, q0:512], op0=ALU.mult, op1=ALU.add), [ps, cqb], [ta])
                    if diag:
                        op("dve", lambda g, ta=ta: g.tensor_tensor(out=ta[:, 0:128], in0=ta[:, 0:128], in1=TRIN[:], op=ALU.add), [ta, TRIN], [ta])
                    op("act", lambda g, ta=ta, pt=pt, nq=nq, gb=gb, h=h: g.activation(out=pt[:, 0:nq], in_=ta[:, 0:nq], func=AF.Exp, bias=Bk[:, gb, h:h + 1], scale=1.0), [ta, Bk], [pt])
                    first = (bi == 0); last = (bi == len(blocks) - 1)
                    op("pe", lambda g, pt=pt, s_=s_, b_=b_, q0=q0, nq=nq, first=first, last=last: g.matmul(PU[0][:, q0:512], lhsT=VH[:, s_ * 512 + b_ * 128: s_ * 512 + (b_ + 1) * 128], rhs=pt[:, 0:nq], start=first, stop=last), [VH, pt], [PU[0]], inc=False)
                    op("pe", lambda g, pt=pt, q0=q0, nq=nq, first=first, last=last: g.matmul(PU[1][:, q0:512], lhsT=onesb[:], rhs=pt[:, 0:nq], start=first, stop=last), [onesb, pt], [PU[1]])
                op("dve", lambda g: g.reciprocal(out=rec[:, 0:512], in_=PU[1][:]), [PU[1]], [rec])
                op("dve", lambda g, h=h: g.tensor_tensor(out=oT[:, h, 0:512], in0=PU[0][:], in1=rec[:, 0:512], op=ALU.mult), [PU[0], rec], [Z])
                dma("pool", kcs, None, kcs[:].rearrange("p (b d) -> p b d", b=16), ck_d[a, :, h * 128:(h + 1) * 128].rearrange("(b t) d -> t b d", t=128))
                dma("pool", vcs, None, vcs[:].rearrange("p (b d) -> p b d", b=16), cv_d[a, :, h * 128:(h + 1) * 128].rearrange("(b t) d -> t b d", t=128))
                for q4 in range(4):
                    pmb = PMb[q4 % 2]; pmf = PM[q4 % 2]
                    for jj in range(4):
                        b = q4 * 4 + jj
                        op("pe", lambda g, pmb=pmb, jj=jj, b=b: g.transpose(out=pmb[:, jj * 128:(jj + 1) * 128], in_=kcs[:, b * 128:(b + 1) * 128], identity=identb[:]), [kcs, identb], [pmf], inc=(jj == 3))
                    op("dve", lambda g, pmb=pmb, q4=q4: g.tensor_copy(out=kcT[:, q4 * 512:(q4 + 1) * 512], in_=pmb[:, 0:512]), [pmf], [kcT])
                bcast_rows(cqb[:, 512:528], cqb, lambda blk, h=h: Rc[0:16, 16, h:h + 1], 1, 16, 16)
                for b in range(16):
                    op("pe", lambda g, b=b, h=h: g.matmul(PD[0][:, b * 16:(b + 1) * 16], lhsT=kcT[:, b * 128:(b + 1) * 128], rhs=qT[:, h, 512:528], start=True, stop=True), [kcT, Z], [PD[0]], inc=False)
                op("pe", lambda g, h=h: g.matmul(PD[0][0:16, 256:272], lhsT=kTs[a][:, h, :], rhs=qT[:, h, 512:528], start=True, stop=True), [kTs[a], Z], [PD[0]])
                ta = tmpA[0]; pt = Pt[0]
                for b in range(17):
                    rows = 128 if b < 16 else 16
                    op("dve", lambda g, b=b, rows=rows: g.scalar_tensor_tensor(out=ta[0:rows, b * 16:(b + 1) * 16], in0=PD[0][0:rows, b * 16:(b + 1) * 16], scalar=SCALE, in1=cqb[0:rows, 512:528], op0=ALU.mult, op1=ALU.add), [PD[0], cqb], [ta])
                op("dve", lambda g: g.tensor_tensor(out=ta[0:16, 256:272], in0=ta[0:16, 256:272], in1=TRIN[0:16, 0:16], op=ALU.add), [ta, TRIN], [ta])
                op("dve", lambda g, h=h: g.tensor_scalar(out=rhc[0:16, 16:17], in0=Rc[0:16, 16, h:h + 1], scalar1=-1.0, scalar2=None, op0=ALU.mult), [Rc], [rhc])
                for b in range(17):
                    rows = 128 if b < 16 else 16
                    bias = Rc[:, b, h:h + 1] if b < 16 else rhc[0:16, 16:17]
                    op("act", lambda g, b=b, rows=rows, bias=bias: g.activation(out=pt[0:rows, b * 16:(b + 1) * 16], in_=ta[0:rows, b * 16:(b + 1) * 16], func=AF.Exp, bias=bias, scale=1.0), [ta, Rc, rhc], [pt])
                for b in range(17):
                    rows = 128 if b < 16 else 16
                    lv = vcs[:, b * 128:(b + 1) * 128] if b < 16 else vss[a][0:16, h * 128:(h + 1) * 128]
                    op("pe", lambda g, b=b, rows=rows, lv=lv: g.matmul(PD[1][:, 0:16], lhsT=lv, rhs=pt[0:rows, b * 16:(b + 1) * 16], start=(b == 0), stop=(b == 16)), [vcs, vss[a], pt], [PD[1]], inc=False)
                    op("pe", lambda g, b=b, rows=rows: g.matmul(PM[1][:, 16:32], lhsT=onesb[0:rows, :], rhs=pt[0:rows, b * 16:(b + 1) * 16], start=(b == 0), stop=(b == 16)), [onesb, pt], [PM[1]], inc=(b == 16))
                op("dve", lambda g: g.reciprocal(out=rec[:, 0:16], in_=PM[1][:, 16:32]), [PM[1]], [rec])
                op("dve", lambda g, h=h: g.tensor_tensor(out=oT[:, h, 512:528], in0=PD[1][:, 0:16], in1=rec[:, 0:16], op=ALU.mult), [PD[1], rec], [Z])
            wol = wo_d[j]
            for hg in range(8):
                wt = WDn[cnt["wd"] % 2]; cnt["wd"] += 1
                dma("pool", wt, None, wt[:, 0:4096].rearrange("p (jj d) -> p jj d", jj=2), wol[hg * 256:(hg + 1) * 256, :].rearrange("(jj p) d -> p jj d", p=128))
                acc_down(wt, lambda jj, dc, wt=wt: wt[:, jj * 2048 + dc * 128: jj * 2048 + (dc + 1) * 128], Z, 2, lambda jj, ca, cb, hg=hg: oT[:, hg * 2 + jj, ca:cb], 1.0)

        def final_out(a):
            rstd_compute()
            for ci, (c0, c1) in enumerate(TC):
                M = c1 - c0
                for q4 in range(4):
                    pm = PM[cnt["m"] % 2]; cnt["m"] += 1
                    for jj in range(4):
                        kc = q4 * 4 + jj
                        yt = tmpA[jj % 2]
                        op("dve", lambda g, kc=kc, yt=yt, c0=c0, c1=c1, M=M: g.scalar_tensor_tensor(out=yt[:, 0:M], in0=xT[:, kc, c0:c1], scalar=pcol(P_FN, kc), in1=rstd[:, c0:c1], op0=ALU.mult, op1=ALU.mult), xd(kc) + [prm, rstd], [yt])
                        op("pe", lambda g, pm=pm, jj=jj, yt=yt, M=M: g.transpose(out=pm[0:M, jj * 128:(jj + 1) * 128], in_=yt[:, 0:M], identity=identf[:]), [yt, identf], [pm])
                    op("act", lambda g, pm=pm, q4=q4, M=M: g.activation(out=stg[0:M, q4 * 512:(q4 + 1) * 512], in_=pm[0:M, :], func=AF.Copy), [pm], [Z])
                if ci < 4:
                    dma("sp", None, Z, y_p[a * 512 + c0: a * 512 + c1, :], stg[0:M, :])
                else:
                    dma("sp", None, Z, y_s[a * 16:(a + 1) * 16, :], stg[0:M, :])

        for tile in TILES:
            own_a = tile - 14 if tile >= 14 else None
            load_x(tile)
            for l in range(NLA):
                if "ffn1" in STG:
                    ffn(l, 0, P_F1)
                if "gmlp" in STG:
                    gmlp(l, own_a)
                if "ffn2" in STG:
                    ffn(l, 1, P_F2)
            if "kv" in STG:
                kv_phase(tile, own_a)
            if own_a is not None and BL:
                attn_prep(own_a)
                for j in range(2):
                    ffn(2 + j, 0, P_F1)
                    fox(j, own_a)
                    ffn(2 + j, 1, P_F2)
                final_out(own_a)
        kb.emit()
    return nc


TILES = list(range(16))
CTX = True
BL = True
NLA = 2
WL = 4
KVS = {'kt', 'ktok', 'v', 'lf'}
STG = {'ffn1', 'gmlp', 'ffn2', 'kv'}
_NC = None


def _prep_inputs(inp, c):
    f = np.float32
    xp = np.asarray(inp["x_prompt"], f)[0]
    xs = np.asarray(inp["x_sample"], f)
    own = [2 * c, 2 * c + 1]
    others = [t for t in range(16) if t not in own]
    tiles = others + own
    xin = np.empty((16, NT, D), f)
    for i, t in enumerate(tiles):
        xin[i, :512] = xp[t * 512:(t + 1) * 512]
        xin[i, 512:] = xs[2 * c + (i - 14 if i >= 14 else 0)]
    def pl(arr):
        arr = np.asarray(arr, f)
        L = arr.shape[0]
        return arr.reshape(L, -1, 128).transpose(2, 0, 1).reshape(128, -1)
    prm = np.concatenate([pl(inp["ffn1_norm"]), pl(inp["mix_norm"]), pl(inp["ffn2_norm"]), pl(np.asarray(inp["kv_norm"])[None]),
                          pl(np.asarray(inp["final_norm"])[None]), pl(inp["gmlp_ln_g"]), pl(inp["gmlp_ln_b"])], axis=1)
    assert prm.shape == (128, NPRM)
    vb = np.zeros((2, 56), f)
    for i, t in enumerate(others):
        ok = t < 2 * c
        vb[0, i * 4:(i + 1) * 4] = 1.0 if ok else 0.0
        vb[1, i * 4:(i + 1) * 4] = 0.0 if ok else NEG
    m = {"xin": xin.reshape(16 * NT, D), "prm": np.ascontiguousarray(prm), "vb": vb,
         "wsT": np.ascontiguousarray(np.asarray(inp["gmlp_w_s"], f).transpose(0, 1, 3, 2)),
         "cache_k": np.asarray(inp["cache_k"], f)[2 * c:2 * c + 2].reshape(2, 2048, D),
         "cache_v": np.asarray(inp["cache_v"], f)[2 * c:2 * c + 2].reshape(2, 2048, D),
         "cache_logf": np.asarray(inp["cache_logf"], f)[2 * c:2 * c + 2]}
    for k in ("ffn1_w_gate", "ffn2_w_gate", "ffn1_w_up", "ffn2_w_up", "ffn1_w_down", "ffn2_w_down", "gmlp_w_in", "gmlp_w_out",
              "gmlp_b_s", "gmlp_ln_g", "gmlp_ln_b", "w_k", "w_v", "w_f", "b_f", "fox_w_q", "fox_w_o"):
        m[k] = np.asarray(inp[k], f)[:WL] if k.startswith('ffn') else np.asarray(inp[k], f)
    return m


def kernel(**inp):
    global _NC
    if _NC is None:
        _NC = build_program()
    cores = list(range(8))
    in_maps = [_prep_inputs(inp, c) for c in cores]
    res = run_bass_kernel_spmd(_NC, in_maps, core_ids=cores)
    R = res.results
    cat = lambda k: np.concatenate([R[c][k] for c in cores], axis=0)
    y_prompt = cat("y_p").reshape(1, 8192, D)
    y_sample = cat("y_s").reshape(16, 16, D)
    k_prompt = cat("k_p").reshape(1, 8192, 16, 128)
    v_prompt = cat("v_p").reshape(1, 8192, 16, 128)
    logf_prompt = cat("lf_p").reshape(1, 8192, 16)
    k_sample = cat("k_s").reshape(16, 16, 16, 128)
    v_sample = cat("v_s").reshape(16, 16, 16, 128)
    logf_sample = cat("lf_s").reshape(16, 16, 16)
    gvs = np.concatenate([R[c]["gv"] for c in cores], axis=1).reshape(2, 16, 16, 4096)
    return (y_prompt, y_sample, k_prompt, v_prompt, logf_prompt, k_sample, v_sample, logf_sample, gvs)
```

```python
import contextlib
import numpy as np
import concourse.bass as bass
import concourse.mybir as mybir
from concourse.bass_utils import run_bass_kernel_spmd

F32 = mybir.dt.float32
BF16 = mybir.dt.bfloat16
AF = mybir.ActivationFunctionType
ALU = mybir.AluOpType
AX = mybir.AxisListType

D = 2048
DFF = 5632
NT = 528
NEG = -1.0e30
TG = [(0, 264), (264, 528)]
TC = [(0, 128), (128, 256), (256, 384), (384, 512), (512, 528)]
SCALE = 128 ** -0.5
P_F1, P_MX, P_F2, P_KV, P_FN, P_LG, P_LB, NPRM = 0, 64, 128, 192, 208, 224, 288, 352


class Tl:
    __slots__ = ("ap", "lw", "rd", "name", "ps")

    def __init__(self, ap, name="", ps=False):
        self.ap = ap
        self.lw = None
        self.rd = {}
        self.name = name
        self.ps = ps

    def __getitem__(self, k):
        return self.ap[k]


class Eng:
    def __init__(self, name):
        self.name = name
        self.count = 0
        self.ops = []
        self.waited = {}
        self.pool = []
        self.pool_cum = []
        self.pool_i = 0


class KB:
    NPOOL = 16

    def __init__(self, nc, stack):
        self.nc = nc
        self.stack = stack
        self.E = {n: Eng(n) for n in ("pe", "act", "dve", "pool", "sp")}
        self.sems = {}
        for n in self.E:
            self.sems[n] = stack.enter_context(nc.semaphore("s_" + n))
        for q in ("sp", "pool"):
            e = self.E[q]
            for i in range(self.NPOOL):
                key = f"d_{q}{i}"
                self.sems[key] = stack.enter_context(nc.semaphore(key))
                e.pool.append(key)
                e.pool_cum.append(0)
        self.ntile = 0

    def sb(self, shape, dt, name=None):
        self.ntile += 1
        name = name or f"t{self.ntile}"
        h = self.stack.enter_context(self.nc.sbuf_tensor(name, list(shape), dt))
        return Tl(h, name)

    def ps(self, shape, dt, name=None):
        self.ntile += 1
        name = name or f"p{self.ntile}"
        h = self.stack.enter_context(self.nc.psum_tensor(name, list(shape), dt))
        return Tl(h, name, ps=True)

    def dram(self, name, shape, dt):
        h = self.nc.dram_tensor(name, list(shape), dt).ap()
        return Tl(h, name)

    def _need(self, e, waits, tok, same_ok):
        if tok is None:
            return
        k, v = tok
        if same_ok and k == e.name:
            return
        if e.waited.get(k, 0) >= v:
            return
        if waits.get(k, 0) < v:
            waits[k] = v

    def _deps(self, e, reads, writes):
        waits = {}
        for t in reads:
            self._need(e, waits, t.lw, e.name == "pe")
            if getattr(t, "ps", False):
                for k, v in t.rd.items():
                    self._need(e, waits, (k, v), True)
        for t in writes:
            self._need(e, waits, t.lw, True)
            for k, v in t.rd.items():
                self._need(e, waits, (k, v), True)
        for k, v in waits.items():
            e.waited[k] = v
        return waits

    def op(self, eng, fn, reads=(), writes=(), inc=True):
        e = self.E[eng]
        waits = self._deps(e, reads, writes)
        if inc:
            e.count += 1
            tok = (eng, e.count)
        else:
            tok = (eng, e.count + 1)
        for t in reads:
            if t.rd.get(tok[0], 0) < tok[1]:
                t.rd[tok[0]] = tok[1]
        for t in writes:
            t.lw = tok
            t.rd = {}
        e.ops.append((waits, fn, eng if inc else None, 1))
        return tok

    def dma(self, q, out_t, in_t, out_ap, in_ap):
        e = self.E[q]
        reads = [in_t] if in_t is not None else []
        writes = [out_t] if out_t is not None else []
        waits = self._deps(e, reads, writes)
        i = e.pool_i
        e.pool_i = (i + 1) % len(e.pool)
        key = e.pool[i]
        prev = e.pool_cum[i]
        if prev > 0 and e.waited.get(key, 0) < prev:
            waits[key] = max(waits.get(key, 0), prev)
            e.waited[key] = prev
        e.pool_cum[i] = prev + 16
        tok = (key, prev + 16)
        for t in reads:
            t.rd[key] = tok[1]
        for t in writes:
            t.lw = tok
            t.rd = {}
        e.ops.append((waits, lambda g: g.dma_start(out=out_ap, in_=in_ap), key, 16))
        return tok

    def emit(self):
        nc = self.nc
        fin = {}
        for q in ("sp", "pool"):
            e = self.E[q]
            for key, cum in zip(e.pool, e.pool_cum):
                if cum > 0:
                    fin[key] = cum
        for n in ("pe", "act", "dve", "pool"):
            if self.E[n].count > 0:
                fin[n] = self.E[n].count
        sems = self.sems
        E = self.E

        def run(e, g):
            for waits, fn, inck, incv in e.ops:
                for k, v in waits.items():
                    g.wait_ge(sems[k], v)
                ins = fn(g)
                if inck is not None:
                    ins.then_inc(sems[inck], incv)

        with nc.Block() as block:
            @block.tensor
            def _(g):
                run(E["pe"], g)

            @block.scalar
            def _(g):
                run(E["act"], g)

            @block.vector
            def _(g):
                run(E["dve"], g)

            @block.gpsimd
            def _(g):
                run(E["pool"], g)

            @block.sync
            def _(g):
                run(E["sp"], g)
                for k, v in fin.items():
                    g.wait_ge(sems[k], v)


def build_program():
    nc = bass.Bass("TRN2", target_bir_lowering=False)

    def din(name, shape, dt=F32):
        return nc.dram_tensor(name, list(shape), dt, kind="ExternalInput").ap()

    def dout(name, shape):
        return nc.dram_tensor(name, list(shape), F32, kind="ExternalOutput").ap()

    xin = din("xin", [16 * NT, D])
    prm_d = din("prm", [128, NPRM])
    vb_d = din("vb", [2, 56])
    wg = [din("ffn1_w_gate", [WL, D, DFF]), din("ffn2_w_gate", [WL, D, DFF])]
    wu = [din("ffn1_w_up", [WL, D, DFF]), din("ffn2_w_up", [WL, D, DFF])]
    wd = [din("ffn1_w_down", [WL, DFF, D]), din("ffn2_w_down", [WL, DFF, D])]
    w_in = din("gmlp_w_in", [2, D, 8192])
    w_out = din("gmlp_w_out", [2, 4096, D])
    wsT_d = din("wsT", [2, 4, 128, 128])
    bs_d = din("gmlp_b_s", [2, 4, 128])
    lng_d = din("gmlp_ln_g", [2, 4096])
    lnb_d = din("gmlp_ln_b", [2, 4096])
    wk_d = din("w_k", [D, D])
    wv_d = din("w_v", [D, D])
    wf_d = din("w_f", [D, 16])
    bf_d = din("b_f", [16])
    wq_d = din("fox_w_q", [2, D, D])
    wo_d = din("fox_w_o", [2, D, D])
    ck_d = din("cache_k", [2, 2048, D])
    cv_d = din("cache_v", [2, 2048, D])
    cl_d = din("cache_logf", [2, 2048, 16])

    y_p = dout("y_p", [1024, D]); y_s = dout("y_s", [32, D])
    k_p = dout("k_p", [1024, D]); v_p = dout("v_p", [1024, D]); lf_p = dout("lf_p", [1024, 16])
    k_s = dout("k_s", [32, D]); v_s = dout("v_s", [32, D]); lf_s = dout("lf_s", [32, 16])
    gv = dout("gv", [2, 32, 4096])

    with contextlib.ExitStack() as st:
        kb = KB(nc, st)
        op, dma = kb.op, kb.dma
        xT = kb.sb([128, 16, NT], F32, "xT")
        xTt = [[Tl(xT.ap, f"xT{dc}_{ti}") for ti in range(2)] for dc in range(16)]
        def xd(dc):
            return [xTt[dc][0], xTt[dc][1]]
        hT = kb.sb([128, 16, NT], BF16, "hT")
        actb = [kb.sb([128, 2, NT], BF16, f"act{i}") for i in range(2)]
        WGU = [kb.sb([128, 8192], BF16, f"WGU{i}") for i in range(2)]
        WDn = [kb.sb([128, 4096], BF16, f"WDn{i}") for i in range(2)]
        prm = kb.sb([128, NPRM], F32, "prm_sb")
        sqt = [kb.sb([128, NT], BF16, f"sqt{i}") for i in range(2)]
        rstd = kb.sb([128, NT], F32, "rstd")
        tmpA = [kb.sb([128, NT], F32, f"tmpA{i}") for i in range(2)]
        tmpB = [kb.sb([128, NT], BF16, f"tmpB{i}") for i in range(2)]
        identf = kb.sb([128, 128], F32, "identf")
        identb = kb.sb([128, 128], BF16, "identb")
        onesb = kb.sb([128, 128], BF16, "onesb")
        onesf = kb.sb([128, 128], F32, "onesf")
        UT = kb.sb([128, 128], F32, "UT")
        SL = kb.sb([128, 128], F32, "SL")
        TRIN = kb.sb([128, 128], F32, "TRIN")
        Z = kb.sb([128, 21504], BF16, "Z")
        class _V:
            def __init__(self, ap): self.ap = ap
            def __getitem__(self, k): return self.ap[k]
        class _Zv:
            def __init__(self, ap): self.ap = ap
            def __getitem__(self, k): return self.ap[k]
            lw = property(lambda s_: Z.lw, lambda s_, v: setattr(Z, "lw", v))
            rd = property(lambda s_: Z.rd, lambda s_, v: setattr(Z, "rd", v))
        zt = [_V(Z[:, i * 4096:(i + 1) * 4096]) for i in range(5)]
        stg = _V(Z[:, 16896:20992].bitcast(F32))
        qT = _V(Z[:, 0:8448].rearrange("p (h n) -> p h n", h=16))
        oT = _V(Z[:, 8448:16896].rearrange("p (h n) -> p h n", h=16))
        st1 = kb.sb([128, 5, 8], F32, "st1")
        st2 = kb.sb([128, 5, 8], F32, "st2")
        stv = kb.sb([128, 5, 8], F32, "stv")
        wsT = [kb.sb([128, 128], BF16, f"wsT{g}") for g in range(4)]
        wsf = kb.sb([128, 128], F32, "wsf")
        rsb = [kb.sb([128, 128], F32, f"rsb{g}") for g in range(4)]
        bsb = [kb.sb([128, 128], F32, f"bsb{g}") for g in range(4)]
        rsx = kb.sb([128, NT], F32, "rsx")
        bsx = kb.sb([128, NT], F32, "bsx")
        lfS = kb.sb([128, 64, 16], F32, "lfS")
        lfs = [kb.sb([128, 16], F32, f"lfs{a}") for a in range(2)]
        wfb = kb.sb([128, 16, 16], BF16, "wfb")
        bfb = kb.sb([128, 16], F32, "bfb")
        kTs = [kb.sb([128, 16, 16], BF16, f"kTs{a}") for a in range(2)]
        vss = [kb.sb([128, 2048], BF16, f"vss{a}") for a in range(2)]
        vbt = kb.sb([128, 2, 56], F32, "vbt")
        PG = [kb.ps([128, 512], F32, f"PG{i}") for i in range(2)]
        PU = [kb.ps([128, 512], F32, f"PU{i}") for i in range(2)]
        PD = [kb.ps([128, 512], F32, f"PD{i}") for i in range(2)]
        PM = [kb.ps([128, 512], F32, f"PM{i}") for i in range(2)]
        PMb = [Tl(PM[i].ap.bitcast(BF16), f"PMb{i}") for i in range(2)]
        kS = kb.dram("kS", [16, 16, 128, 512], BF16)
        vS = kb.dram("vS", [16, 4, 128, 2048], BF16)

        op("pool", lambda g: g.memset(identf[:], 1.0), [], [identf])
        op("pool", lambda g: g.affine_select(out=identf[:], in_=identf[:], pattern=[[-1, 128]], compare_op=ALU.is_equal, fill=0.0, base=0, channel_multiplier=1), [identf], [identf])
        op("pool", lambda g: g.memset(UT[:], 1.0), [], [UT])
        op("pool", lambda g: g.affine_select(out=UT[:], in_=UT[:], pattern=[[1, 128]], compare_op=ALU.is_ge, fill=0.0, base=0, channel_multiplier=-1), [UT], [UT])
        op("pool", lambda g: g.memset(SL[:], 1.0), [], [SL])
        op("pool", lambda g: g.affine_select(out=SL[:], in_=SL[:], pattern=[[-1, 128]], compare_op=ALU.is_gt, fill=0.0, base=0, channel_multiplier=1), [SL], [SL])
        op("pool", lambda g: g.memset(TRIN[:], 0.0), [], [TRIN])
        op("pool", lambda g: g.affine_select(out=TRIN[:], in_=TRIN[:], pattern=[[1, 128]], compare_op=ALU.is_ge, fill=NEG, base=0, channel_multiplier=-1), [TRIN], [TRIN])
        op("pool", lambda g: g.memset(onesb[:], 1.0), [], [onesb])
        op("pool", lambda g: g.memset(onesf[:], 1.0), [], [onesf])
        op("dve", lambda g: g.tensor_copy(out=identb[:], in_=identf[:]), [identf], [identb])
        dma("sp", prm, None, prm[:], prm_d[:, :])
        dma("sp", bfb, None, bfb[:], bf_d.partition_broadcast(128))
        dma("sp", vbt, None, vbt[:], vb_d.partition_broadcast(128))
        dma("pool", wfb, None, wfb[:], wf_d.rearrange("(kc p) f -> p kc f", p=128))

        cnt = {"w": 0, "wd": 0, "m": 0, "d": 0, "a": 0, "t": 0}

        def pcol(base, i):
            return prm[:, base + i:base + i + 1]

        def load_x(tile):
            for ci, (c0, c1) in enumerate(TC):
                M = c1 - c0
                r0 = tile * NT + c0
                dma("sp", Z, None, stg[0:M, :], xin[r0:r0 + M, :])
                for q4 in range(4):
                    pm = PM[cnt["m"] % 2]; cnt["m"] += 1
                    for j in range(4):
                        kc = q4 * 4 + j
                        op("pe", lambda g, pm=pm, j=j, kc=kc, M=M: g.transpose(out=pm[:, j * 128:j * 128 + M], in_=stg[0:M, kc * 128:(kc + 1) * 128], identity=identf[0:M, 0:M]),
                           [Z, identf], [pm], inc=(j == 3))
                    for j in range(4):
                        kc = q4 * 4 + j
                        eng = "act" if j % 2 else "dve"
                        if eng == "act":
                            op("act", lambda g, pm=pm, j=j, kc=kc, M=M, c0=c0, c1=c1: g.activation(out=xT[:, kc, c0:c1], in_=pm[:, j * 128:j * 128 + M], func=AF.Copy), [pm], xd(kc))
                        else:
                            op("dve", lambda g, pm=pm, j=j, kc=kc, M=M, c0=c0, c1=c1: g.tensor_copy(out=xT[:, kc, c0:c1], in_=pm[:, j * 128:j * 128 + M]), [pm], xd(kc))

        def rstd_compute():
            for kc in range(16):
                s = sqt[kc % 2]
                op("act", lambda g, s=s, kc=kc: g.activation(out=s[:], in_=xT[:, kc, :], func=AF.Square), xd(kc), [s])
                for ti, (a, b) in enumerate(TG):
                    op("pe", lambda g, s=s, ti=ti, a=a, b=b, kc=kc: g.matmul(PM[ti][:, 0:264], lhsT=onesb[:], rhs=s[:, a:b], start=(kc == 0), stop=(kc == 15)),
                       [s, onesb], [PM[ti]], inc=(kc == 15 or ti == 1))
            for ti, (a, b) in enumerate(TG):
                op("act", lambda g, ti=ti, a=a, b=b: g.activation(out=rstd[:, a:b], in_=PM[ti][:, 0:264], func=AF.Sqrt, bias=1e-6, scale=1.0 / D), [PM[ti]], [rstd])
            op("dve", lambda g: g.reciprocal(out=rstd[:], in_=rstd[:]), [rstd], [rstd])

        def rmsnorm(base):
            rstd_compute()
            for kc in range(16):
                op("dve", lambda g, kc=kc: g.scalar_tensor_tensor(out=hT[:, kc, :], in0=xT[:, kc, :], scalar=pcol(base, kc), in1=rstd[:], op0=ALU.mult, op1=ALU.mult),
                   xd(kc) + [prm, rstd], [hT])

        def proj_chunk(wt, wap_fn, src=None):
            src = src or hT
            for kc in range(16):
                for ti, (a, b) in enumerate(TG):
                    op("pe", lambda g, kc=kc, ti=ti, a=a, b=b: g.matmul(PG[ti][:, 0:264], lhsT=wap_fn(kc), rhs=src[:, kc, a:b], start=(kc == 0), stop=(kc == 15)),
                       [wt, src], [PG[ti]], inc=(kc == 15))

        PDP = [PD[0], PD[1], PM[0], PM[1]]

        def down_units(wt, wap_fn, srct, nk, rhs_fn, scale):
            units = []
            for dc in range(16):
                for ti, (a, b) in enumerate(TG):
                    def emit(dc=dc, ti=ti, a=a, b=b):
                        pd = PDP[cnt["d"] % 4]; cnt["d"] += 1
                        for j in range(nk):
                            op("pe", lambda g, pd=pd, j=j: g.matmul(pd[:, 0:264], lhsT=wap_fn(j, dc), rhs=rhs_fn(j, a, b), start=(j == 0), stop=(j == nk - 1)),
                               [wt, srct], [pd], inc=(j == nk - 1))
                        op("dve", lambda g, pd=pd: g.scalar_tensor_tensor(out=xT[:, dc, a:b], in0=pd[:, 0:264], scalar=scale, in1=xT[:, dc, a:b], op0=ALU.mult, op1=ALU.add),
                           [pd, xTt[dc][ti]], [xTt[dc][ti]])
                    units.append(emit)
            return units

        def acc_down(wt, wap_fn, srct, nk, rhs_fn, scale):
            for u in down_units(wt, wap_fn, srct, nk, rhs_fn, scale):
                u()

        def drain(pending, n):
            for _ in range(min(n, len(pending))):
                pending.pop(0)()

        def ffn(l, which, base):
            rmsnorm(base + l * 16)
            wgl = wg[which][l].rearrange("(kc p) f -> p kc f", p=128)
            wul = wu[which][l].rearrange("(kc p) f -> p kc f", p=128)
            wdl = wd[which][l]
            NG = 22

            def load_gu(grp):
                wt = WGU[cnt["w"] % 2]; cnt["w"] += 1
                dma("pool", wt, None, wt[:, 0:4096].rearrange("p (kc f) -> p kc f", kc=16), wgl[:, :, grp * 256:(grp + 1) * 256])
                dma("pool", wt, None, wt[:, 4096:8192].rearrange("p (kc f) -> p kc f", kc=16), wul[:, :, grp * 256:(grp + 1) * 256])
                return wt

            def load_d(grp):
                wt = WDn[cnt["wd"] % 2]; cnt["wd"] += 1
                dma("pool", wt, None, wt[:, 0:4096].rearrange("p (j d) -> p j d", j=2), wdl[grp * 256:(grp + 1) * 256, :].rearrange("(j p) d -> p j d", p=128))
                return wt

            def gate_up(wt, ab, pending):
                for j in range(2):
                    proj_chunk(wt, lambda kc, j=j, wt=wt: wt[:, kc * 256 + j * 128: kc * 256 + (j + 1) * 128])
                    for ti, (a, b) in enumerate(TG):
                        op("act", lambda g, ti=ti: g.activation(out=tmpA[ti][:, 0:264], in_=PG[ti][:, 0:264], func=AF.Silu), [PG[ti]], [tmpA[ti]])
                    drain(pending, 8)
                    for kc in range(16):
                        for ti, (a, b) in enumerate(TG):
                            op("pe", lambda g, kc=kc, ti=ti, a=a, b=b, j=j, wt=wt: g.matmul(PU[ti][:, 0:264], lhsT=wt[:, 4096 + kc * 256 + j * 128: 4096 + kc * 256 + (j + 1) * 128], rhs=hT[:, kc, a:b], start=(kc == 0), stop=(kc == 15)),
                               [wt, hT], [PU[ti]], inc=(kc == 15))
                    for ti, (a, b) in enumerate(TG):
                        op("dve", lambda g, ti=ti, a=a, b=b, j=j, ab=ab: g.tensor_tensor(out=ab[:, j, a:b], in0=tmpA[ti][:, 0:264], in1=PU[ti][:, 0:264], op=ALU.mult),
                           [tmpA[ti], PU[ti]], [ab])
                    drain(pending, 8)

            def down(wt, ab):
                return down_units(wt, lambda j, dc, wt=wt: wt[:, j * 2048 + dc * 128: j * 2048 + (dc + 1) * 128], ab, 2, lambda j, a, b, ab=ab: ab[:, j, a:b], 0.5)

            gu = {0: load_gu(0)}
            dd = {0: load_d(0)}
            pending = []
            for grp in range(NG):
                if grp + 1 < NG:
                    gu[grp + 1] = load_gu(grp + 1)
                ab = actb[cnt["a"] % 2]; cnt["a"] += 1
                gate_up(gu[grp], ab, pending)
                drain(pending, 1000)
                if grp + 1 < NG:
                    dd[grp + 1] = load_d(grp + 1)
                pending = down(dd[grp], ab)
            drain(pending, 1000)

        def gmlp(l, own_a):
            rmsnorm(P_MX + l * 16)
            for g4 in range(4):
                dma("sp", wsf, None, wsf[:], wsT_d[l, g4])
                op("dve", lambda g, g4=g4: g.tensor_tensor(out=wsT[g4][:], in0=wsf[:], in1=UT[:], op=ALU.mult), [wsf, UT], [wsT[g4]])
                pm = PM[cnt["m"] % 2]; cnt["m"] += 1
                op("pe", lambda g, pm=pm, g4=g4: g.matmul(pm[:, 0:128], lhsT=onesb[:], rhs=wsT[g4][:], start=True, stop=True), [onesb, wsT[g4]], [pm])
                op("act", lambda g, pm=pm, g4=g4: g.activation(out=rsb[g4][:], in_=pm[:, 0:128], func=AF.Copy), [pm], [rsb[g4]])
                dma("sp", bsb[g4], None, bsb[g4][:], bs_d[l, g4].partition_broadcast(128))
            winl = w_in[l].rearrange("(kc p) f -> p kc f", p=128)
            op("dve", lambda g: g.memset(st1[:], 0.0), [], [st1])
            op("dve", lambda g: g.memset(st2[:], 0.0), [], [st2])
            for cg in range(8):
                wt = WGU[cnt["w"] % 2]; cnt["w"] += 1
                dma("pool", wt, None, wt[:, 0:8192].rearrange("p (kc f) -> p kc f", kc=16), winl[:, :, 4096 + cg * 512: 4096 + (cg + 1) * 512])
                for ci, (c0, c1) in enumerate(TC):
                    M = c1 - c0
                    if ci == 4 and own_a is None:
                        continue
                    pm = PM[cnt["m"] % 2]; cnt["m"] += 1
                    for kc in range(16):
                        op("pe", lambda g, pm=pm, kc=kc, c0=c0, c1=c1, M=M, wt=wt: g.matmul(pm[0:M, :], lhsT=hT[:, kc, c0:c1], rhs=wt[:, kc * 512:(kc + 1) * 512], start=(kc == 0), stop=(kc == 15)),
                           [hT, wt], [pm], inc=(kc == 15))
                    op("act", lambda g, pm=pm, ci=ci, cg=cg, M=M: g.activation(out=zt[ci][0:M, cg * 512:(cg + 1) * 512], in_=pm[0:M, :], func=AF.Gelu, accum_out=st1[0:M, ci, cg:cg + 1]),
                       [pm], [Z, st1])
                    jt = tmpB[cnt["t"] % 2]; cnt["t"] += 1
                    op("dve", lambda g, jt=jt, ci=ci, cg=cg, M=M: g.scalar_tensor_tensor(out=jt[0:M, 0:512], in0=zt[ci][0:M, cg * 512:(cg + 1) * 512], scalar=1.0, in1=zt[ci][0:M, cg * 512:(cg + 1) * 512], op0=ALU.mult, op1=ALU.mult, accum_out=st2[0:M, ci, cg:cg + 1]),
                       [Z], [jt, st2])
            for ci, (c0, c1) in enumerate(TC):
                M = c1 - c0
                if ci == 4 and own_a is None:
                    continue
                op("dve", lambda g, ci=ci, M=M: g.reduce_sum(out=stv[0:M, ci, 0:1], in_=st1[0:M, ci, :], axis=AX.X), [st1], [stv])
                op("dve", lambda g, ci=ci, M=M: g.reduce_sum(out=stv[0:M, ci, 1:2], in_=st2[0:M, ci, :], axis=AX.X), [st2, stv], [stv])
                op("dve", lambda g, ci=ci, M=M: g.tensor_scalar(out=stv[0:M, ci, 0:2], in0=stv[0:M, ci, 0:2], scalar1=1.0 / 4096, scalar2=None, op0=ALU.mult), [stv], [stv])
                op("dve", lambda g, ci=ci, M=M: g.tensor_tensor(out=stv[0:M, ci, 2:3], in0=stv[0:M, ci, 0:1], in1=stv[0:M, ci, 0:1], op=ALU.mult), [stv], [stv])
                op("dve", lambda g, ci=ci, M=M: g.tensor_tensor(out=stv[0:M, ci, 2:3], in0=stv[0:M, ci, 1:2], in1=stv[0:M, ci, 2:3], op=ALU.subtract), [stv], [stv])
                op("act", lambda g, ci=ci, M=M: g.activation(out=stv[0:M, ci, 2:3], in_=stv[0:M, ci, 2:3], func=AF.Sqrt, bias=1e-5, scale=1.0), [stv], [stv])
                op("dve", lambda g, ci=ci, M=M: g.reciprocal(out=stv[0:M, ci, 2:3], in_=stv[0:M, ci, 2:3]), [stv], [stv])
                op("dve", lambda g, ci=ci, M=M: g.scalar_tensor_tensor(out=stv[0:M, ci, 3:4], in0=stv[0:M, ci, 0:1], scalar=-1.0, in1=stv[0:M, ci, 2:3], op0=ALU.mult, op1=ALU.mult), [stv], [stv])
                op("act", lambda g, ci=ci, M=M: g.activation(out=zt[ci][0:M, :], in_=zt[ci][0:M, :], func=AF.Identity, bias=stv[0:M, ci, 3:4], scale=stv[0:M, ci, 2:3]), [Z, stv], [Z])
            if own_a is not None:
                for pc in range(8):
                    ga = tmpA[0]; ba = tmpA[1]
                    dma("sp", ga, None, ga[0:16, 0:512], lng_d[l, pc * 512:(pc + 1) * 512].partition_broadcast(16))
                    dma("sp", ba, None, ba[0:16, 0:512], lnb_d[l, pc * 512:(pc + 1) * 512].partition_broadcast(16))
                    op("dve", lambda g, pc=pc, ga=ga: g.tensor_tensor(out=ga[0:16, 0:512], in0=zt[4][0:16, pc * 512:(pc + 1) * 512], in1=ga[0:16, 0:512], op=ALU.mult), [Z, ga], [ga])
                    op("dve", lambda g, ga=ga, ba=ba: g.tensor_tensor(out=ga[0:16, 0:512], in0=ga[0:16, 0:512], in1=ba[0:16, 0:512], op=ALU.add), [ga, ba], [ga])
                    dma("sp", None, ga, gv[l, own_a * 16:(own_a + 1) * 16, pc * 512:(pc + 1) * 512], ga[0:16, 0:512])
            woutl = w_out[l]

            def load_u(cgrp):
                wt = WGU[cnt["w"] % 2]; cnt["w"] += 1
                dma("pool", wt, None, wt[:, 0:4096].rearrange("p (kc f) -> p kc f", kc=16), winl[:, :, cgrp * 256:(cgrp + 1) * 256])
                return wt

            def load_o(cgrp):
                wt = WDn[cnt["wd"] % 2]; cnt["wd"] += 1
                dma("pool", wt, None, wt[:, 0:4096].rearrange("p (j d) -> p j d", j=2), woutl[cgrp * 256:(cgrp + 1) * 256, :].rearrange("(j p) d -> p j d", p=128))
                return wt

            def umix(wt, ab, cgrp, pending):
                for j in range(2):
                    drain(pending, 16)
                    cc = cgrp * 2 + j
                    g4 = cc // 8
                    proj_chunk(wt, lambda kc, j=j, wt=wt: wt[:, kc * 256 + j * 128: kc * 256 + (j + 1) * 128])
                    uT = tmpB[cnt["t"] % 2]; cnt["t"] += 1
                    for ti, (a, b) in enumerate(TG):
                        op("act", lambda g, uT=uT, ti=ti, a=a, b=b: g.activation(out=uT[:, a:b], in_=PG[ti][:, 0:264], func=AF.Gelu), [PG[ti]], [uT])
                    for ci in range(4):
                        op("pe", lambda g, ci=ci, cc=cc, g4=g4: g.matmul(PU[0][:, ci * 128:(ci + 1) * 128], lhsT=zt[ci][:, cc * 128:(cc + 1) * 128], rhs=wsT[g4][:], start=True, stop=True),
                           [Z, wsT[g4]], [PU[0]])
                    if own_a is not None:
                        op("pe", lambda g, cc=cc, g4=g4: g.matmul(PU[1][:, 0:16], lhsT=zt[4][0:16, cc * 128:(cc + 1) * 128], rhs=wsT[g4][0:16, 0:16], start=True, stop=True),
                           [Z, wsT[g4]], [PU[1]])
                    E = tmpA[0]
                    if cc % 8 == 0:
                        for (c0, c1) in TC:
                            op("act", lambda g, g4=g4, c0=c0, c1=c1: g.activation(out=rsx[:, c0:c1], in_=rsb[g4][:, 0:c1 - c0], func=AF.Copy), [rsb[g4]], [rsx])
                            op("act", lambda g, g4=g4, c0=c0, c1=c1: g.activation(out=bsx[:, c0:c1], in_=bsb[g4][:, 0:c1 - c0], func=AF.Copy), [bsb[g4]], [bsx])
                    op("dve", lambda g, E=E, cc=cc: g.scalar_tensor_tensor(out=E[:], in0=rsx[:], scalar=pcol(P_LB + l * 32, cc), in1=bsx[:], op0=ALU.mult, op1=ALU.add),
                       [rsx, bsx, prm], [E])
                    op("dve", lambda g, E=E, cc=cc: g.scalar_tensor_tensor(out=E[:, 0:512], in0=PU[0][:, 0:512], scalar=pcol(P_LG + l * 32, cc), in1=E[:, 0:512], op0=ALU.mult, op1=ALU.add),
                       [PU[0], prm, E], [E])
                    if own_a is not None:
                        op("dve", lambda g, E=E, cc=cc: g.scalar_tensor_tensor(out=E[:, 512:528], in0=PU[1][:, 0:16], scalar=pcol(P_LG + l * 32, cc), in1=E[:, 512:528], op0=ALU.mult, op1=ALU.add),
                           [PU[1], prm, E], [E])
                    op("dve", lambda g, E=E, uT=uT, j=j, ab=ab: g.tensor_tensor(out=ab[:, j, :], in0=E[:], in1=uT[:], op=ALU.mult), [E, uT], [ab])

            def wout(wt, ab):
                return down_units(wt, lambda j, dc, wt=wt: wt[:, j * 2048 + dc * 128: j * 2048 + (dc + 1) * 128], ab, 2, lambda j, a, b, ab=ab: ab[:, j, a:b], 1.0)

            NG = 16
            gu = {0: load_u(0)}
            dd = {0: load_o(0)}
            pending = []
            for cgrp in range(NG):
                if cgrp + 1 < NG:
                    gu[cgrp + 1] = load_u(cgrp + 1)
                ab = actb[cnt["a"] % 2]; cnt["a"] += 1
                umix(gu[cgrp], ab, cgrp, pending)
                drain(pending, 1000)
                if cgrp + 1 < NG:
                    dd[cgrp + 1] = load_o(cgrp + 1)
                pending = wout(dd[cgrp], ab)
            drain(pending, 1000)

        def kv_phase(slot, own_a, after_norm=None):
            rmsnorm(P_KV)
            if after_norm is not None:
                after_norm()
            wkl = wk_d.rearrange("(kc p) f -> p kc f", p=128)
            wvl = wv_d.rearrange("(kc p) f -> p kc f", p=128)
            for c4 in range(4 if 'kt' in KVS else 0):
                wt = WGU[cnt["w"] % 2]; cnt["w"] += 1
                dma("pool", wt, None, wt[:, 0:8192].rearrange("p (kc f) -> p kc f", kc=16), wkl[:, :, c4 * 512:(c4 + 1) * 512])
                for j in range(4):
                    h = c4 * 4 + j
                    proj_chunk(wt, lambda kc, j=j, wt=wt: wt[:, kc * 512 + j * 128: kc * 512 + (j + 1) * 128])
                    kt = tmpB[cnt["t"] % 2]; cnt["t"] += 1
                    for ti, (a, b) in enumerate(TG):
                        op("act", lambda g, kt=kt, ti=ti, a=a, b=b: g.activation(out=kt[:, a:b], in_=PG[ti][:, 0:264], func=AF.Copy), [PG[ti]], [kt])
                    dma("sp", kS, kt, kS[slot, h], kt[:, 0:512])
                    if own_a is not None:
                        op("dve", lambda g, kt=kt, h=h: g.tensor_copy(out=kTs[own_a][:, h, :], in_=kt[:, 512:528]), [kt], [kTs[own_a]])
                if own_a is not None and 'ktok' in KVS:
                    for ci, (c0, c1) in enumerate(TC):
                        M = c1 - c0
                        pm = PM[cnt["m"] % 2]; cnt["m"] += 1
                        for kc in range(16):
                            op("pe", lambda g, pm=pm, kc=kc, c0=c0, c1=c1, M=M, wt=wt: g.matmul(pm[0:M, :], lhsT=hT[:, kc, c0:c1], rhs=wt[:, kc * 512:(kc + 1) * 512], start=(kc == 0), stop=(kc == 15)),
                               [hT, wt], [pm], inc=(kc == 15))
                        ot = tmpA[cnt["t"] % 2]; cnt["t"] += 1
                        op("act", lambda g, pm=pm, ot=ot, M=M: g.activation(out=ot[0:M, 0:512], in_=pm[0:M, :], func=AF.Copy), [pm], [ot])
                        if ci < 4:
                            dma("sp", None, ot, k_p[own_a * 512 + c0: own_a * 512 + c1, c4 * 512:(c4 + 1) * 512], ot[0:M, 0:512])
                        else:
                            dma("sp", None, ot, k_s[own_a * 16:(own_a + 1) * 16, c4 * 512:(c4 + 1) * 512], ot[0:M, 0:512])
            for c4 in range(4 if 'v' in KVS else 0):
                wt = WGU[cnt["w"] % 2]; cnt["w"] += 1
                dma("pool", wt, None, wt[:, 0:8192].rearrange("p (kc f) -> p kc f", kc=16), wvl[:, :, c4 * 512:(c4 + 1) * 512])
                for ci, (c0, c1) in enumerate(TC):
                    M = c1 - c0
                    if ci == 4 and own_a is None:
                        continue
                    pm = PM[cnt["m"] % 2]; cnt["m"] += 1
                    for kc in range(16):
                        op("pe", lambda g, pm=pm, kc=kc, c0=c0, c1=c1, M=M, wt=wt: g.matmul(pm[0:M, :], lhsT=hT[:, kc, c0:c1], rhs=wt[:, kc * 512:(kc + 1) * 512], start=(kc == 0), stop=(kc == 15)),
                           [hT, wt], [pm], inc=(kc == 15))
                    if ci < 4:
                        vt = tmpB[cnt["t"] % 2]; cnt["t"] += 1
                        op("act", lambda g, pm=pm, vt=vt, M=M: g.activation(out=vt[0:M, 0:512], in_=pm[0:M, :], func=AF.Copy), [pm], [vt])
                        dma("sp", vS, vt, vS[slot, ci, :, c4 * 512:(c4 + 1) * 512], vt[:, 0:512])
                    else:
                        op("act", lambda g, pm=pm, c4=c4: g.activation(out=vss[own_a][0:16, c4 * 512:(c4 + 1) * 512], in_=pm[0:16, :], func=AF.Copy), [pm], [vss[own_a]])
                    if own_a is not None:
                        ot = tmpA[cnt["t"] % 2]; cnt["t"] += 1
                        op("dve", lambda g, pm=pm, ot=ot, M=M: g.tensor_copy(out=ot[0:M, 0:512], in_=pm[0:M, :]), [pm], [ot])
                        if ci < 4:
                            dma("sp", None, ot, v_p[own_a * 512 + c0: own_a * 512 + c1, c4 * 512:(c4 + 1) * 512], ot[0:M, 0:512])
                        else:
                            dma("sp", None, ot, v_s[own_a * 16:(own_a + 1) * 16, c4 * 512:(c4 + 1) * 512], ot[0:M, 0:512])
            for ci, (c0, c1) in enumerate(TC if 'lf' in KVS else []):
                M = c1 - c0
                if ci == 4 and own_a is None:
                    continue
                pm = PM[cnt["m"] % 2]; cnt["m"] += 1
                for kc in range(16):
                    op("pe", lambda g, pm=pm, kc=kc, c0=c0, c1=c1, M=M: g.matmul(pm[0:M, 0:16], lhsT=hT[:, kc, c0:c1], rhs=wfb[:, kc, :], start=(kc == 0), stop=(kc == 15)),
                       [hT, wfb], [pm], inc=(kc == 15))
                ta = tmpA[cnt["t"] % 2]; cnt["t"] += 1
                dst = lfS[0:M, slot * 4 + ci, :] if ci < 4 else lfs[own_a][0:16, :]
                dstt = lfS if ci < 4 else lfs[own_a]
                op("dve", lambda g, pm=pm, ta=ta, M=M: g.tensor_tensor(out=ta[0:M, 0:16], in0=pm[0:M, 0:16], in1=bfb[0:M, :], op=ALU.add), [pm, bfb], [ta])
                op("act", lambda g, ta=ta, M=M: g.activation(out=ta[0:M, 16:32], in_=ta[0:M, 0:16], func=AF.Abs), [ta], [ta])
                op("act", lambda g, ta=ta, M=M: g.activation(out=ta[0:M, 16:32], in_=ta[0:M, 16:32], func=AF.Exp, scale=-1.0), [ta], [ta])
                op("act", lambda g, ta=ta, M=M: g.activation(out=ta[0:M, 16:32], in_=ta[0:M, 16:32], func=AF.Ln, bias=1.0, scale=1.0), [ta], [ta])
                op("dve", lambda g, ta=ta, M=M: g.tensor_scalar(out=ta[0:M, 0:16], in0=ta[0:M, 0:16], scalar1=0.0, scalar2=None, op0=ALU.min), [ta], [ta])
                op("dve", lambda g, ta=ta, M=M, dst=dst: g.tensor_tensor(out=dst, in0=ta[0:M, 0:16], in1=ta[0:M, 16:32], op=ALU.subtract), [ta], [dstt])
                if own_a is not None:
                    if ci < 4:
                        dma("sp", None, lfS, lf_p[own_a * 512 + c0: own_a * 512 + c1, :], dst)
                    else:
                        dma("sp", None, lfs[own_a], lf_s[own_a * 16:(own_a + 1) * 16, :], dst)


        Bk = kb.sb([128, 64, 16], F32, "Bk")
        Wq = kb.sb([128, 8, 16], F32, "Wq")
        Lm = _Zv(Z[:, 16896:18944].bitcast(F32).rearrange("p (b h) -> p b h", h=16))
        Rc = kb.sb([128, 17, 16], F32, "Rc")
        Lc = kb.sb([128, 17, 16], F32, "Lc")
        cqb = kb.sb([128, NT], F32, "cqb")
        rhc = kb.sb([128, 512], F32, "rhc")
        Pt = [kb.sb([128, 512], BF16, f"Pt{i}") for i in range(2)]
        kcs = kb.sb([128, 2048], BF16, "kcs")
        kcT = kb.sb([128, 2048], BF16, "kcT")
        vcs = kb.sb([128, 2048], BF16, "vcs")
        rec = rstd

        def suffix_excl(L, R, nb):
            n = nb * 16
            Lf = L[:, 0:nb, :].rearrange("p b h -> p (b h)"); Rf = R[:, 0:nb, :].rearrange("p b h -> p (b h)")
            for c0 in range(0, n, 512):
                c1 = min(n, c0 + 512)
                op("pe", lambda g, c0=c0, c1=c1: g.matmul(PM[0][:, 0:c1 - c0], lhsT=SL[:], rhs=Lf[:, c0:c1], start=True, stop=True), [SL, L], [PM[0]])
                op("pe", lambda g, c0=c0, c1=c1: g.matmul(PM[1][:, 0:c1 - c0], lhsT=onesf[:], rhs=Lf[:, c0:c1], start=True, stop=True), [onesf, L], [PM[1]])
                op("dve", lambda g, c0=c0, c1=c1: g.tensor_copy(out=Rf[:, c0:c1], in_=PM[0][:, 0:c1 - c0]), [PM[0]], [R])
                op("act", lambda g, c0=c0, c1=c1: g.activation(out=Lf[:, c0:c1], in_=PM[1][:, 0:c1 - c0], func=AF.Copy), [PM[1]], [L])
            op("dve", lambda g: g.memset(rhc[:, 0:16], 0.0), [], [rhc])
            for b in range(nb - 2, -1, -1):
                op("dve", lambda g, b=b: g.tensor_tensor(out=rhc[:, 0:16], in0=rhc[:, 0:16], in1=L[:, b + 1, :], op=ALU.add), [rhc, L], [rhc])
                op("dve", lambda g, b=b: g.tensor_tensor(out=R[:, b, :], in0=R[:, b, :], in1=rhc[:, 0:16], op=ALU.add), [R, rhc], [R])

        def prefix_incl(L, W, nb, rows=128):
            n = nb * 16
            Lf = L[0:rows, 0:nb, :].rearrange("p b h -> p (b h)"); Wf = W[0:rows, 0:nb, :].rearrange("p b h -> p (b h)")
            op("pe", lambda g: g.matmul(PM[0][0:rows, 0:n], lhsT=UT[0:rows, 0:rows], rhs=Lf, start=True, stop=True), [UT, L], [PM[0]])
            op("pe", lambda g: g.matmul(PM[1][0:rows, 0:n], lhsT=onesf[0:rows, 0:rows], rhs=Lf, start=True, stop=True), [onesf, L], [PM[1]])
            op("dve", lambda g: g.tensor_copy(out=Wf, in_=PM[0][0:rows, 0:n]), [PM[0]], [W])
            op("act", lambda g: g.activation(out=Lf, in_=PM[1][0:rows, 0:n], func=AF.Copy), [PM[1]], [L])
            op("dve", lambda g: g.memset(rhc[:, 0:16], 0.0), [], [rhc])
            for b in range(1, nb):
                op("dve", lambda g, b=b: g.tensor_tensor(out=rhc[0:rows, 0:16], in0=rhc[0:rows, 0:16], in1=L[0:rows, b - 1, :], op=ALU.add), [rhc, L], [rhc])
                op("dve", lambda g, b=b: g.tensor_tensor(out=W[0:rows, b, :], in0=W[0:rows, b, :], in1=rhc[0:rows, 0:16], op=ALU.add), [W, rhc], [W])

        def attn_prep(a):
            for b in range(56 if CTX else 0):
                op("dve", lambda g, b=b: g.tensor_scalar(out=Lm[:, b, :], in0=lfS[:, b, :], scalar1=vbt[:, 0, b:b + 1], scalar2=None, op0=ALU.mult), [lfS, vbt], [Lm])
            if CTX:
                suffix_excl(Lm, Bk, 56)
            for b in range(56 if CTX else 0):
                op("dve", lambda g, b=b: g.tensor_scalar(out=Bk[:, b, :], in0=Bk[:, b, :], scalar1=vbt[:, 1, b:b + 1], scalar2=None, op0=ALU.add), [Bk, vbt], [Bk])
            nbo = 4 * (a + 1)
            op("dve", lambda g: g.tensor_copy(out=Lm[:, 0:nbo, :], in_=lfS[:, 56:56 + nbo, :]), [lfS], [Lm])
            prefix_incl(Lm, Wq, nbo)
            op("dve", lambda g: g.tensor_scalar(out=Bk[:, 56:56 + nbo, :], in0=Wq[:, 0:nbo, :], scalar1=-1.0, scalar2=None, op0=ALU.mult), [Wq], [Bk])
            dma("sp", Lc, None, Lc[:, 0:16, :], cl_d[a].rearrange("(b t) h -> t b h", t=128))
            suffix_excl(Lc, Rc, 16)
            op("dve", lambda g: g.tensor_copy(out=Lc[0:16, 16:17, :], in_=lfs[a][0:16, :].rearrange("p (o h) -> p o h", o=1)), [lfs[a]], [Lc])
            prefix_incl(_Sl(Lc, 16), _Sl(Rc, 16), 1, rows=16)

        class _Sl:
            def __init__(self, t, b0):
                self.t = t; self.b0 = b0
            def __getitem__(self, k):
                p, b, h = k
                b = slice(b.start + self.b0, b.stop + self.b0) if isinstance(b, slice) else b + self.b0
                return self.t.ap[p, b, h]
            lw = property(lambda s: s.t.lw, lambda s, v: setattr(s.t, "lw", v))
            rd = property(lambda s: s.t.rd, lambda s, v: setattr(s.t, "rd", v))

        def bcast_rows(dst_ap, dst_t, src_col_fn, nblk, rows, width):
            for blk in range(nblk):
                op("dve", lambda g, blk=blk: g.tensor_scalar(out=rhc[0:rows, blk * width:(blk + 1) * width], in0=identf[0:rows, 0:width], scalar1=src_col_fn(blk), scalar2=None, op0=ALU.mult), [identf, Wq, Rc], [rhc])
            n = nblk * width
            op("pe", lambda g: g.matmul(PM[0][:, 0:n], lhsT=onesf[0:rows, :], rhs=rhc[0:rows, 0:n], start=True, stop=True), [onesf, rhc], [PM[0]])
            op("act", lambda g: g.activation(out=dst_ap, in_=PM[0][:, 0:n], func=AF.Copy), [PM[0]], [dst_t])

        def fox(j, a):
            l = 2 + j
            rmsnorm(P_MX + l * 16)
            wql = wq_d[j].rearrange("(kc p) f -> p kc f", p=128)
            for c4 in range(4):
                wt = WGU[cnt["w"] % 2]; cnt["w"] += 1
                dma("pool", wt, None, wt[:, 0:8192].rearrange("p (kc f) -> p kc f", kc=16), wql[:, :, c4 * 512:(c4 + 1) * 512])
                for jj in range(4):
                    h = c4 * 4 + jj
                    proj_chunk(wt, lambda kc, jj=jj, wt=wt: wt[:, kc * 512 + jj * 128: kc * 512 + (jj + 1) * 128])
                    for ti, (ca, cb) in enumerate(TG):
                        op("act", lambda g, h=h, ti=ti, ca=ca, cb=cb: g.activation(out=qT[:, h, ca:cb], in_=PG[ti][:, 0:264], func=AF.Copy), [PG[ti]], [Z])
            KH, VH = WGU[0], WGU[1]
            nslot = 15 + a
            for h in range(16):
                slo = 0 if CTX else 14
                dma("sp", KH, kS, KH[:, slo * 512:nslot * 512].rearrange("p (s n) -> p s n", s=nslot - slo), kS[slo:nslot, h].rearrange("s d n -> d s n"))
                for s_ in range(slo, nslot):
                    dma("sp", VH, vS, VH[:, s_ * 512:(s_ + 1) * 512].rearrange("p (b d) -> p b d", b=4), vS[s_, :, :, h * 128:(h + 1) * 128].rearrange("b t d -> t b d"))
                bcast_rows(cqb[:, 0:512], cqb, lambda blk, h=h: Wq[:, 4 * a + blk, h:h + 1], 4, 128, 128)
                blocks = ([(s_, b_) for s_ in range(14) for b_ in range(4)] if CTX else []) + [(14 + a2, b_) for a2 in range(a + 1) for b_ in range(4)]
                for bi, (s_, b_) in enumerate(blocks):
                    diag = (s_ == 14 + a)
                    q0 = b_ * 128 if diag else 0
                    nq = 512 - q0
                    ps = PG[bi % 2]
                    pt = Pt[bi % 2]
                    ta = tmpA[bi % 2]
                    gb = s_ * 4 + b_
                    op("pe", lambda g, ps=ps, s_=s_, b_=b_, q0=q0, nq=nq, h=h: g.matmul(ps[:, 0:nq], lhsT=KH[:, s_ * 512 + b_ * 128: s_ * 512 + (b_ + 1) * 128], rhs=qT[:, h, q0:512], start=True, stop=True), [KH, Z], [ps])
                    op("dve", lambda g, ps=ps, ta=ta, q0=q0, nq=nq: g.scalar_tensor_tensor(out=ta[:, 0:nq], in0=ps[:, 0:nq], scalar=SCALE, in1=cqb[:, q0:512], op0=ALU.mult, op1=ALU.add), [ps, cqb], [ta])
                    if diag:
                        op("dve", lambda g, ta=ta: g.tensor_tensor(out=ta[:, 0:128], in0=ta[:, 0:128], in1=TRIN[:], op=ALU.add), [ta, TRIN], [ta])
                    op("act", lambda g, ta=ta, pt=pt, nq=nq, gb=gb, h=h: g.activation(out=pt[:, 0:nq], in_=ta[:, 0:nq], func=AF.Exp, bias=Bk[:, gb, h:h + 1], scale=1.0), [ta, Bk], [pt])
                    first = (bi == 0); last = (bi == len(blocks) - 1)
                    op("pe", lambda g, pt=pt, s_=s_, b_=b_, q0=q0, nq=nq, first=first, last=last: g.matmul(PU[0][:, q0:512], lhsT=VH[:, s_ * 512 + b_ * 128: s_ * 512 + (b_ + 1) * 128], rhs=pt[:, 0:nq], start=first, stop=last), [VH, pt], [PU[0]], inc=False)
                    op("pe", lambda g, pt=pt, q0=q0, nq=nq, first=first, last=last: g.matmul(PU[1][:, q0:512], lhsT=onesb[:], rhs=pt[:, 0:nq], start=first, stop=last), [onesb, pt], [PU[1]])
                op("dve", lambda g: g.reciprocal(out=rec[:, 0:512], in_=PU[1][:]), [PU[1]], [rec])
                op("dve", lambda g, h=h: g.tensor_tensor(out=oT[:, h, 0:512], in0=PU[0][:], in1=rec[:, 0:512], op=ALU.mult), [PU[0], rec], [Z])
                dma("pool", kcs, None, kcs[:].rearrange("p (b d) -> p b d", b=16), ck_d[a, :, h * 128:(h + 1) * 128].rearrange("(b t) d -> t b d", t=128))
                dma("pool", vcs, None, vcs[:].rearrange("p (b d) -> p b d", b=16), cv_d[a, :, h * 128:(h + 1) * 128].rearrange("(b t) d -> t b d", t=128))
                for q4 in range(4):
                    pmb = PMb[q4 % 2]; pmf = PM[q4 % 2]
                    for jj in range(4):
                        b = q4 * 4 + jj
                        op("pe", lambda g, pmb=pmb, jj=jj, b=b: g.transpose(out=pmb[:, jj * 128:(jj + 1) * 128], in_=kcs[:, b * 128:(b + 1) * 128], identity=identb[:]), [kcs, identb], [pmf], inc=(jj == 3))
                    op("dve", lambda g, pmb=pmb, q4=q4: g.tensor_copy(out=kcT[:, q4 * 512:(q4 + 1) * 512], in_=pmb[:, 0:512]), [pmf], [kcT])
                bcast_rows(cqb[:, 512:528], cqb, lambda blk, h=h: Rc[0:16, 16, h:h + 1], 1, 16, 16)
                for b in range(16):
                    op("pe", lambda g, b=b, h=h: g.matmul(PD[0][:, b * 16:(b + 1) * 16], lhsT=kcT[:, b * 128:(b + 1) * 128], rhs=qT[:, h, 512:528], start=True, stop=True), [kcT, Z], [PD[0]], inc=False)
                op("pe", lambda g, h=h: g.matmul(PD[0][0:16, 256:272], lhsT=kTs[a][:, h, :], rhs=qT[:, h, 512:528], start=True, stop=True), [kTs[a], Z], [PD[0]])
                ta = tmpA[0]; pt = Pt[0]
                for b in range(17):
                    rows = 128 if b < 16 else 16
                    op("dve", lambda g, b=b, rows=rows: g.scalar_tensor_tensor(out=ta[0:rows, b * 16:(b + 1) * 16], in0=PD[0][0:rows, b * 16:(b + 1) * 16], scalar=SCALE, in1=cqb[0:rows, 512:528], op0=ALU.mult, op1=ALU.add), [PD[0], cqb], [ta])
                op("dve", lambda g: g.tensor_tensor(out=ta[0:16, 256:272], in0=ta[0:16, 256:272], in1=TRIN[0:16, 0:16], op=ALU.add), [ta, TRIN], [ta])
                op("dve", lambda g, h=h: g.tensor_scalar(out=rhc[0:16, 16:17], in0=Rc[0:16, 16, h:h + 1], scalar1=-1.0, scalar2=None, op0=ALU.mult), [Rc], [rhc])
                for b in range(17):
                    rows = 128 if b < 16 else 16
                    bias = Rc[:, b, h:h + 1] if b < 16 else rhc[0:16, 16:17]
                    op("act", lambda g, b=b, rows=rows, bias=bias: g.activation(out=pt[0:rows, b * 16:(b + 1) * 16], in_=ta[0:rows, b * 16:(b + 1) * 16], func=AF.Exp, bias=bias, scale=1.0), [ta, Rc, rhc], [pt])
                for b in range(17):
                    rows = 128 if b < 16 else 16
                    lv = vcs[:, b * 128:(b + 1) * 128] if b < 16 else vss[a][0:16, h * 128:(h + 1) * 128]
                    op("pe", lambda g, b=b, rows=rows, lv=lv: g.matmul(PD[1][:, 0:16], lhsT=lv, rhs=pt[0:rows, b * 16:(b + 1) * 16], start=(b == 0), stop=(b == 16)), [vcs, vss[a], pt], [PD[1]], inc=False)
                    op("pe", lambda g, b=b, rows=rows: g.matmul(PM[1][:, 16:32], lhsT=onesb[0:rows, :], rhs=pt[0:rows, b * 16:(b + 1) * 16], start=(b == 0), stop=(b == 16)), [onesb, pt], [PM[1]], inc=(b == 16))
                op("dve", lambda g: g.reciprocal(out=rec[:, 0:16], in_=PM[1][:, 16:32]), [PM[1]], [rec])
                op("dve", lambda g, h=h: g.tensor_tensor(out=oT[:, h, 512:528], in0=PD[1][:, 0:16], in1=rec[:, 0:16], op=ALU.mult), [PD[1], rec], [Z])
            wol = wo_d[j]
            for hg in range(8):
                wt = WDn[cnt["wd"] % 2]; cnt["wd"] += 1
                dma("pool", wt, None, wt[:, 0:4096].rearrange("p (jj d) -> p jj d", jj=2), wol[hg * 256:(hg + 1) * 256, :].rearrange("(jj p) d -> p jj d", p=128))
                acc_down(wt, lambda jj, dc, wt=wt: wt[:, jj * 2048 + dc * 128: jj * 2048 + (dc + 1) * 128], Z, 2, lambda jj, ca, cb, hg=hg: oT[:, hg * 2 + jj, ca:cb], 1.0)

        def final_out(a):
            rstd_compute()
            for ci, (c0, c1) in enumerate(TC):
                M = c1 - c0
                for q4 in range(4):
                    pm = PM[cnt["m"] % 2]; cnt["m"] += 1
                    for jj in range(4):
                        kc = q4 * 4 + jj
                        yt = tmpA[jj % 2]
                        op("dve", lambda g, kc=kc, yt=yt, c0=c0, c1=c1, M=M: g.scalar_tensor_tensor(out=yt[:, 0:M], in0=xT[:, kc, c0:c1], scalar=pcol(P_FN, kc), in1=rstd[:, c0:c1], op0=ALU.mult, op1=ALU.mult), xd(kc) + [prm, rstd], [yt])
                        op("pe", lambda g, pm=pm, jj=jj, yt=yt, M=M: g.transpose(out=pm[0:M, jj * 128:(jj + 1) * 128], in_=yt[:, 0:M], identity=identf[:]), [yt, identf], [pm])
                    op("act", lambda g, pm=pm, q4=q4, M=M: g.activation(out=stg[0:M, q4 * 512:(q4 + 1) * 512], in_=pm[0:M, :], func=AF.Copy), [pm], [Z])
                if ci < 4:
                    dma("sp", None, Z, y_p[a * 512 + c0: a * 512 + c1, :], stg[0:M, :])
                else:
                    dma("sp", None, Z, y_s[a * 16:(a + 1) * 16, :], stg[0:M, :])

        preloaded = set()
        for ti_, tile in enumerate(TILES):
            own_a = tile - 14 if tile >= 14 else None
            if tile not in preloaded:
                load_x(tile)
            nxt = TILES[ti_ + 1] if (ti_ + 1 < len(TILES) and own_a is None) else None
            for l in range(NLA):
                if "ffn1" in STG:
                    ffn(l, 0, P_F1)
                if "gmlp" in STG:
                    gmlp(l, own_a)
                if "ffn2" in STG:
                    ffn(l, 1, P_F2)
            if "kv" in STG:
                def _pre(nxt=nxt):
                    load_x(nxt); preloaded.add(nxt)
                kv_phase(tile, own_a, _pre if nxt is not None else None)
            if own_a is not None and BL:
                attn_prep(own_a)
                for j in range(2):
                    ffn(2 + j, 0, P_F1)
                    fox(j, own_a)
                    ffn(2 + j, 1, P_F2)
                final_out(own_a)
        kb.emit()
    return nc


TILES = list(range(16))
CTX = True
BL = True
NLA = 2
WL = 4
KVS = {'kt', 'ktok', 'v', 'lf'}
STG = {'ffn1', 'gmlp', 'ffn2', 'kv'}
_NC = None


def _prep_inputs(inp, c):
    f = np.float32
    xp = np.asarray(inp["x_prompt"], f)[0]
    xs = np.asarray(inp["x_sample"], f)
    own = [2 * c, 2 * c + 1]
    others = [t for t in range(16) if t not in own]
    tiles = others + own
    xin = np.empty((16, NT, D), f)
    for i, t in enumerate(tiles):
        xin[i, :512] = xp[t * 512:(t + 1) * 512]
        xin[i, 512:] = xs[2 * c + (i - 14 if i >= 14 else 0)]
    def pl(arr):
        arr = np.asarray(arr, f)
        L = arr.shape[0]
        return arr.reshape(L, -1, 128).transpose(2, 0, 1).reshape(128, -1)
    prm = np.concatenate([pl(inp["ffn1_norm"]), pl(inp["mix_norm"]), pl(inp["ffn2_norm"]), pl(np.asarray(inp["kv_norm"])[None]),
                          pl(np.asarray(inp["final_norm"])[None]), pl(inp["gmlp_ln_g"]), pl(inp["gmlp_ln_b"])], axis=1)
    assert prm.shape == (128, NPRM)
    vb = np.zeros((2, 56), f)
    for i, t in enumerate(others):
        ok = t < 2 * c
        vb[0, i * 4:(i + 1) * 4] = 1.0 if ok else 0.0
        vb[1, i * 4:(i + 1) * 4] = 0.0 if ok else NEG
    m = {"xin": xin.reshape(16 * NT, D), "prm": np.ascontiguousarray(prm), "vb": vb,
         "wsT": np.ascontiguousarray(np.asarray(inp["gmlp_w_s"], f).transpose(0, 1, 3, 2)),
         "cache_k": np.asarray(inp["cache_k"], f)[2 * c:2 * c + 2].reshape(2, 2048, D),
         "cache_v": np.asarray(inp["cache_v"], f)[2 * c:2 * c + 2].reshape(2, 2048, D),
         "cache_logf": np.asarray(inp["cache_logf"], f)[2 * c:2 * c + 2]}
    for k in ("ffn1_w_gate", "ffn2_w_gate", "ffn1_w_up", "ffn2_w_up", "ffn1_w_down", "ffn2_w_down", "gmlp_w_in", "gmlp_w_out",
              "gmlp_b_s", "gmlp_ln_g", "gmlp_ln_b", "w_k", "w_v", "w_f", "b_f", "fox_w_q", "fox_w_o"):
        m[k] = np.asarray(inp[k], f)[:WL] if k.startswith('ffn') else np.asarray(inp[k], f)
    return m


def kernel(**inp):
    global _NC
    if _NC is None:
        _NC = build_program()
    cores = list(range(8))
    in_maps = [_prep_inputs(inp, c) for c in cores]
    res = run_bass_kernel_spmd(_NC, in_maps, core_ids=cores)
    R = res.results
    cat = lambda k: np.concatenate([R[c][k] for c in cores], axis=0)
    y_prompt = cat("y_p").reshape(1, 8192, D)
    y_sample = cat("y_s").reshape(16, 16, D)
    k_prompt = cat("k_p").reshape(1, 8192, 16, 128)
    v_prompt = cat("v_p").reshape(1, 8192, 16, 128)
    logf_prompt = cat("lf_p").reshape(1, 8192, 16)
    k_sample = cat("k_s").reshape(16, 16, 16, 128)
    v_sample = cat("v_s").reshape(16, 16, 16, 128)
    logf_sample = cat("lf_s").reshape(16, 16, 16)
    gvs = np.concatenate([R[c]["gv"] for c in cores], axis=1).reshape(2, 16, 16, 4096)
    return (y_prompt, y_sample, k_prompt, v_prompt, logf_prompt, k_sample, v_sample, logf_sample, gvs)
```

```python
import contextlib
import numpy as np
import concourse.bass as bass
import concourse.mybir as mybir
from concourse.bass_utils import run_bass_kernel_spmd

F32 = mybir.dt.float32
BF16 = mybir.dt.bfloat16
AF = mybir.ActivationFunctionType
ALU = mybir.AluOpType
AX = mybir.AxisListType

D = 2048
DFF = 5632
NT = 528
NEG = -1.0e30
TG = [(0, 264), (264, 528)]
TC = [(0, 128), (128, 256), (256, 384), (384, 512), (512, 528)]
SCALE = 128 ** -0.5
P_F1, P_MX, P_F2, P_KV, P_FN, P_LG, P_LB, NPRM = 0, 64, 128, 192, 208, 224, 288, 352


class Tl:
    __slots__ = ("ap", "lw", "rd", "name", "ps")

    def __init__(self, ap, name="", ps=False):
        self.ap = ap
        self.lw = None
        self.rd = {}
        self.name = name
        self.ps = ps

    def __getitem__(self, k):
        return self.ap[k]


class Eng:
    def __init__(self, name):
        self.name = name
        self.count = 0
        self.ops = []
        self.waited = {}
        self.pool = []
        self.pool_cum = []
        self.pool_i = 0


class KB:
    NPOOL = 16

    def __init__(self, nc, stack):
        self.nc = nc
        self.stack = stack
        self.E = {n: Eng(n) for n in ("pe", "act", "dve", "pool", "sp")}
        self.sems = {}
        for n in self.E:
            self.sems[n] = stack.enter_context(nc.semaphore("s_" + n))
        for q in ("sp", "pool"):
            e = self.E[q]
            for i in range(self.NPOOL):
                key = f"d_{q}{i}"
                self.sems[key] = stack.enter_context(nc.semaphore(key))
                e.pool.append(key)
                e.pool_cum.append(0)
        self.ntile = 0

    def sb(self, shape, dt, name=None):
        self.ntile += 1
        name = name or f"t{self.ntile}"
        h = self.stack.enter_context(self.nc.sbuf_tensor(name, list(shape), dt))
        return Tl(h, name)

    def ps(self, shape, dt, name=None):
        self.ntile += 1
        name = name or f"p{self.ntile}"
        h = self.stack.enter_context(self.nc.psum_tensor(name, list(shape), dt))
        return Tl(h, name, ps=True)

    def dram(self, name, shape, dt):
        h = self.nc.dram_tensor(name, list(shape), dt).ap()
        return Tl(h, name)

    def _need(self, e, waits, tok, same_ok):
        if tok is None:
            return
        k, v = tok
        if same_ok and k == e.name:
            return
        if e.waited.get(k, 0) >= v:
            return
        if waits.get(k, 0) < v:
            waits[k] = v

    def _deps(self, e, reads, writes):
        waits = {}
        for t in reads:
            self._need(e, waits, t.lw, e.name == "pe")
            if getattr(t, "ps", False):
                for k, v in t.rd.items():
                    self._need(e, waits, (k, v), True)
        for t in writes:
            self._need(e, waits, t.lw, True)
            for k, v in t.rd.items():
                self._need(e, waits, (k, v), True)
        for k, v in waits.items():
            e.waited[k] = v
        return waits

    def op(self, eng, fn, reads=(), writes=(), inc=True):
        e = self.E[eng]
        waits = self._deps(e, reads, writes)
        if inc:
            e.count += 1
            tok = (eng, e.count)
        else:
            tok = (eng, e.count + 1)
        for t in reads:
            if t.rd.get(tok[0], 0) < tok[1]:
                t.rd[tok[0]] = tok[1]
        for t in writes:
            t.lw = tok
            t.rd = {}
        e.ops.append((waits, fn, eng if inc else None, 1))
        return tok

    def dma(self, q, out_t, in_t, out_ap, in_ap):
        e = self.E[q]
        reads = [in_t] if in_t is not None else []
        writes = [out_t] if out_t is not None else []
        waits = self._deps(e, reads, writes)
        i = e.pool_i
        e.pool_i = (i + 1) % len(e.pool)
        key = e.pool[i]
        prev = e.pool_cum[i]
        if prev > 0 and e.waited.get(key, 0) < prev:
            waits[key] = max(waits.get(key, 0), prev)
            e.waited[key] = prev
        e.pool_cum[i] = prev + 16
        tok = (key, prev + 16)
        for t in reads:
            t.rd[key] = tok[1]
        for t in writes:
            t.lw = tok
            t.rd = {}
        e.ops.append((waits, lambda g: g.dma_start(out=out_ap, in_=in_ap), key, 16))
        return tok

    def emit(self):
        nc = self.nc
        fin = {}
        for q in ("sp", "pool"):
            e = self.E[q]
            for key, cum in zip(e.pool, e.pool_cum):
                if cum > 0:
                    fin[key] = cum
        for n in ("pe", "act", "dve", "pool"):
            if self.E[n].count > 0:
                fin[n] = self.E[n].count
        sems = self.sems
        E = self.E

        def run(e, g):
            for waits, fn, inck, incv in e.ops:
                for k, v in waits.items():
                    g.wait_ge(sems[k], v)
                ins = fn(g)
                if inck is not None:
                    ins.then_inc(sems[inck], incv)

        with nc.Block() as block:
            @block.tensor
            def _(g):
                run(E["pe"], g)

            @block.scalar
            def _(g):
                run(E["act"], g)

            @block.vector
            def _(g):
                run(E["dve"], g)

            @block.gpsimd
            def _(g):
                run(E["pool"], g)

            @block.sync
            def _(g):
                run(E["sp"], g)
                for k, v in fin.items():
                    g.wait_ge(sems[k], v)


def build_program():
    nc = bass.Bass("TRN2", target_bir_lowering=False)

    def din(name, shape, dt=F32):
        return nc.dram_tensor(name, list(shape), dt, kind="ExternalInput").ap()

    def dout(name, shape):
        return nc.dram_tensor(name, list(shape), F32, kind="ExternalOutput").ap()

    xin = din("xin", [16 * NT, D])
    prm_d = din("prm", [128, NPRM])
    vb_d = din("vb", [2, 56])
    wg = [din("ffn1_w_gate", [WL, D, DFF]), din("ffn2_w_gate", [WL, D, DFF])]
    wu = [din("ffn1_w_up", [WL, D, DFF]), din("ffn2_w_up", [WL, D, DFF])]
    wd = [din("ffn1_w_down", [WL, DFF, D]), din("ffn2_w_down", [WL, DFF, D])]
    w_in = din("gmlp_w_in", [2, D, 8192])
    w_out = din("gmlp_w_out", [2, 4096, D])
    wsT_d = din("wsT", [2, 4, 128, 128])
    bs_d = din("gmlp_b_s", [2, 4, 128])
    lng_d = din("gmlp_ln_g", [2, 4096])
    lnb_d = din("gmlp_ln_b", [2, 4096])
    wk_d = din("w_k", [D, D])
    wv_d = din("w_v", [D, D])
    wf_d = din("w_f", [D, 16])
    bf_d = din("b_f", [16])
    wq_d = din("fox_w_q", [2, D, D])
    wo_d = din("fox_w_o", [2, D, D])
    ck_d = din("cache_k", [2, 2048, D])
    cv_d = din("cache_v", [2, 2048, D])
    cl_d = din("cache_logf", [2, 2048, 16])

    y_p = dout("y_p", [1024, D]); y_s = dout("y_s", [32, D])
    k_p = dout("k_p", [1024, D]); v_p = dout("v_p", [1024, D]); lf_p = dout("lf_p", [1024, 16])
    k_s = dout("k_s", [32, D]); v_s = dout("v_s", [32, D]); lf_s = dout("lf_s", [32, 16])
    gv = dout("gv", [2, 32, 4096])

    with contextlib.ExitStack() as st:
        kb = KB(nc, st)
        op, dma = kb.op, kb.dma
        xT = kb.sb([128, 16, NT], F32, "xT")
        xTt = [[Tl(xT.ap, f"xT{dc}_{ti}") for ti in range(2)] for dc in range(16)]
        def xd(dc):
            return [xTt[dc][0], xTt[dc][1]]
        hT = kb.sb([128, 16, NT], BF16, "hT")
        actb = [kb.sb([128, 2, NT], BF16, f"act{i}") for i in range(2)]
        WGU = [kb.sb([128, 8192], BF16, f"WGU{i}") for i in range(2)]
        WDn = [kb.sb([128, 4096], BF16, f"WDn{i}") for i in range(2)]
        prm = kb.sb([128, NPRM], F32, "prm_sb")
        sqt = [kb.sb([128, NT], BF16, f"sqt{i}") for i in range(2)]
        rstd = kb.sb([128, NT], F32, "rstd")
        tmpA = [kb.sb([128, NT], F32, f"tmpA{i}") for i in range(2)]
        tmpB = [kb.sb([128, NT], BF16, f"tmpB{i}") for i in range(2)]
        identf = kb.sb([128, 128], F32, "identf")
        identb = kb.sb([128, 128], BF16, "identb")
        onesb = kb.sb([128, 128], BF16, "onesb")
        onesf = kb.sb([128, 128], F32, "onesf")
        UT = kb.sb([128, 128], F32, "UT")
        SL = kb.sb([128, 128], F32, "SL")
        TRIN = kb.sb([128, 128], F32, "TRIN")
        Z = kb.sb([128, 21504], BF16, "Z")
        class _V:
            def __init__(self, ap): self.ap = ap
            def __getitem__(self, k): return self.ap[k]
        class _Zv:
            def __init__(self, ap): self.ap = ap
            def __getitem__(self, k): return self.ap[k]
            lw = property(lambda s_: Z.lw, lambda s_, v: setattr(Z, "lw", v))
            rd = property(lambda s_: Z.rd, lambda s_, v: setattr(Z, "rd", v))
        zt = [_V(Z[:, i * 4096:(i + 1) * 4096]) for i in range(5)]
        stg = _V(Z[:, 16896:20992].bitcast(F32))
        qT = _V(Z[:, 0:8448].rearrange("p (h n) -> p h n", h=16))
        oT = _V(Z[:, 8448:16896].rearrange("p (h n) -> p h n", h=16))
        st1 = kb.sb([128, 5, 8], F32, "st1")
        st2 = kb.sb([128, 5, 8], F32, "st2")
        stv = kb.sb([128, 5, 8], F32, "stv")
        wsT = [kb.sb([128, 128], BF16, f"wsT{g}") for g in range(4)]
        wsf = kb.sb([128, 128], F32, "wsf")
        rsb = [kb.sb([128, 128], F32, f"rsb{g}") for g in range(4)]
        bsb = [kb.sb([128, 128], F32, f"bsb{g}") for g in range(4)]
        rsx = kb.sb([128, NT], F32, "rsx")
        bsx = kb.sb([128, NT], F32, "bsx")
        lfS = kb.sb([128, 64, 16], F32, "lfS")
        lfs = [kb.sb([128, 16], F32, f"lfs{a}") for a in range(2)]
        wfb = kb.sb([128, 16, 16], BF16, "wfb")
        bfb = kb.sb([128, 16], F32, "bfb")
        kTs = [kb.sb([128, 16, 16], BF16, f"kTs{a}") for a in range(2)]
        vss = [kb.sb([128, 2048], BF16, f"vss{a}") for a in range(2)]
        vbt = kb.sb([128, 2, 56], F32, "vbt")
        PG = [kb.ps([128, 512], F32, f"PG{i}") for i in range(2)]
        PU = [kb.ps([128, 512], F32, f"PU{i}") for i in range(2)]
        PD = [kb.ps([128, 512], F32, f"PD{i}") for i in range(2)]
        PM = [kb.ps([128, 512], F32, f"PM{i}") for i in range(2)]
        PMb = [Tl(PM[i].ap.bitcast(BF16), f"PMb{i}") for i in range(2)]
        kS = kb.dram("kS", [16, 16, 128, 512], BF16)
        vS = kb.dram("vS", [16, 4, 128, 2048], BF16)

        op("pool", lambda g: g.memset(identf[:], 1.0), [], [identf])
        op("pool", lambda g: g.affine_select(out=identf[:], in_=identf[:], pattern=[[-1, 128]], compare_op=ALU.is_equal, fill=0.0, base=0, channel_multiplier=1), [identf], [identf])
        op("pool", lambda g: g.memset(UT[:], 1.0), [], [UT])
        op("pool", lambda g: g.affine_select(out=UT[:], in_=UT[:], pattern=[[1, 128]], compare_op=ALU.is_ge, fill=0.0, base=0, channel_multiplier=-1), [UT], [UT])
        op("pool", lambda g: g.memset(SL[:], 1.0), [], [SL])
        op("pool", lambda g: g.affine_select(out=SL[:], in_=SL[:], pattern=[[-1, 128]], compare_op=ALU.is_gt, fill=0.0, base=0, channel_multiplier=1), [SL], [SL])
        op("pool", lambda g: g.memset(TRIN[:], 0.0), [], [TRIN])
        op("pool", lambda g: g.affine_select(out=TRIN[:], in_=TRIN[:], pattern=[[1, 128]], compare_op=ALU.is_ge, fill=NEG, base=0, channel_multiplier=-1), [TRIN], [TRIN])
        op("pool", lambda g: g.memset(onesb[:], 1.0), [], [onesb])
        op("pool", lambda g: g.memset(onesf[:], 1.0), [], [onesf])
        op("dve", lambda g: g.tensor_copy(out=identb[:], in_=identf[:]), [identf], [identb])
        dma("sp", prm, None, prm[:], prm_d[:, :])
        dma("sp", bfb, None, bfb[:], bf_d.partition_broadcast(128))
        dma("sp", vbt, None, vbt[:], vb_d.partition_broadcast(128))
        dma("pool", wfb, None, wfb[:], wf_d.rearrange("(kc p) f -> p kc f", p=128))

        cnt = {"w": 0, "wd": 0, "m": 0, "d": 0, "a": 0, "t": 0}

        def pcol(base, i):
            return prm[:, base + i:base + i + 1]

        def load_x(tile):
            for ci, (c0, c1) in enumerate(TC):
                M = c1 - c0
                r0 = tile * NT + c0
                dma("sp", Z, None, stg[0:M, :], xin[r0:r0 + M, :])
                for q4 in range(4):
                    pm = PM[cnt["m"] % 2]; cnt["m"] += 1
                    for j in range(4):
                        kc = q4 * 4 + j
                        op("pe", lambda g, pm=pm, j=j, kc=kc, M=M: g.transpose(out=pm[:, j * 128:j * 128 + M], in_=stg[0:M, kc * 128:(kc + 1) * 128], identity=identf[0:M, 0:M]),
                           [Z, identf], [pm], inc=(j == 3))
                    for j in range(4):
                        kc = q4 * 4 + j
                        eng = "act" if j % 2 else "dve"
                        if eng == "act":
                            op("act", lambda g, pm=pm, j=j, kc=kc, M=M, c0=c0, c1=c1: g.activation(out=xT[:, kc, c0:c1], in_=pm[:, j * 128:j * 128 + M], func=AF.Copy), [pm], xd(kc))
                        else:
                            op("dve", lambda g, pm=pm, j=j, kc=kc, M=M, c0=c0, c1=c1: g.tensor_copy(out=xT[:, kc, c0:c1], in_=pm[:, j * 128:j * 128 + M]), [pm], xd(kc))

        def rstd_compute():
            for kc in range(16):
                s = sqt[kc % 2]
                if kc % 2 == 0:
                    op("act", lambda g, s=s, kc=kc: g.activation(out=s[:], in_=xT[:, kc, :], func=AF.Square), xd(kc), [s])
                else:
                    op("dve", lambda g, s=s, kc=kc: g.tensor_tensor(out=s[:], in0=xT[:, kc, :], in1=xT[:, kc, :], op=ALU.mult), xd(kc), [s])
                for ti, (a, b) in enumerate(TG):
                    op("pe", lambda g, s=s, ti=ti, a=a, b=b, kc=kc: g.matmul(PM[ti][:, 0:264], lhsT=onesb[:], rhs=s[:, a:b], start=(kc == 0), stop=(kc == 15)),
                       [s, onesb], [PM[ti]], inc=(kc == 15 or ti == 1))
            for ti, (a, b) in enumerate(TG):
                op("act", lambda g, ti=ti, a=a, b=b: g.activation(out=rstd[:, a:b], in_=PM[ti][:, 0:264], func=AF.Sqrt, bias=1e-6, scale=1.0 / D), [PM[ti]], [rstd])
            op("dve", lambda g: g.reciprocal(out=rstd[:], in_=rstd[:]), [rstd], [rstd])

        def rmsnorm(base):
            rstd_compute()
            for kc in range(16):
                op("dve", lambda g, kc=kc: g.scalar_tensor_tensor(out=hT[:, kc, :], in0=xT[:, kc, :], scalar=pcol(base, kc), in1=rstd[:], op0=ALU.mult, op1=ALU.mult),
                   xd(kc) + [prm, rstd], [hT])

        def proj_chunk(wt, wap_fn, src=None):
            src = src or hT
            for kc in range(16):
                for ti, (a, b) in enumerate(TG):
                    op("pe", lambda g, kc=kc, ti=ti, a=a, b=b: g.matmul(PG[ti][:, 0:264], lhsT=wap_fn(kc), rhs=src[:, kc, a:b], start=(kc == 0), stop=(kc == 15)),
                       [wt, src], [PG[ti]], inc=(kc == 15))

        PDP = [PD[0], PD[1], PM[0], PM[1]]

        def down_units(wt, wap_fn, srct, nk, rhs_fn, scale):
            units = []
            for dc in range(16):
                for ti, (a, b) in enumerate(TG):
                    def emit(dc=dc, ti=ti, a=a, b=b):
                        pd = PDP[cnt["d"] % 4]; cnt["d"] += 1
                        for j in range(nk):
                            op("pe", lambda g, pd=pd, j=j: g.matmul(pd[:, 0:264], lhsT=wap_fn(j, dc), rhs=rhs_fn(j, a, b), start=(j == 0), stop=(j == nk - 1)),
                               [wt, srct], [pd], inc=(j == nk - 1))
                        op("dve", lambda g, pd=pd: g.scalar_tensor_tensor(out=xT[:, dc, a:b], in0=pd[:, 0:264], scalar=scale, in1=xT[:, dc, a:b], op0=ALU.mult, op1=ALU.add),
                           [pd, xTt[dc][ti]], [xTt[dc][ti]])
                    units.append(emit)
            return units

        def acc_down(wt, wap_fn, srct, nk, rhs_fn, scale):
            for u in down_units(wt, wap_fn, srct, nk, rhs_fn, scale):
                u()

        def drain(pending, n):
            for _ in range(min(n, len(pending))):
                pending.pop(0)()

        def ffn(l, which, base):
            rmsnorm(base + l * 16)
            wgl = wg[which][l].rearrange("(kc p) f -> p kc f", p=128)
            wul = wu[which][l].rearrange("(kc p) f -> p kc f", p=128)
            wdl = wd[which][l]
            NG = 22

            def load_gu(grp):
                wt = WGU[cnt["w"] % 2]; cnt["w"] += 1
                dma("pool", wt, None, wt[:, 0:4096].rearrange("p (kc f) -> p kc f", kc=16), wgl[:, :, grp * 256:(grp + 1) * 256])
                dma("pool", wt, None, wt[:, 4096:8192].rearrange("p (kc f) -> p kc f", kc=16), wul[:, :, grp * 256:(grp + 1) * 256])
                return wt

            def load_d(grp):
                wt = WDn[cnt["wd"] % 2]; cnt["wd"] += 1
                dma("pool", wt, None, wt[:, 0:4096].rearrange("p (j d) -> p j d", j=2), wdl[grp * 256:(grp + 1) * 256, :].rearrange("(j p) d -> p j d", p=128))
                return wt

            def gate_up(wt, ab, pending):
                for j in range(2):
                    proj_chunk(wt, lambda kc, j=j, wt=wt: wt[:, kc * 256 + j * 128: kc * 256 + (j + 1) * 128])
                    for ti, (a, b) in enumerate(TG):
                        op("act", lambda g, ti=ti: g.activation(out=tmpA[ti][:, 0:264], in_=PG[ti][:, 0:264], func=AF.Silu), [PG[ti]], [tmpA[ti]])
                    drain(pending, 8)
                    for kc in range(16):
                        for ti, (a, b) in enumerate(TG):
                            op("pe", lambda g, kc=kc, ti=ti, a=a, b=b, j=j, wt=wt: g.matmul(PU[ti][:, 0:264], lhsT=wt[:, 4096 + kc * 256 + j * 128: 4096 + kc * 256 + (j + 1) * 128], rhs=hT[:, kc, a:b], start=(kc == 0), stop=(kc == 15)),
                               [wt, hT], [PU[ti]], inc=(kc == 15))
                    for ti, (a, b) in enumerate(TG):
                        op("dve", lambda g, ti=ti, a=a, b=b, j=j, ab=ab: g.tensor_tensor(out=ab[:, j, a:b], in0=tmpA[ti][:, 0:264], in1=PU[ti][:, 0:264], op=ALU.mult),
                           [tmpA[ti], PU[ti]], [ab])
                    drain(pending, 8)

            def down(wt, ab):
                return down_units(wt, lambda j, dc, wt=wt: wt[:, j * 2048 + dc * 128: j * 2048 + (dc + 1) * 128], ab, 2, lambda j, a, b, ab=ab: ab[:, j, a:b], 0.5)

            gu = {0: load_gu(0)}
            dd = {0: load_d(0)}
            pending = []
            for grp in range(NG):
                if grp + 1 < NG:
                    gu[grp + 1] = load_gu(grp + 1)
                ab = actb[cnt["a"] % 2]; cnt["a"] += 1
                gate_up(gu[grp], ab, pending)
                drain(pending, 1000)
                if grp + 1 < NG:
                    dd[grp + 1] = load_d(grp + 1)
                pending = down(dd[grp], ab)
            drain(pending, 1000)

        def gmlp(l, own_a):
            rmsnorm(P_MX + l * 16)
            for g4 in range(4):
                dma("sp", wsf, None, wsf[:], wsT_d[l, g4])
                op("dve", lambda g, g4=g4: g.tensor_tensor(out=wsT[g4][:], in0=wsf[:], in1=UT[:], op=ALU.mult), [wsf, UT], [wsT[g4]])
                pm = PM[cnt["m"] % 2]; cnt["m"] += 1
                op("pe", lambda g, pm=pm, g4=g4: g.matmul(pm[:, 0:128], lhsT=onesb[:], rhs=wsT[g4][:], start=True, stop=True), [onesb, wsT[g4]], [pm])
                op("act", lambda g, pm=pm, g4=g4: g.activation(out=rsb[g4][:], in_=pm[:, 0:128], func=AF.Copy), [pm], [rsb[g4]])
                dma("sp", bsb[g4], None, bsb[g4][:], bs_d[l, g4].partition_broadcast(128))
            winl = w_in[l].rearrange("(kc p) f -> p kc f", p=128)
            op("dve", lambda g: g.memset(st1[:], 0.0), [], [st1])
            op("dve", lambda g: g.memset(st2[:], 0.0), [], [st2])
            for cg in range(8):
                wt = WGU[cnt["w"] % 2]; cnt["w"] += 1
                dma("pool", wt, None, wt[:, 0:8192].rearrange("p (kc f) -> p kc f", kc=16), winl[:, :, 4096 + cg * 512: 4096 + (cg + 1) * 512])
                for ci, (c0, c1) in enumerate(TC):
                    M = c1 - c0
                    if ci == 4 and own_a is None:
                        continue
                    pm = PM[cnt["m"] % 2]; cnt["m"] += 1
                    for kc in range(16):
                        op("pe", lambda g, pm=pm, kc=kc, c0=c0, c1=c1, M=M, wt=wt: g.matmul(pm[0:M, :], lhsT=hT[:, kc, c0:c1], rhs=wt[:, kc * 512:(kc + 1) * 512], start=(kc == 0), stop=(kc == 15)),
                           [hT, wt], [pm], inc=(kc == 15))
                    op("act", lambda g, pm=pm, ci=ci, cg=cg, M=M: g.activation(out=zt[ci][0:M, cg * 512:(cg + 1) * 512], in_=pm[0:M, :], func=AF.Gelu, accum_out=st1[0:M, ci, cg:cg + 1]),
                       [pm], [Z, st1])
                    jt = tmpB[cnt["t"] % 2]; cnt["t"] += 1
                    op("dve", lambda g, jt=jt, ci=ci, cg=cg, M=M: g.scalar_tensor_tensor(out=jt[0:M, 0:512], in0=zt[ci][0:M, cg * 512:(cg + 1) * 512], scalar=1.0, in1=zt[ci][0:M, cg * 512:(cg + 1) * 512], op0=ALU.mult, op1=ALU.mult, accum_out=st2[0:M, ci, cg:cg + 1]),
                       [Z], [jt, st2])
            for ci, (c0, c1) in enumerate(TC):
                M = c1 - c0
                if ci == 4 and own_a is None:
                    continue
                op("dve", lambda g, ci=ci, M=M: g.reduce_sum(out=stv[0:M, ci, 0:1], in_=st1[0:M, ci, :], axis=AX.X), [st1], [stv])
                op("dve", lambda g, ci=ci, M=M: g.reduce_sum(out=stv[0:M, ci, 1:2], in_=st2[0:M, ci, :], axis=AX.X), [st2, stv], [stv])
                op("dve", lambda g, ci=ci, M=M: g.tensor_scalar(out=stv[0:M, ci, 0:2], in0=stv[0:M, ci, 0:2], scalar1=1.0 / 4096, scalar2=None, op0=ALU.mult), [stv], [stv])
                op("dve", lambda g, ci=ci, M=M: g.tensor_tensor(out=stv[0:M, ci, 2:3], in0=stv[0:M, ci, 0:1], in1=stv[0:M, ci, 0:1], op=ALU.mult), [stv], [stv])
                op("dve", lambda g, ci=ci, M=M: g.tensor_tensor(out=stv[0:M, ci, 2:3], in0=stv[0:M, ci, 1:2], in1=stv[0:M, ci, 2:3], op=ALU.subtract), [stv], [stv])
                op("act", lambda g, ci=ci, M=M: g.activation(out=stv[0:M, ci, 2:3], in_=stv[0:M, ci, 2:3], func=AF.Sqrt, bias=1e-5, scale=1.0), [stv], [stv])
                op("dve", lambda g, ci=ci, M=M: g.reciprocal(out=stv[0:M, ci, 2:3], in_=stv[0:M, ci, 2:3]), [stv], [stv])
                op("dve", lambda g, ci=ci, M=M: g.scalar_tensor_tensor(out=stv[0:M, ci, 3:4], in0=stv[0:M, ci, 0:1], scalar=-1.0, in1=stv[0:M, ci, 2:3], op0=ALU.mult, op1=ALU.mult), [stv], [stv])
                op("act", lambda g, ci=ci, M=M: g.activation(out=zt[ci][0:M, :], in_=zt[ci][0:M, :], func=AF.Identity, bias=stv[0:M, ci, 3:4], scale=stv[0:M, ci, 2:3]), [Z, stv], [Z])
            if own_a is not None:
                for pc in range(8):
                    ga = tmpA[0]; ba = tmpA[1]
                    dma("sp", ga, None, ga[0:16, 0:512], lng_d[l, pc * 512:(pc + 1) * 512].partition_broadcast(16))
                    dma("sp", ba, None, ba[0:16, 0:512], lnb_d[l, pc * 512:(pc + 1) * 512].partition_broadcast(16))
                    op("dve", lambda g, pc=pc, ga=ga: g.tensor_tensor(out=ga[0:16, 0:512], in0=zt[4][0:16, pc * 512:(pc + 1) * 512], in1=ga[0:16, 0:512], op=ALU.mult), [Z, ga], [ga])
                    op("dve", lambda g, ga=ga, ba=ba: g.tensor_tensor(out=ga[0:16, 0:512], in0=ga[0:16, 0:512], in1=ba[0:16, 0:512], op=ALU.add), [ga, ba], [ga])
                    dma("sp", None, ga, gv[l, own_a * 16:(own_a + 1) * 16, pc * 512:(pc + 1) * 512], ga[0:16, 0:512])
            woutl = w_out[l]

            def load_u(cgrp):
                wt = WGU[cnt["w"] % 2]; cnt["w"] += 1
                dma("pool", wt, None, wt[:, 0:4096].rearrange("p (kc f) -> p kc f", kc=16), winl[:, :, cgrp * 256:(cgrp + 1) * 256])
                return wt

            def load_o(cgrp):
                wt = WDn[cnt["wd"] % 2]; cnt["wd"] += 1
                dma("pool", wt, None, wt[:, 0:4096].rearrange("p (j d) -> p j d", j=2), woutl[cgrp * 256:(cgrp + 1) * 256, :].rearrange("(j p) d -> p j d", p=128))
                return wt

            def umix(wt, ab, cgrp, pending):
                for j in range(2):
                    drain(pending, 16)
                    cc = cgrp * 2 + j
                    g4 = cc // 8
                    proj_chunk(wt, lambda kc, j=j, wt=wt: wt[:, kc * 256 + j * 128: kc * 256 + (j + 1) * 128])
                    uT = tmpB[cnt["t"] % 2]; cnt["t"] += 1
                    for ti, (a, b) in enumerate(TG):
                        op("act", lambda g, uT=uT, ti=ti, a=a, b=b: g.activation(out=uT[:, a:b], in_=PG[ti][:, 0:264], func=AF.Gelu), [PG[ti]], [uT])
                    for ci in range(4):
                        op("pe", lambda g, ci=ci, cc=cc, g4=g4: g.matmul(PU[0][:, ci * 128:(ci + 1) * 128], lhsT=zt[ci][:, cc * 128:(cc + 1) * 128], rhs=wsT[g4][:], start=True, stop=True),
                           [Z, wsT[g4]], [PU[0]])
                    if own_a is not None:
                        op("pe", lambda g, cc=cc, g4=g4: g.matmul(PU[1][:, 0:16], lhsT=zt[4][0:16, cc * 128:(cc + 1) * 128], rhs=wsT[g4][0:16, 0:16], start=True, stop=True),
                           [Z, wsT[g4]], [PU[1]])
                    E = tmpA[0]
                    if cc % 8 == 0:
                        for (c0, c1) in TC:
                            op("act", lambda g, g4=g4, c0=c0, c1=c1: g.activation(out=rsx[:, c0:c1], in_=rsb[g4][:, 0:c1 - c0], func=AF.Copy), [rsb[g4]], [rsx])
                            op("act", lambda g, g4=g4, c0=c0, c1=c1: g.activation(out=bsx[:, c0:c1], in_=bsb[g4][:, 0:c1 - c0], func=AF.Copy), [bsb[g4]], [bsx])
                    op("dve", lambda g, E=E, cc=cc: g.scalar_tensor_tensor(out=E[:], in0=rsx[:], scalar=pcol(P_LB + l * 32, cc), in1=bsx[:], op0=ALU.mult, op1=ALU.add),
                       [rsx, bsx, prm], [E])
                    op("dve", lambda g, E=E, cc=cc: g.scalar_tensor_tensor(out=E[:, 0:512], in0=PU[0][:, 0:512], scalar=pcol(P_LG + l * 32, cc), in1=E[:, 0:512], op0=ALU.mult, op1=ALU.add),
                       [PU[0], prm, E], [E])
                    if own_a is not None:
                        op("dve", lambda g, E=E, cc=cc: g.scalar_tensor_tensor(out=E[:, 512:528], in0=PU[1][:, 0:16], scalar=pcol(P_LG + l * 32, cc), in1=E[:, 512:528], op0=ALU.mult, op1=ALU.add),
                           [PU[1], prm, E], [E])
                    op("dve", lambda g, E=E, uT=uT, j=j, ab=ab: g.tensor_tensor(out=ab[:, j, :], in0=E[:], in1=uT[:], op=ALU.mult), [E, uT], [ab])

            def wout(wt, ab):
                return down_units(wt, lambda j, dc, wt=wt: wt[:, j * 2048 + dc * 128: j * 2048 + (dc + 1) * 128], ab, 2, lambda j, a, b, ab=ab: ab[:, j, a:b], 1.0)

            NG = 16
            gu = {0: load_u(0)}
            dd = {0: load_o(0)}
            pending = []
            for cgrp in range(NG):
                if cgrp + 1 < NG:
                    gu[cgrp + 1] = load_u(cgrp + 1)
                ab = actb[cnt["a"] % 2]; cnt["a"] += 1
                umix(gu[cgrp], ab, cgrp, pending)
                drain(pending, 1000)
                if cgrp + 1 < NG:
                    dd[cgrp + 1] = load_o(cgrp + 1)
                pending = wout(dd[cgrp], ab)
            drain(pending, 1000)

        def kv_phase(slot, own_a, after_norm=None):
            rmsnorm(P_KV)
            if after_norm is not None:
                after_norm()
            wkl = wk_d.rearrange("(kc p) f -> p kc f", p=128)
            wvl = wv_d.rearrange("(kc p) f -> p kc f", p=128)
            for c4 in range(4 if 'kt' in KVS else 0):
                wt = WGU[cnt["w"] % 2]; cnt["w"] += 1
                dma("pool", wt, None, wt[:, 0:8192].rearrange("p (kc f) -> p kc f", kc=16), wkl[:, :, c4 * 512:(c4 + 1) * 512])
                for j in range(4):
                    h = c4 * 4 + j
                    proj_chunk(wt, lambda kc, j=j, wt=wt: wt[:, kc * 512 + j * 128: kc * 512 + (j + 1) * 128])
                    kt = tmpB[cnt["t"] % 2]; cnt["t"] += 1
                    for ti, (a, b) in enumerate(TG):
                        op("act", lambda g, kt=kt, ti=ti, a=a, b=b: g.activation(out=kt[:, a:b], in_=PG[ti][:, 0:264], func=AF.Copy), [PG[ti]], [kt])
                    dma("sp", kS, kt, kS[slot, h], kt[:, 0:512])
                    if own_a is not None:
                        op("dve", lambda g, kt=kt, h=h: g.tensor_copy(out=kTs[own_a][:, h, :], in_=kt[:, 512:528]), [kt], [kTs[own_a]])
                if own_a is not None and 'ktok' in KVS:
                    for ci, (c0, c1) in enumerate(TC):
                        M = c1 - c0
                        pm = PM[cnt["m"] % 2]; cnt["m"] += 1
                        for kc in range(16):
                            op("pe", lambda g, pm=pm, kc=kc, c0=c0, c1=c1, M=M, wt=wt: g.matmul(pm[0:M, :], lhsT=hT[:, kc, c0:c1], rhs=wt[:, kc * 512:(kc + 1) * 512], start=(kc == 0), stop=(kc == 15)),
                               [hT, wt], [pm], inc=(kc == 15))
                        ot = tmpA[cnt["t"] % 2]; cnt["t"] += 1
                        op("act", lambda g, pm=pm, ot=ot, M=M: g.activation(out=ot[0:M, 0:512], in_=pm[0:M, :], func=AF.Copy), [pm], [ot])
                        if ci < 4:
                            dma("sp", None, ot, k_p[own_a * 512 + c0: own_a * 512 + c1, c4 * 512:(c4 + 1) * 512], ot[0:M, 0:512])
                        else:
                            dma("sp", None, ot, k_s[own_a * 16:(own_a + 1) * 16, c4 * 512:(c4 + 1) * 512], ot[0:M, 0:512])
            for c4 in range(4 if 'v' in KVS else 0):
                wt = WGU[cnt["w"] % 2]; cnt["w"] += 1
                dma("pool", wt, None, wt[:, 0:8192].rearrange("p (kc f) -> p kc f", kc=16), wvl[:, :, c4 * 512:(c4 + 1) * 512])
                for ci, (c0, c1) in enumerate(TC):
                    M = c1 - c0
                    if ci == 4 and own_a is None:
                        continue
                    pm = PM[cnt["m"] % 2]; cnt["m"] += 1
                    for kc in range(16):
                        op("pe", lambda g, pm=pm, kc=kc, c0=c0, c1=c1, M=M, wt=wt: g.matmul(pm[0:M, :], lhsT=hT[:, kc, c0:c1], rhs=wt[:, kc * 512:(kc + 1) * 512], start=(kc == 0), stop=(kc == 15)),
                           [hT, wt], [pm], inc=(kc == 15))
                    if ci < 4:
                        vt = tmpB[cnt["t"] % 2]; cnt["t"] += 1
                        op("act", lambda g, pm=pm, vt=vt, M=M: g.activation(out=vt[0:M, 0:512], in_=pm[0:M, :], func=AF.Copy), [pm], [vt])
                        dma("sp", vS, vt, vS[slot, ci, :, c4 * 512:(c4 + 1) * 512], vt[:, 0:512])
                    else:
                        op("act", lambda g, pm=pm, c4=c4: g.activation(out=vss[own_a][0:16, c4 * 512:(c4 + 1) * 512], in_=pm[0:16, :], func=AF.Copy), [pm], [vss[own_a]])
                    if own_a is not None:
                        ot = tmpA[cnt["t"] % 2]; cnt["t"] += 1
                        op("dve", lambda g, pm=pm, ot=ot, M=M: g.tensor_copy(out=ot[0:M, 0:512], in_=pm[0:M, :]), [pm], [ot])
                        if ci < 4:
                            dma("sp", None, ot, v_p[own_a * 512 + c0: own_a * 512 + c1, c4 * 512:(c4 + 1) * 512], ot[0:M, 0:512])
                        else:
                            dma("sp", None, ot, v_s[own_a * 16:(own_a + 1) * 16, c4 * 512:(c4 + 1) * 512], ot[0:M, 0:512])
            for ci, (c0, c1) in enumerate(TC if 'lf' in KVS else []):
                M = c1 - c0
                if ci == 4 and own_a is None:
                    continue
                pm = PM[cnt["m"] % 2]; cnt["m"] += 1
                for kc in range(16):
                    op("pe", lambda g, pm=pm, kc=kc, c0=c0, c1=c1, M=M: g.matmul(pm[0:M, 0:16], lhsT=hT[:, kc, c0:c1], rhs=wfb[:, kc, :], start=(kc == 0), stop=(kc == 15)),
                       [hT, wfb], [pm], inc=(kc == 15))
                ta = tmpA[cnt["t"] % 2]; cnt["t"] += 1
                dst = lfS[0:M, slot * 4 + ci, :] if ci < 4 else lfs[own_a][0:16, :]
                dstt = lfS if ci < 4 else lfs[own_a]
                op("dve", lambda g, pm=pm, ta=ta, M=M: g.tensor_tensor(out=ta[0:M, 0:16], in0=pm[0:M, 0:16], in1=bfb[0:M, :], op=ALU.add), [pm, bfb], [ta])
                op("act", lambda g, ta=ta, M=M: g.activation(out=ta[0:M, 16:32], in_=ta[0:M, 0:16], func=AF.Abs), [ta], [ta])
                op("act", lambda g, ta=ta, M=M: g.activation(out=ta[0:M, 16:32], in_=ta[0:M, 16:32], func=AF.Exp, scale=-1.0), [ta], [ta])
                op("act", lambda g, ta=ta, M=M: g.activation(out=ta[0:M, 16:32], in_=ta[0:M, 16:32], func=AF.Ln, bias=1.0, scale=1.0), [ta], [ta])
                op("dve", lambda g, ta=ta, M=M: g.tensor_scalar(out=ta[0:M, 0:16], in0=ta[0:M, 0:16], scalar1=0.0, scalar2=None, op0=ALU.min), [ta], [ta])
                op("dve", lambda g, ta=ta, M=M, dst=dst: g.tensor_tensor(out=dst, in0=ta[0:M, 0:16], in1=ta[0:M, 16:32], op=ALU.subtract), [ta], [dstt])
                if own_a is not None:
                    if ci < 4:
                        dma("sp", None, lfS, lf_p[own_a * 512 + c0: own_a * 512 + c1, :], dst)
                    else:
                        dma("sp", None, lfs[own_a], lf_s[own_a * 16:(own_a + 1) * 16, :], dst)


        Bk = kb.sb([128, 64, 16], F32, "Bk")
        Wq = kb.sb([128, 8, 16], F32, "Wq")
        Lm = _Zv(Z[:, 16896:18944].bitcast(F32).rearrange("p (b h) -> p b h", h=16))
        Rc = kb.sb([128, 17, 16], F32, "Rc")
        Lc = kb.sb([128, 17, 16], F32, "Lc")
        cqb = kb.sb([128, NT], F32, "cqb")
        rhc = kb.sb([128, 512], F32, "rhc")
        Pt = [kb.sb([128, 512], BF16, f"Pt{i}") for i in range(2)]
        kcs = kb.sb([128, 2048], BF16, "kcs")
        kcT = kb.sb([128, 2048], BF16, "kcT")
        vcs = kb.sb([128, 2048], BF16, "vcs")
        rec = rstd

        def suffix_excl(L, R, nb):
            n = nb * 16
            Lf = L[:, 0:nb, :].rearrange("p b h -> p (b h)"); Rf = R[:, 0:nb, :].rearrange("p b h -> p (b h)")
            for c0 in range(0, n, 512):
                c1 = min(n, c0 + 512)
                op("pe", lambda g, c0=c0, c1=c1: g.matmul(PM[0][:, 0:c1 - c0], lhsT=SL[:], rhs=Lf[:, c0:c1], start=True, stop=True), [SL, L], [PM[0]])
                op("pe", lambda g, c0=c0, c1=c1: g.matmul(PM[1][:, 0:c1 - c0], lhsT=onesf[:], rhs=Lf[:, c0:c1], start=True, stop=True), [onesf, L], [PM[1]])
                op("dve", lambda g, c0=c0, c1=c1: g.tensor_copy(out=Rf[:, c0:c1], in_=PM[0][:, 0:c1 - c0]), [PM[0]], [R])
                op("act", lambda g, c0=c0, c1=c1: g.activation(out=Lf[:, c0:c1], in_=PM[1][:, 0:c1 - c0], func=AF.Copy), [PM[1]], [L])
            op("dve", lambda g: g.memset(rhc[:, 0:16], 0.0), [], [rhc])
            for b in range(nb - 2, -1, -1):
                op("dve", lambda g, b=b: g.tensor_tensor(out=rhc[:, 0:16], in0=rhc[:, 0:16], in1=L[:, b + 1, :], op=ALU.add), [rhc, L], [rhc])
                op("dve", lambda g, b=b: g.tensor_tensor(out=R[:, b, :], in0=R[:, b, :], in1=rhc[:, 0:16], op=ALU.add), [R, rhc], [R])

        def prefix_incl(L, W, nb, rows=128):
            n = nb * 16
            Lf = L[0:rows, 0:nb, :].rearrange("p b h -> p (b h)"); Wf = W[0:rows, 0:nb, :].rearrange("p b h -> p (b h)")
            op("pe", lambda g: g.matmul(PM[0][0:rows, 0:n], lhsT=UT[0:rows, 0:rows], rhs=Lf, start=True, stop=True), [UT, L], [PM[0]])
            op("pe", lambda g: g.matmul(PM[1][0:rows, 0:n], lhsT=onesf[0:rows, 0:rows], rhs=Lf, start=True, stop=True), [onesf, L], [PM[1]])
            op("dve", lambda g: g.tensor_copy(out=Wf, in_=PM[0][0:rows, 0:n]), [PM[0]], [W])
            op("act", lambda g: g.activation(out=Lf, in_=PM[1][0:rows, 0:n], func=AF.Copy), [PM[1]], [L])
            op("dve", lambda g: g.memset(rhc[:, 0:16], 0.0), [], [rhc])
            for b in range(1, nb):
                op("dve", lambda g, b=b: g.tensor_tensor(out=rhc[0:rows, 0:16], in0=rhc[0:rows, 0:16], in1=L[0:rows, b - 1, :], op=ALU.add), [rhc, L], [rhc])
                op("dve", lambda g, b=b: g.tensor_tensor(out=W[0:rows, b, :], in0=W[0:rows, b, :], in1=rhc[0:rows, 0:16], op=ALU.add), [W, rhc], [W])

        def attn_prep(a):
            for b in range(56 if CTX else 0):
                op("dve", lambda g, b=b: g.tensor_scalar(out=Lm[:, b, :], in0=lfS[:, b, :], scalar1=vbt[:, 0, b:b + 1], scalar2=None, op0=ALU.mult), [lfS, vbt], [Lm])
            if CTX:
                suffix_excl(Lm, Bk, 56)
            for b in range(56 if CTX else 0):
                op("dve", lambda g, b=b: g.tensor_scalar(out=Bk[:, b, :], in0=Bk[:, b, :], scalar1=vbt[:, 1, b:b + 1], scalar2=None, op0=ALU.add), [Bk, vbt], [Bk])
            nbo = 4 * (a + 1)
            op("dve", lambda g: g.tensor_copy(out=Lm[:, 0:nbo, :], in_=lfS[:, 56:56 + nbo, :]), [lfS], [Lm])
            prefix_incl(Lm, Wq, nbo)
            op("dve", lambda g: g.tensor_scalar(out=Bk[:, 56:56 + nbo, :], in0=Wq[:, 0:nbo, :], scalar1=-1.0, scalar2=None, op0=ALU.mult), [Wq], [Bk])
            dma("sp", Lc, None, Lc[:, 0:16, :], cl_d[a].rearrange("(b t) h -> t b h", t=128))
            suffix_excl(Lc, Rc, 16)
            op("dve", lambda g: g.tensor_copy(out=Lc[0:16, 16:17, :], in_=lfs[a][0:16, :].rearrange("p (o h) -> p o h", o=1)), [lfs[a]], [Lc])
            prefix_incl(_Sl(Lc, 16), _Sl(Rc, 16), 1, rows=16)

        class _Sl:
            def __init__(self, t, b0):
                self.t = t; self.b0 = b0
            def __getitem__(self, k):
                p, b, h = k
                b = slice(b.start + self.b0, b.stop + self.b0) if isinstance(b, slice) else b + self.b0
                return self.t.ap[p, b, h]
            lw = property(lambda s: s.t.lw, lambda s, v: setattr(s.t, "lw", v))
            rd = property(lambda s: s.t.rd, lambda s, v: setattr(s.t, "rd", v))

        def bcast_rows(dst_ap, dst_t, src_col_fn, nblk, rows, width):
            for blk in range(nblk):
                op("dve", lambda g, blk=blk: g.tensor_scalar(out=rhc[0:rows, blk * width:(blk + 1) * width], in0=identf[0:rows, 0:width], scalar1=src_col_fn(blk), scalar2=None, op0=ALU.mult), [identf, Wq, Rc], [rhc])
            n = nblk * width
            op("pe", lambda g: g.matmul(PM[0][:, 0:n], lhsT=onesf[0:rows, :], rhs=rhc[0:rows, 0:n], start=True, stop=True), [onesf, rhc], [PM[0]])
            op("act", lambda g: g.activation(out=dst_ap, in_=PM[0][:, 0:n], func=AF.Copy), [PM[0]], [dst_t])

        def fox(j, a):
            l = 2 + j
            rmsnorm(P_MX + l * 16)
            wql = wq_d[j].rearrange("(kc p) f -> p kc f", p=128)
            for c4 in range(4):
                wt = WGU[cnt["w"] % 2]; cnt["w"] += 1
                dma("pool", wt, None, wt[:, 0:8192].rearrange("p (kc f) -> p kc f", kc=16), wql[:, :, c4 * 512:(c4 + 1) * 512])
                for jj in range(4):
                    h = c4 * 4 + jj
                    proj_chunk(wt, lambda kc, jj=jj, wt=wt: wt[:, kc * 512 + jj * 128: kc * 512 + (jj + 1) * 128])
                    for ti, (ca, cb) in enumerate(TG):
                        op("act", lambda g, h=h, ti=ti, ca=ca, cb=cb: g.activation(out=qT[:, h, ca:cb], in_=PG[ti][:, 0:264], func=AF.Copy), [PG[ti]], [Z])
            KH, VH = WGU[0], WGU[1]
            nslot = 15 + a
            for h in range(16):
                slo = 0 if CTX else 14
                dma("sp", KH, kS, KH[:, slo * 512:nslot * 512].rearrange("p (s n) -> p s n", s=nslot - slo), kS[slo:nslot, h].rearrange("s d n -> d s n"))
                for s_ in range(slo, nslot):
                    dma("sp", VH, vS, VH[:, s_ * 512:(s_ + 1) * 512].rearrange("p (b d) -> p b d", b=4), vS[s_, :, :, h * 128:(h + 1) * 128].rearrange("b t d -> t b d"))
                bcast_rows(cqb[:, 0:512], cqb, lambda blk, h=h: Wq[:, 4 * a + blk, h:h + 1], 4, 128, 128)
                blocks = ([(s_, b_) for s_ in range(14) for b_ in range(4)] if CTX else []) + [(14 + a2, b_) for a2 in range(a + 1) for b_ in range(4)]
                for bi, (s_, b_) in enumerate(blocks):
                    diag = (s_ == 14 + a)
                    q0 = b_ * 128 if diag else 0
                    nq = 512 - q0
                    ps = PG[bi % 2]
                    pt = Pt[bi % 2]
                    ta = tmpA[bi % 2]
                    gb = s_ * 4 + b_
                    op("pe", lambda g, ps=ps, s_=s_, b_=b_, q0=q0, nq=nq, h=h: g.matmul(ps[:, 0:nq], lhsT=KH[:, s_ * 512 + b_ * 128: s_ * 512 + (b_ + 1) * 128], rhs=qT[:, h, q0:512], start=True, stop=True), [KH, Z], [ps])
                    op("dve", lambda g, ps=ps, ta=ta, q0=q0, nq=nq: g.scalar_tensor_tensor(out=ta[:, 0:nq], in0=ps[:, 0:nq], scalar=SCALE, in1=cqb[:, q0:512], op0=ALU.mult, op1=ALU.add), [ps, cqb], [ta])
                    if diag:
                        op("dve", lambda g, ta=ta: g.tensor_tensor(out=ta[:, 0:128], in0=ta[:, 0:128], in1=TRIN[:], op=ALU.add), [ta, TRIN], [ta])
                    op("act", lambda g, ta=ta, pt=pt, nq=nq, gb=gb, h=h: g.activation(out=pt[:, 0:nq], in_=ta[:, 0:nq], func=AF.Exp, bias=Bk[:, gb, h:h + 1], scale=1.0), [ta, Bk], [pt])
                    first = (bi == 0); last = (bi == len(blocks) - 1)
                    op("pe", lambda g, pt=pt, s_=s_, b_=b_, q0=q0, nq=nq, first=first, last=last: g.matmul(PU[0][:, q0:512], lhsT=VH[:, s_ * 512 + b_ * 128: s_ * 512 + (b_ + 1) * 128], rhs=pt[:, 0:nq], start=first, stop=last), [VH, pt], [PU[0]], inc=False)
                    op("pe", lambda g, pt=pt, q0=q0, nq=nq, first=first, last=last: g.matmul(PU[1][:, q0:512], lhsT=onesb[:], rhs=pt[:, 0:nq], start=first, stop=last), [onesb, pt], [PU[1]])
                op("dve", lambda g: g.reciprocal(out=rec[:, 0:512], in_=PU[1][:]), [PU[1]], [rec])
                op("dve", lambda g, h=h: g.tensor_tensor(out=oT[:, h, 0:512], in0=PU[0][:], in1=rec[:, 0:512], op=ALU.mult), [PU[0], rec], [Z])
                dma("pool", kcs, None, kcs[:].rearrange("p (b d) -> p b d", b=16), ck_d[a, :, h * 128:(h + 1) * 128].rearrange("(b t) d -> t b d", t=128))
                dma("pool", vcs, None, vcs[:].rearrange("p (b d) -> p b d", b=16), cv_d[a, :, h * 128:(h + 1) * 128].rearrange("(b t) d -> t b d", t=128))
                for q4 in range(4):
                    pmb = PMb[q4 % 2]; pmf = PM[q4 % 2]
                    for jj in range(4):
                        b = q4 * 4 + jj
                        op("pe", lambda g, pmb=pmb, jj=jj, b=b: g.transpose(out=pmb[:, jj * 128:(jj + 1) * 128], in_=kcs[:, b * 128:(b + 1) * 128], identity=identb[:]), [kcs, identb], [pmf], inc=(jj == 3))
                    op("dve", lambda g, pmb=pmb, q4=q4: g.tensor_copy(out=kcT[:, q4 * 512:(q4 + 1) * 512], in_=pmb[:, 0:512]), [pmf], [kcT])
                bcast_rows(cqb[:, 512:528], cqb, lambda blk, h=h: Rc[0:16, 16, h:h + 1], 1, 16, 16)
                for b in range(16):
                    op("pe", lambda g, b=b, h=h: g.matmul(PD[0][:, b * 16:(b + 1) * 16], lhsT=kcT[:, b * 128:(b + 1) * 128], rhs=qT[:, h, 512:528], start=True, stop=True), [kcT, Z], [PD[0]], inc=False)
                op("pe", lambda g, h=h: g.matmul(PD[0][0:16, 256:272], lhsT=kTs[a][:, h, :], rhs=qT[:, h, 512:528], start=True, stop=True), [kTs[a], Z], [PD[0]])
                ta = tmpA[0]; pt = Pt[0]
                for b in range(17):
                    rows = 128 if b < 16 else 16
                    op("dve", lambda g, b=b, rows=rows: g.scalar_tensor_tensor(out=ta[0:rows, b * 16:(b + 1) * 16], in0=PD[0][0:rows, b * 16:(b + 1) * 16], scalar=SCALE, in1=cqb[0:rows, 512:528], op0=ALU.mult, op1=ALU.add), [PD[0], cqb], [ta])
                op("dve", lambda g: g.tensor_tensor(out=ta[0:16, 256:272], in0=ta[0:16, 256:272], in1=TRIN[0:16, 0:16], op=ALU.add), [ta, TRIN], [ta])
                op("dve", lambda g, h=h: g.tensor_scalar(out=rhc[0:16, 16:17], in0=Rc[0:16, 16, h:h + 1], scalar1=-1.0, scalar2=None, op0=ALU.mult), [Rc], [rhc])
                for b in range(17):
                    rows = 128 if b < 16 else 16
                    bias = Rc[:, b, h:h + 1] if b < 16 else rhc[0:16, 16:17]
                    op("act", lambda g, b=b, rows=rows, bias=bias: g.activation(out=pt[0:rows, b * 16:(b + 1) * 16], in_=ta[0:rows, b * 16:(b + 1) * 16], func=AF.Exp, bias=bias, scale=1.0), [ta, Rc, rhc], [pt])
                for b in range(17):
                    rows = 128 if b < 16 else 16
                    lv = vcs[:, b * 128:(b + 1) * 128] if b < 16 else vss[a][0:16, h * 128:(h + 1) * 128]
                    op("pe", lambda g, b=b, rows=rows, lv=lv: g.matmul(PD[1][:, 0:16], lhsT=lv, rhs=pt[0:rows, b * 16:(b + 1) * 16], start=(b == 0), stop=(b == 16)), [vcs, vss[a], pt], [PD[1]], inc=False)
                    op("pe", lambda g, b=b, rows=rows: g.matmul(PM[1][:, 16:32], lhsT=onesb[0:rows, :], rhs=pt[0:rows, b * 16:(b + 1) * 16], start=(b == 0), stop=(b == 16)), [onesb, pt], [PM[1]], inc=(b == 16))
                op("dve", lambda g: g.reciprocal(out=rec[:, 0:16], in_=PM[1][:, 16:32]), [PM[1]], [rec])
                op("dve", lambda g, h=h: g.tensor_tensor(out=oT[:, h, 512:528], in0=PD[1][:, 0:16], in1=rec[:, 0:16], op=ALU.mult), [PD[1], rec], [Z])
            wol = wo_d[j]
            for hg in range(8):
                wt = WDn[cnt["wd"] % 2]; cnt["wd"] += 1
                dma("pool", wt, None, wt[:, 0:4096].rearrange("p (jj d) -> p jj d", jj=2), wol[hg * 256:(hg + 1) * 256, :].rearrange("(jj p) d -> p jj d", p=128))
                acc_down(wt, lambda jj, dc, wt=wt: wt[:, jj * 2048 + dc * 128: jj * 2048 + (dc + 1) * 128], Z, 2, lambda jj, ca, cb, hg=hg: oT[:, hg * 2 + jj, ca:cb], 1.0)

        def final_out(a):
            rstd_compute()
            for ci, (c0, c1) in enumerate(TC):
                M = c1 - c0
                for q4 in range(4):
                    pm = PM[cnt["m"] % 2]; cnt["m"] += 1
                    for jj in range(4):
                        kc = q4 * 4 + jj
                        yt = tmpA[jj % 2]
                        op("dve", lambda g, kc=kc, yt=yt, c0=c0, c1=c1, M=M: g.scalar_tensor_tensor(out=yt[:, 0:M], in0=xT[:, kc, c0:c1], scalar=pcol(P_FN, kc), in1=rstd[:, c0:c1], op0=ALU.mult, op1=ALU.mult), xd(kc) + [prm, rstd], [yt])
                        op("pe", lambda g, pm=pm, jj=jj, yt=yt, M=M: g.transpose(out=pm[0:M, jj * 128:(jj + 1) * 128], in_=yt[:, 0:M], identity=identf[:]), [yt, identf], [pm])
                    op("act", lambda g, pm=pm, q4=q4, M=M: g.activation(out=stg[0:M, q4 * 512:(q4 + 1) * 512], in_=pm[0:M, :], func=AF.Copy), [pm], [Z])
                if ci < 4:
                    dma("sp", None, Z, y_p[a * 512 + c0: a * 512 + c1, :], stg[0:M, :])
                else:
                    dma("sp", None, Z, y_s[a * 16:(a + 1) * 16, :], stg[0:M, :])

        preloaded = set()
        for ti_, tile in enumerate(TILES):
            own_a = tile - 14 if tile >= 14 else None
            if tile not in preloaded:
                load_x(tile)
            nxt = TILES[ti_ + 1] if (ti_ + 1 < len(TILES) and own_a is None) else None
            for l in range(NLA):
                if "ffn1" in STG:
                    ffn(l, 0, P_F1)
                if "gmlp" in STG:
                    gmlp(l, own_a)
                if "ffn2" in STG:
                    ffn(l, 1, P_F2)
            if "kv" in STG:
                def _pre(nxt=nxt):
                    load_x(nxt); preloaded.add(nxt)
                kv_phase(tile, own_a, _pre if nxt is not None else None)
            if own_a is not None and BL:
                attn_prep(own_a)
                for j in range(2):
                    ffn(2 + j, 0, P_F1)
                    fox(j, own_a)
                    ffn(2 + j, 1, P_F2)
                final_out(own_a)
        kb.emit()
    return nc


TILES = list(range(16))
CTX = True
BL = True
NLA = 2
WL = 4
KVS = {'kt', 'ktok', 'v', 'lf'}
STG = {'ffn1', 'gmlp', 'ffn2', 'kv'}
_NC = None


def _prep_inputs(inp, c):
    f = np.float32
    xp = np.asarray(inp["x_prompt"], f)[0]
    xs = np.asarray(inp["x_sample"], f)
    own = [2 * c, 2 * c + 1]
    others = [t for t in range(16) if t not in own]
    tiles = others + own
    xin = np.empty((16, NT, D), f)
    for i, t in enumerate(tiles):
        xin[i, :512] = xp[t * 512:(t + 1) * 512]
        xin[i, 512:] = xs[2 * c + (i - 14 if i >= 14 else 0)]
    def pl(arr):
        arr = np.asarray(arr, f)
        L = arr.shape[0]
        return arr.reshape(L, -1, 128).transpose(2, 0, 1).reshape(128, -1)
    prm = np.concatenate([pl(inp["ffn1_norm"]), pl(inp["mix_norm"]), pl(inp["ffn2_norm"]), pl(np.asarray(inp["kv_norm"])[None]),
                          pl(np.asarray(inp["final_norm"])[None]), pl(inp["gmlp_ln_g"]), pl(inp["gmlp_ln_b"])], axis=1)
    assert prm.shape == (128, NPRM)
    vb = np.zeros((2, 56), f)
    for i, t in enumerate(others):
        ok = t < 2 * c
        vb[0, i * 4:(i + 1) * 4] = 1.0 if ok else 0.0
        vb[1, i * 4:(i + 1) * 4] = 0.0 if ok else NEG
    m = {"xin": xin.reshape(16 * NT, D), "prm": np.ascontiguousarray(prm), "vb": vb,
         "wsT": np.ascontiguousarray(np.asarray(inp["gmlp_w_s"], f).transpose(0, 1, 3, 2)),
         "cache_k": np.asarray(inp["cache_k"], f)[2 * c:2 * c + 2].reshape(2, 2048, D),
         "cache_v": np.asarray(inp["cache_v"], f)[2 * c:2 * c + 2].reshape(2, 2048, D),
         "cache_logf": np.asarray(inp["cache_logf"], f)[2 * c:2 * c + 2]}
    for k in ("ffn1_w_gate", "ffn2_w_gate", "ffn1_w_up", "ffn2_w_up", "ffn1_w_down", "ffn2_w_down", "gmlp_w_in", "gmlp_w_out",
              "gmlp_b_s", "gmlp_ln_g", "gmlp_ln_b", "w_k", "w_v", "w_f", "b_f", "fox_w_q", "fox_w_o"):
        m[k] = np.asarray(inp[k], f)[:WL] if k.startswith('ffn') else np.asarray(inp[k], f)
    return m


def kernel(**inp):
    global _NC
    if _NC is None:
        _NC = build_program()
    cores = list(range(8))
    in_maps = [_prep_inputs(inp, c) for c in cores]
    res = run_bass_kernel_spmd(_NC, in_maps, core_ids=cores)
    R = res.results
    cat = lambda k: np.concatenate([R[c][k] for c in cores], axis=0)
    y_prompt = cat("y_p").reshape(1, 8192, D)
    y_sample = cat("y_s").reshape(16, 16, D)
    k_prompt = cat("k_p").reshape(1, 8192, 16, 128)
    v_prompt = cat("v_p").reshape(1, 8192, 16, 128)
    logf_prompt = cat("lf_p").reshape(1, 8192, 16)
    k_sample = cat("k_s").reshape(16, 16, 16, 128)
    v_sample = cat("v_s").reshape(16, 16, 16, 128)
    logf_sample = cat("lf_s").reshape(16, 16, 16)
    gvs = np.concatenate([R[c]["gv"] for c in cores], axis=1).reshape(2, 16, 16, 4096)
    return (y_prompt, y_sample, k_prompt, v_prompt, logf_prompt, k_sample, v_sample, logf_sample, gvs)
```
